# Optimizing a Trainium2 kernel written in Bass

```python
import math
import jax
import jax.numpy as jnp
from jax import lax
import numpy as np

D_MODEL = 1024
BATCH = 8
SEQ = 2048
DEPTH = 4
DEC_BATCH = 128
DEC_SEQ = 4
PAST_LEN = 16384
PAGE_SIZE = 128

N_META = 16
D_MIX = D_MODEL
D_BRANCH = D_MIX // 4

SSD_HEAD_DIM = 64
SSD_HEADS = D_BRANCH // SSD_HEAD_DIM
SSD_STATE = 64
SSD_GROUPS = 2
SSD_CONV = 4
SSD_CHUNK = 128
SSD_XBC = D_BRANCH + 2 * SSD_GROUPS * SSD_STATE

S5_GROUP = 16
S5_GROUPS = D_BRANCH // S5_GROUP
S5_STATE = 64

GDN_HEAD_DIM = 64
GDN_HEADS = D_BRANCH // GDN_HEAD_DIM
GDN_CONV = 4
GDN_CHUNK = 64
GDN_QKV = 3 * D_BRANCH

HG_HEAD_DIM = 64
HG_HEADS = D_BRANCH // HG_HEAD_DIM
HG_CHUNK = 64

_IN_SIZES = (D_BRANCH, SSD_XBC, SSD_HEADS,
             D_BRANCH, D_BRANCH,
             D_BRANCH, GDN_QKV, GDN_HEADS, GDN_HEADS,
             D_BRANCH, D_BRANCH, D_BRANCH, D_BRANCH)
D_IN = 8 * D_BRANCH + SSD_XBC + GDN_QKV + SSD_HEADS + 2 * GDN_HEADS

DN_ALPHA = (2 * DEPTH) ** 0.25
DN_BETA = (8 * DEPTH) ** -0.25
LN_EPS = 1e-5
RMS_EPS = 1e-6
L2_EPS = 1e-6

kernel_name = 'hybrid_ssd_s5_gdn_hgrn2_step'


def _split_last(t, sizes):
    out = []
    start = 0
    for size in sizes:
        out.append(t[..., start:start + size])
        start += size
    return out


def _layernorm(x, g, b):
    xf = x.astype(jnp.float32)
    mu = jnp.mean(xf, -1, keepdims=True)
    var = jnp.mean(jnp.square(xf - mu), -1, keepdims=True)
    return ((xf - mu) * lax.rsqrt(var + LN_EPS) * g + b).astype(x.dtype)


def _rms_heads(y, g):
    return y * lax.rsqrt(jnp.mean(jnp.square(y), -1, keepdims=True) + RMS_EPS) * g


def _l2norm(t):
    return t * lax.rsqrt(jnp.sum(jnp.square(t), -1, keepdims=True) + L2_EPS)


def _causal_conv(x, buf, w, b):
    c = x.shape[-1]
    width = w.shape[0]
    xx = jnp.concatenate([buf.astype(x.dtype), x], axis=1)
    y = lax.conv_general_dilated(xx, w[:, None, :].astype(x.dtype), window_strides=(1,), padding='VALID',
                                 dimension_numbers=('NWC', 'WIO', 'NWC'), feature_group_count=c)
    return y + b, xx[:, xx.shape[1] - (width - 1):]


def _chunk_len(t_len, chunk):
    return chunk if t_len % chunk == 0 else t_len


def _with_prefix(fn, seqs, s0, prefix, chunk):
    if prefix > 0:
        y0, s = fn(*[a[:, :prefix] for a in seqs], s0, prefix)
        rest = [a[:, prefix:] for a in seqs]
        y1, s = fn(*rest, s, _chunk_len(rest[0].shape[1], chunk))
        return jnp.concatenate([y0, y1], axis=1), s
    return fn(*seqs, s0, _chunk_len(seqs[0].shape[1], chunk))


def _ssd_chunked(x, dt, a, bm, cm, s0, chunk):
    bsz, t_len, n_heads, _ = x.shape
    nc = t_len // chunk
    rep = n_heads // bm.shape[2]
    bh = jnp.repeat(bm, rep, axis=2)
    chh = jnp.repeat(cm, rep, axis=2)

    def ch(t):
        return t.reshape((bsz, nc, chunk) + t.shape[2:])

    xdt = ch(x * dt[..., None])
    bc, cc = ch(bh), ch(chh)
    acum = jnp.cumsum(ch(a), axis=2)
    incl = jnp.tril(jnp.ones((chunk, chunk), dtype=bool))
    seg = acum[:, :, :, None, :] - acum[:, :, None, :, :]
    lmat = jnp.exp(jnp.where(incl[:, :, None], seg, -jnp.inf))
    scores = jnp.einsum('bcihn,bcjhn->bcijh', cc, bc) * lmat
    y_diag = jnp.einsum('bcijh,bcjhp->bcihp', scores, xdt)
    to_end = jnp.exp(acum[:, :, -1:, :] - acum)
    chunk_state = jnp.einsum('bcjhn,bcjh,bcjhp->bchnp', bc, to_end, xdt)
    chunk_decay = jnp.exp(acum[:, :, -1, :])

    def step(s, inp):
        cs, cd = inp
        return s * cd[..., None, None] + cs, s

    s_fin, s_in = lax.scan(step, s0, (chunk_state.swapaxes(0, 1), chunk_decay.swapaxes(0, 1)))
    y_off = jnp.einsum('bcihn,cbhnp,bcih->bcihp', cc, s_in, jnp.exp(acum))
    return (y_diag + y_off).reshape(x.shape), s_fin


def _s5_scan(u, lam_re, lam_im, log_dt, b_re, b_im, c_re, c_im, d, s0_re, s0_im):
    dt = jnp.exp(log_dt)[:, None]
    mag = jnp.exp(lam_re * dt)
    ang = lam_im * dt
    lb_re = mag * jnp.cos(ang)
    lb_im = mag * jnp.sin(ang)
    den = jnp.square(lam_re) + jnp.square(lam_im)
    nr = lb_re - 1.0
    coef_re = (nr * lam_re + lb_im * lam_im) / den
    coef_im = (lb_im * lam_re - nr * lam_im) / den
    bb_re = coef_re[..., None] * b_re - coef_im[..., None] * b_im
    bb_im = coef_re[..., None] * b_im + coef_im[..., None] * b_re
    bu_re = jnp.einsum('btgq,gnq->btgn', u, bb_re)
    bu_im = jnp.einsum('btgq,gnq->btgn', u, bb_im)
    bu_re = bu_re.at[:, 0].add(lb_re * s0_re - lb_im * s0_im)
    bu_im = bu_im.at[:, 0].add(lb_re * s0_im + lb_im * s0_re)
    a_re = jnp.broadcast_to(lb_re, bu_re.shape)
    a_im = jnp.broadcast_to(lb_im, bu_im.shape)

    def combine(e1, e2):
        a1r, a1i, b1r, b1i = e1
        a2r, a2i, b2r, b2i = e2
        return (a2r * a1r - a2i * a1i, a2r * a1i + a2i * a1r,
                a2r * b1r - a2i * b1i + b2r, a2r * b1i + a2i * b1r + b2i)

    _, _, h_re, h_im = lax.associative_scan(combine, (a_re, a_im, bu_re, bu_im), axis=1)
    y = (jnp.einsum('gqn,btgn->btgq', c_re, h_re) - jnp.einsum('gqn,btgn->btgq', c_im, h_im)
         + d * u)
    return y, h_re[:, -1], h_im[:, -1]


def _gdn_chunked(q, k, v, beta, g, s0, chunk):
    bsz, t_len, n_heads, dk = q.shape
    dv = v.shape[-1]
    nc = t_len // chunk

    def ch(t):
        return t.reshape((bsz, nc, chunk) + t.shape[2:])

    q = ch(q * dk ** -0.5)
    k, v, beta = ch(k), ch(v), ch(beta)
    gc = jnp.cumsum(ch(g), axis=2)
    incl = jnp.tril(jnp.ones((chunk, chunk), dtype=bool))
    strict = jnp.tril(jnp.ones((chunk, chunk), dtype=bool), -1)
    diff = gc[:, :, :, None, :] - gc[:, :, None, :, :]
    dec = jnp.exp(jnp.where(incl[:, :, None], diff, -jnp.inf))
    kk = jnp.einsum('bcihd,bcjhd->bcijh', k, k)
    m = jnp.where(strict[:, :, None], kk * dec * beta[:, :, :, None, :], 0.0)
    m = m.transpose(0, 1, 4, 2, 3)
    eye = jnp.eye(chunk, dtype=m.dtype)
    tinv = lax.linalg.triangular_solve(eye + m, jnp.broadcast_to(eye, m.shape),
                                       left_side=True, lower=True, unit_diagonal=True)
    u_val = jnp.einsum('bchij,bcjhd->bcihd', tinv, v * beta[..., None])
    w_key = jnp.einsum('bchij,bcjhd->bcihd', tinv, k * (beta * jnp.exp(gc))[..., None])
    aq = jnp.einsum('bcihd,bcjhd->bcijh', q, k) * dec

    def step(s, inp):
        qi, ki, ui, wi, gi, ai = inp
        v_new = ui - jnp.einsum('blhk,bhkv->blhv', wi, s)
        o = (jnp.einsum('blhk,bhkv->blhv', qi * jnp.exp(gi)[..., None], s)
             + jnp.einsum('blsh,bshv->blhv', ai, v_new))
        gl = gi[:, -1]
        s = (s * jnp.exp(gl)[..., None, None]
             + jnp.einsum('bshk,bshv->bhkv', ki * jnp.exp(gl[:, None] - gi)[..., None], v_new))
        return s, o

    xs = tuple(t.swapaxes(0, 1) for t in (q, k, u_val, w_key, gc, aq))
    s_fin, o = lax.scan(step, s0, xs)
    return o.swapaxes(0, 1).reshape(bsz, t_len, n_heads, dv), s_fin


def _hgrn_chunked(q, k, v, logf, s0, chunk):
    bsz, t_len, n_heads, _ = q.shape
    dv = v.shape[-1]
    nc = t_len // chunk

    def ch(t):
        return t.reshape((bsz, nc, chunk) + t.shape[2:])

    gc = jnp.cumsum(ch(logf), axis=2)
    incl = jnp.tril(jnp.ones((chunk, chunk), dtype=bool))

    def step(s, inp):
        qi, ki, vi, gi = inp
        diff = gi[:, :, None] - gi[:, None]
        dec = jnp.exp(jnp.where(incl[:, :, None, None], diff, -jnp.inf))
        a = jnp.einsum('bihk,bjhk,bijhk->bijh', qi, ki, dec)
        o = (jnp.einsum('bijh,bjhv->bihv', a, vi)
             + jnp.einsum('bihk,bhkv->bihv', qi * jnp.exp(gi), s))
        gl = gi[:, -1]
        s = s * jnp.exp(gl)[..., None] + jnp.einsum('bjhk,bjhv->bhkv', ki * jnp.exp(gl[:, None] - gi), vi)
        return s, o

    xs = tuple(t.swapaxes(0, 1) for t in (ch(q), ch(k), ch(v), gc))
    s_fin, o = lax.scan(step, s0, xs)
    return o.swapaxes(0, 1).reshape(bsz, t_len, n_heads, dv), s_fin


def _hgrn_lower_bounds(lb_raw):
    p = jax.nn.softmax(lb_raw.astype(jnp.float32), axis=0)
    c = jnp.cumsum(p, axis=0)
    return c - c[0]


def _layer(x, st, lw, lb, prefix):
    lw = {k: (v if k in ('w_in', 'w_out') else v.astype(jnp.float32)) for k, v in lw.items()}
    s_ssd, c_ssd, s5r, s5i, s_gdn, c_gdn, s_hg = [s.astype(jnp.float32) for s in st]
    bsz, t_len, _ = x.shape
    u = jnp.einsum('btd,de->bte', x, lw['w_in']).astype(jnp.float32)
    (z_a, xbc, dt_raw, z_b, u_b, z_c, qkv, b_raw, a_raw,
     z_d, q_d, f_d, i_d) = _split_last(u, _IN_SIZES)

    xbc, c_ssd_new = _causal_conv(xbc, c_ssd, lw['ssd_conv_w'], lw['ssd_conv_b'])
    xbc = jax.nn.silu(xbc)
    xs, bm, cm = _split_last(xbc, (D_BRANCH, SSD_GROUPS * SSD_STATE, SSD_GROUPS * SSD_STATE))
    xs = xs.reshape(bsz, t_len, SSD_HEADS, SSD_HEAD_DIM)
    bm = bm.reshape(bsz, t_len, SSD_GROUPS, SSD_STATE)
    cm = cm.reshape(bsz, t_len, SSD_GROUPS, SSD_STATE)
    dt = jax.nn.softplus(dt_raw + lw['ssd_dt_bias'])
    a = -dt * jnp.exp(lw['ssd_a_log'])
    y, s_ssd_new = _with_prefix(_ssd_chunked, (xs, dt, a, bm, cm), s_ssd, prefix, SSD_CHUNK)
    y = y + lw['ssd_d'][:, None] * xs
    y_a = _rms_heads(y, lw['ssd_norm_g']).reshape(bsz, t_len, D_BRANCH) * jax.nn.silu(z_a)

    y5, s5r_new, s5i_new = _s5_scan(u_b.reshape(bsz, t_len, S5_GROUPS, S5_GROUP),
                                    lw['s5_lam_re'], lw['s5_lam_im'], lw['s5_log_dt'],
                                    lw['s5_b_re'], lw['s5_b_im'], lw['s5_c_re'], lw['s5_c_im'],
                                    lw['s5_d'], s5r, s5i)
    y5 = jax.nn.gelu(y5.reshape(bsz, t_len, D_BRANCH))
    y5 = y5 * jax.nn.sigmoid(jnp.einsum('btc,ce->bte', y5, lw['s5_glu_w']) + lw['s5_glu_b'])
    y_b = y5 * jax.nn.silu(z_b)

    qkv, c_gdn_new = _causal_conv(qkv, c_gdn, lw['gdn_conv_w'], lw['gdn_conv_b'])
    qkv = jax.nn.silu(qkv)
    q, k, v = [t.reshape(bsz, t_len, GDN_HEADS, GDN_HEAD_DIM)
               for t in _split_last(qkv, (D_BRANCH, D_BRANCH, D_BRANCH))]
    q, k = _l2norm(q), _l2norm(k)
    beta = jax.nn.sigmoid(b_raw)
    g = -jnp.exp(lw['gdn_a_log']) * jax.nn.softplus(a_raw + lw['gdn_dt_bias'])
    o, s_gdn_new = _with_prefix(_gdn_chunked, (q, k, v, beta, g), s_gdn, prefix, GDN_CHUNK)
    y_c = _rms_heads(o, lw['gdn_norm_g']).reshape(bsz, t_len, D_BRANCH) * jax.nn.silu(z_c)

    lb_h = lb.reshape(HG_HEADS, HG_HEAD_DIM)
    f_h = f_d.reshape(bsz, t_len, HG_HEADS, HG_HEAD_DIM)
    logf = jnp.logaddexp(jnp.log(lb_h), jnp.log1p(-lb_h) + jax.nn.log_sigmoid(f_h))
    k_h = (1.0 - lb_h) * jax.nn.sigmoid(-f_h)
    q_h = jax.nn.silu(q_d).reshape(bsz, t_len, HG_HEADS, HG_HEAD_DIM)
    v_h = i_d.reshape(bsz, t_len, HG_HEADS, HG_HEAD_DIM)
    o, s_hg_new = _with_prefix(_hgrn_chunked, (q_h, k_h, v_h, logf), s_hg, prefix, HG_CHUNK)
    y_d = _rms_heads(o, lw['hg_norm_g']).reshape(bsz, t_len, D_BRANCH) * jax.nn.silu(z_d)

    mix = jnp.concatenate([y_a, y_b, y_c, y_d], axis=-1).astype(x.dtype)
    out = jnp.einsum('btc,cd->btd', mix, lw['w_out'])
    x_new = _layernorm(DN_ALPHA * x + out, lw['ln_g'], lw['ln_b'])
    return x_new, (s_ssd_new, c_ssd_new, s5r_new, s5i_new, s_gdn_new, c_gdn_new, s_hg_new)


def setup_inputs(seed: int = 0) -> dict:
    key = jax.random.key(seed)
    keys = iter(jax.random.split(key, 64))
    f32 = jnp.float32
    L = DEPTH

    def nrm(shape, scale):
        return scale * jax.random.normal(next(keys), shape, f32)

    def unif(shape, lo, hi):
        return jax.random.uniform(next(keys), shape, f32, lo, hi)

    def dt_bias(shape):
        dt = jnp.exp(unif(shape, math.log(1e-3), math.log(1e-1)))
        return dt + jnp.log(-jnp.expm1(-dt))

    inp = {}
    inp['x_prompt'] = nrm((BATCH, SEQ, D_MODEL), 1.0)
    inp['x_sample'] = nrm((DEC_BATCH, DEC_SEQ, D_MODEL), 1.0)
    inp['state_ssd'] = nrm((L, DEC_BATCH, SSD_HEADS, SSD_STATE, SSD_HEAD_DIM), 0.1)
    inp['state_ssd_conv'] = nrm((L, DEC_BATCH, SSD_CONV - 1, SSD_XBC), 1.0)
    inp['state_s5_re'] = nrm((L, DEC_BATCH, S5_GROUPS, S5_STATE), 0.1)
    inp['state_s5_im'] = nrm((L, DEC_BATCH, S5_GROUPS, S5_STATE), 0.1)
    inp['state_gdn'] = nrm((L, DEC_BATCH, GDN_HEADS, GDN_HEAD_DIM, GDN_HEAD_DIM), 0.1)
    inp['state_gdn_conv'] = nrm((L, DEC_BATCH, GDN_CONV - 1, GDN_QKV), 1.0)
    inp['state_hgrn'] = nrm((L, DEC_BATCH, HG_HEADS, HG_HEAD_DIM, HG_HEAD_DIM), 0.1)
    inp['meta_tokens'] = nrm((N_META, D_MODEL), 1.0)
    inp['ln_in_g'] = 1.0 + nrm((D_MODEL,), 0.02)
    inp['ln_in_b'] = nrm((D_MODEL,), 0.02)
    inp['w_in'] = nrm((L, D_MODEL, D_IN), D_MODEL ** -0.5)
    inp['ssd_conv_w'] = nrm((L, SSD_CONV, SSD_XBC), SSD_CONV ** -0.5)
    inp['ssd_conv_b'] = nrm((L, SSD_XBC), 0.02)
    inp['ssd_dt_bias'] = dt_bias((L, SSD_HEADS))
    inp['ssd_a_log'] = jnp.log(unif((L, SSD_HEADS), 1.0, 16.0))
    inp['ssd_d'] = 1.0 + nrm((L, SSD_HEADS), 0.1)
    inp['ssd_norm_g'] = 1.0 + nrm((L, SSD_HEADS, SSD_HEAD_DIM), 0.02)
    inp['s5_lam_re'] = -0.5 + nrm((L, S5_GROUPS, S5_STATE), 0.01)
    inp['s5_lam_im'] = math.pi * jnp.arange(S5_STATE, dtype=f32) + nrm((L, S5_GROUPS, S5_STATE), 0.01)
    inp['s5_log_dt'] = unif((L, S5_GROUPS), math.log(1e-3), math.log(1e-1))
    inp['s5_b_re'] = nrm((L, S5_GROUPS, S5_STATE, S5_GROUP), (2 * S5_GROUP) ** -0.5)
    inp['s5_b_im'] = nrm((L, S5_GROUPS, S5_STATE, S5_GROUP), (2 * S5_GROUP) ** -0.5)
    inp['s5_c_re'] = nrm((L, S5_GROUPS, S5_GROUP, S5_STATE), S5_STATE ** -0.5)
    inp['s5_c_im'] = nrm((L, S5_GROUPS, S5_GROUP, S5_STATE), S5_STATE ** -0.5)
    inp['s5_d'] = nrm((L, S5_GROUPS, S5_GROUP), 1.0)
    inp['s5_glu_w'] = nrm((L, D_BRANCH, D_BRANCH), D_BRANCH ** -0.5)
    inp['s5_glu_b'] = nrm((L, D_BRANCH), 0.02)
    inp['gdn_conv_w'] = nrm((L, GDN_CONV, GDN_QKV), GDN_CONV ** -0.5)
    inp['gdn_conv_b'] = nrm((L, GDN_QKV), 0.02)
    inp['gdn_a_log'] = jnp.log(unif((L, GDN_HEADS), 1.0, 16.0))
    inp['gdn_dt_bias'] = dt_bias((L, GDN_HEADS))
    inp['gdn_norm_g'] = 1.0 + nrm((L, GDN_HEADS, GDN_HEAD_DIM), 0.02)
    inp['hg_lb_raw'] = nrm((L, D_BRANCH), 0.5)
    inp['hg_norm_g'] = 1.0 + nrm((L, HG_HEADS, HG_HEAD_DIM), 0.02)
    inp['w_out'] = nrm((L, D_MIX, D_MODEL), D_MIX ** -0.5 * DN_BETA)
    inp['ln_g'] = 1.0 + nrm((L, D_MODEL), 0.02)
    inp['ln_b'] = nrm((L, D_MODEL), 0.02)
    return inp


def reference(x_prompt, x_sample, state_ssd, state_ssd_conv, state_s5_re, state_s5_im, state_gdn,
              state_gdn_conv, state_hgrn, meta_tokens, ln_in_g, ln_in_b, w_in, ssd_conv_w, ssd_conv_b,
              ssd_dt_bias, ssd_a_log, ssd_d, ssd_norm_g, s5_lam_re, s5_lam_im, s5_log_dt, s5_b_re,
              s5_b_im, s5_c_re, s5_c_im, s5_d, s5_glu_w, s5_glu_b, gdn_conv_w, gdn_conv_b, gdn_a_log,
              gdn_dt_bias, gdn_norm_g, hg_lb_raw, hg_norm_g, w_out, ln_g, ln_b):
    f32 = jnp.float32
    bp = x_prompt.shape[0]
    lbs = _hgrn_lower_bounds(hg_lb_raw)
    meta = jnp.broadcast_to(meta_tokens[None].astype(x_prompt.dtype), (bp, N_META, D_MODEL))
    hp = _layernorm(jnp.concatenate([meta, x_prompt], axis=1), ln_in_g, ln_in_b)
    hs = _layernorm(x_sample, ln_in_g, ln_in_b)
    zero_st = (jnp.zeros((bp, SSD_HEADS, SSD_STATE, SSD_HEAD_DIM), f32),
               jnp.zeros((bp, SSD_CONV - 1, SSD_XBC), f32),
               jnp.zeros((bp, S5_GROUPS, S5_STATE), f32),
               jnp.zeros((bp, S5_GROUPS, S5_STATE), f32),
               jnp.zeros((bp, GDN_HEADS, GDN_HEAD_DIM, GDN_HEAD_DIM), f32),
               jnp.zeros((bp, GDN_CONV - 1, GDN_QKV), f32),
               jnp.zeros((bp, HG_HEADS, HG_HEAD_DIM, HG_HEAD_DIM), f32))
    p_states, s_states = [], []
    for l in range(DEPTH):
        lw = dict(w_in=w_in[l], ssd_conv_w=ssd_conv_w[l], ssd_conv_b=ssd_conv_b[l],
                  ssd_dt_bias=ssd_dt_bias[l], ssd_a_log=ssd_a_log[l], ssd_d=ssd_d[l],
                  ssd_norm_g=ssd_norm_g[l], s5_lam_re=s5_lam_re[l], s5_lam_im=s5_lam_im[l],
                  s5_log_dt=s5_log_dt[l], s5_b_re=s5_b_re[l], s5_b_im=s5_b_im[l], s5_c_re=s5_c_re[l],
                  s5_c_im=s5_c_im[l], s5_d=s5_d[l], s5_glu_w=s5_glu_w[l], s5_glu_b=s5_glu_b[l],
                  gdn_conv_w=gdn_conv_w[l], gdn_conv_b=gdn_conv_b[l], gdn_a_log=gdn_a_log[l],
                  gdn_dt_bias=gdn_dt_bias[l], gdn_norm_g=gdn_norm_g[l], hg_norm_g=hg_norm_g[l],
                  w_out=w_out[l], ln_g=ln_g[l], ln_b=ln_b[l])
        hp, stp = _layer(hp, zero_st, lw, lbs[l], N_META)
        st_in = (state_ssd[l], state_ssd_conv[l], state_s5_re[l], state_s5_im[l],
                 state_gdn[l], state_gdn_conv[l], state_hgrn[l])
        hs, sts = _layer(hs, st_in, lw, lbs[l], 0)
        p_states.append(stp)
        s_states.append(sts)
    p_ssd, p_ssd_conv, p_s5_re, p_s5_im, p_gdn, p_gdn_conv, p_hgrn = [
        jnp.stack([s[i] for s in p_states]) for i in range(7)]
    s_ssd, s_ssd_conv, s_s5_re, s_s5_im, s_gdn, s_gdn_conv, s_hgrn = [
        jnp.stack([s[i] for s in s_states]) for i in range(7)]
    y_prompt = hp[:, N_META:]
    y_sample = hs
    return (y_prompt, y_sample, p_ssd, p_ssd_conv, p_s5_re, p_s5_im, p_gdn, p_gdn_conv, p_hgrn,
            s_ssd, s_ssd_conv, s_s5_re, s_s5_im, s_gdn, s_gdn_conv, s_hgrn)
```

```python
import contextlib
import math
import numpy as np
import concourse.bass as bass
import concourse.mybir as mybir
from concourse.bass_utils import run_bass_kernel_spmd

F32 = mybir.dt.float32
BF16 = mybir.dt.bfloat16
ALU = mybir.AluOpType
AF = mybir.ActivationFunctionType
AX = mybir.AxisListType

NL = 4
NCORE = 8
D = 1024
NSEQ = 16
TS = 4
NROW = 16 + 2048 + NSEQ * TS
BIG = 30000.0
ENGS = ("pe", "dve", "act", "pool", "sp")

FM_ZA, FM_ZB, FM_ZC, FM_ZD = 0, 2, 4, 6
FM_CONV = 8
FM_UB = 18
FM_QD = 20
FM_FD = 22
NFM = 24
NTM = 268

_R = dict(za=(0, 256), xs=(256, 512), B=(512, 640), C=(640, 768), dt=(768, 772), zb=(772, 1028),
          ub=(1028, 1284), zc=(1284, 1540), q=(1540, 1796), k=(1796, 2052), v=(2052, 2308),
          br=(2308, 2312), ar=(2312, 2316), zd=(2316, 2572), qd=(2572, 2828), fd=(2828, 3084),
          idd=(3084, 3340))
_FM_ORDER = ["za", "zb", "zc", "zd", "xs", "B", "C", "q", "k", "v", "ub", "qd", "fd"]
_TM_ORDER = ["dt", "br", "ar", "idd"]
FM_COLS = np.concatenate([np.arange(*_R[n]) for n in _FM_ORDER])
TM_COLS = np.concatenate([np.arange(*_R[n]) for n in _TM_ORDER])

C_ID, C_ONES, C_ONES64 = 0, 1, 2
def C_TRI(s): return 3 + 4 * s
def C_TRISU(s): return 4 + 4 * s
def C_NBI(s): return 5 + 4 * s
def C_NBS(s): return 6 + 4 * s
def C_RST(s): return 15 + s
def C_BLK(s): return 18 + s
NCST = 21
CP_CW, CP_CB, CP_GSSD, CP_GGDN, CP_GHG, CP_DSSD, CP_GLUB, CP_S5D = 0, 40, 50, 52, 54, 56, 58, 60
NCP = 62


def _build_consts():
    c = np.zeros((NCST, 128, 128), np.float32)
    c[C_ID] = np.eye(128)
    c[C_ONES] = 1.0
    for a in range(2):
        c[C_ONES64, 64 * a:64 * a + 64, 64 * a:64 * a + 64] = 1.0
    idx = np.arange(128)
    for s, lb in enumerate((128, 64, 4)):
        blk = idx // lb
        same = blk[:, None] == blk[None, :]
        le = idx[:, None] <= idx[None, :]
        lt = idx[:, None] < idx[None, :]
        gt = idx[:, None] > idx[None, :]
        c[C_TRI(s)] = (same & le)
        c[C_TRISU(s)] = (same & gt)
        c[C_NBI(s)] = np.where(same & le, 0.0, -BIG)
        c[C_NBS(s)] = np.where(same & lt, 0.0, -BIG)
        c[C_RST(s)] = np.broadcast_to(np.where(idx % lb == 0, 0.0, 1.0)[None, :], (128, 128))
        nb = 128 // lb
        oh = np.zeros((128, 128), np.float32)
        oh[idx, blk] = 1.0
        c[C_BLK(s)] = oh
    return c


class Prog:
    def __init__(self, nc, stack):
        self.nc = nc
        self.stack = stack
        self.q = {e: [] for e in ENGS}
        self.cnt = {}
        self.sems = {}
        self.seen = {e: {} for e in ENGS}
        self.keys = {}
        self.nops = 0
        self.pe_last = {}
        self.rec = None
        for e in ENGS:
            self._sem("E_" + e)

    def _sem(self, name):
        if name not in self.sems:
            self.sems[name] = self.stack.enter_context(self.nc.semaphore(name))
            self.cnt[name] = 0
        return self.sems[name]

    def _collect(self, eng, reads, writes):
        waits = {}
        xr = [k for k in reads if k.startswith("ps")]
        if xr:
            writes = list(writes) + xr

        def add(ev):
            if ev is None:
                return
            s, v = ev
            if eng == "pe" and s == "E_pe":
                return
            if s.startswith("D_"):
                v = self.cnt[s]
            if self.seen[eng].get(s, 0) >= v:
                return
            if waits.get(s, 0) < v:
                waits[s] = v

        for k in reads:
            st = self.keys.get(k)
            if st is not None:
                add(st["w"])
        for k in writes:
            st = self.keys.get(k)
            if st is not None:
                add(st["w"])
                for s, v in st["r"].items():
                    add((s, v))
        for s, v in waits.items():
            self.seen[eng][s] = v
        return list(waits.items())

    def _commit(self, ev, reads, writes):
        xr = [k for k in reads if k.startswith("ps")]
        if xr:
            writes = list(writes) + xr
            reads = [k for k in reads if not k.startswith("ps")]
        for k in reads:
            st = self.keys.setdefault(k, {"w": None, "r": {}})
            if st["r"].get(ev[0], 0) < ev[1]:
                st["r"][ev[0]] = ev[1]
        for k in writes:
            self.keys[k] = {"w": ev, "r": {}}

    def op(self, eng, fn, reads=(), writes=(), pe_sig=None):
        if self.rec is not None:
            self.rec.append(("op", eng, fn, tuple(reads), tuple(writes), pe_sig))
            return
        waits = self._collect(eng, reads, writes)
        s = "E_" + eng
        if pe_sig is not None:
            sig = pe_sig
            for k in writes:
                if k.startswith("ps"):
                    last = self.pe_last.get(k)
                    if last is not None and last[0] != sig and self.seen["pe"].get("E_pe", 0) < last[1]:
                        waits = [w for w in waits if w[0] != "E_pe"] + [("E_pe", last[1])]
                        self.seen["pe"]["E_pe"] = last[1]
                    self.pe_last[k] = (sig, self.cnt[s] + 1)
        self.cnt[s] += 1
        self.nops += 1
        ev = (s, self.cnt[s])
        sems = self.sems

        def emit(e):
            for ws, wv in waits:
                e.wait_ge(sems[ws], wv)
            fn(e).then_inc(sems[s], 1)

        self.q[eng].append(emit)
        self._commit(ev, reads, writes)

    def dma(self, out, in_, group, reads=(), writes=(), eng="sp"):
        if self.rec is not None:
            self.rec.append(("dma", out, in_, group, tuple(reads), tuple(writes), eng))
            return
        waits = self._collect(eng, reads, writes)
        s = "D_" + group
        self._sem(s)
        self.cnt[s] += 16
        self.nops += 1
        ev = (s, self.cnt[s])
        sems = self.sems

        def emit(e):
            for ws, wv in waits:
                e.wait_ge(sems[ws], wv)
            e.dma_start(out=out, in_=in_).then_inc(sems[s], 16)

        self.q[eng].append(emit)
        self._commit(ev, reads, writes)

    def record(self, fn):
        assert self.rec is None
        self.rec = []
        try:
            fn()
        finally:
            lst, self.rec = self.rec, None
        return lst

    def replay(self, lists, ratio):
        pos = [0] * len(lists)
        while any(p < len(l) for p, l in zip(pos, lists)):
            for i, l in enumerate(lists):
                for _ in range(ratio[i]):
                    if pos[i] < len(l):
                        it = l[pos[i]]
                        pos[i] += 1
                        if it[0] == "op":
                            self.op(it[1], it[2], it[3], it[4], it[5])
                        else:
                            self.dma(it[1], it[2], it[3], it[4], it[5], it[6])

    def finish(self, eng="sp"):
        waits = [(s, v) for s, v in self.cnt.items()
                 if v > 0 and self.seen[eng].get(s, 0) < v and s != "E_" + eng]
        sems = self.sems

        def emit(e):
            for ws, wv in waits:
                e.wait_ge(sems[ws], wv)

        self.q[eng].append(emit)

    def emit(self):
        q = self.q
        with self.nc.Block() as block:
            @block.tensor
            def _(e):
                for f in q["pe"]:
                    f(e)

            @block.vector
            def _(e):
                for f in q["dve"]:
                    f(e)

            @block.scalar
            def _(e):
                for f in q["act"]:
                    f(e)

            @block.gpsimd
            def _(e):
                for f in q["pool"]:
                    f(e)

            @block.sync
            def _(e):
                for f in q["sp"]:
                    f(e)


class Builder:
    def __init__(self, nl, parts):
        self.nl = nl
        self.parts = parts
        self.nc = bass.Bass("TRN2", target_bir_lowering=False)
        self.st = contextlib.ExitStack()
        self.P = Prog(self.nc, self.st)
        self.tiles = [("meta", 0, 16)] + [("prompt", 16 + 128 * i, 128) for i in range(16)] \
            + [("sample", 16 + 2048, NSEQ * TS)]
        import os
        r = int(os.environ.get('KROT', '0'))
        import os
        self.use_fr = os.environ.get("KFR", "0") == "1"
        self.pools = {"main": [0, 1, 2, 3], "s5": [4, 5, 6], "pfx": [7]}
        self.pool = "main"

    def din(self, name, shape, dt=F32):
        return self.nc.dram_tensor(name, list(shape), dt, kind="ExternalInput")

    def dout(self, name, shape, dt=F32):
        return self.nc.dram_tensor(name, list(shape), dt, kind="ExternalOutput")

    def sb(self, name, shape, dt=F32):
        return self.st.enter_context(self.nc.sbuf_tensor(name, list(shape), dt))

    def op(self, eng, meth, *a, R=(), W=(), **kw):
        self.P.op(eng, lambda e: getattr(e, meth)(*a, **kw), reads=R, writes=W)

    @staticmethod
    def _sig(ap):
        k = ap.shape[0]
        kk = 32 if k <= 32 else (64 if k <= 64 else 128)
        return (ap.base_partition() if kk < 128 else 0, kk)

    def mm(self, out, lhsT, rhs, start=True, stop=True, R=(), W=(), fr=False):
        sig = self._sig(lhsT)
        if fr and self.use_fr and lhsT.shape[-1] == 128 and rhs.shape[-1] % 2 == 0:
            lhsT = lhsT.bitcast(mybir.dt.float32r)
            rhs = rhs.bitcast(mybir.dt.float32r)
        self.P.op("pe", lambda e: e.matmul(out, lhsT, rhs, start=start, stop=stop), reads=R, writes=W,
                  pe_sig=sig)

    def tr(self, out, in_, ident, R=(), W=()):
        self.P.op("pe", lambda e: e.transpose(out, in_, ident), reads=R, writes=W, pe_sig=self._sig(in_))

    def act(self, out, in_, func, bias=None, scale=None, R=(), W=()):
        kw = {}
        if bias is not None:
            kw["bias"] = bias
        if scale is not None:
            kw["scale"] = scale
        self.P.op("act", lambda e: e.activation(out, in_, func, **kw), reads=R, writes=W)

    def psa(self):
        return self.pools[self.pool].pop(0)

    def psf(self, b):
        self.pools["main" if b < 4 else ("pfx" if b == 7 else "s5")].append(b)

    def build(self):
        nc, nl = self.nc, self.nl
        sb = self.sb
        self.xin = self.din("xin", [NROW, D])
        self.wfm = self.din("wfm", [nl, D, NFM * 128])
        self.wtm = self.din("wtm", [nl, D, NTM])
        self.wout = self.din("wout", [nl, D, D])
        self.glu = self.din("glu", [nl, 256, 256])
        self.cstd = self.din("cst", [NCST, 128, 128])
        self.cpd = self.din("cp", [nl, 128, NCP])
        self.rpd = self.din("rp", [nl, 128, 16])
        self.lnd = self.din("lnp", [nl + 1, 2, 128, D])
        self.lbd = self.din("lbraw", [128, 2, NL])
        self.s5pd = self.din("s5p", [nl, 128, 3, 8])
        self.bld = self.din("s5bl", [nl, 128, 8, 2, 128])
        self.cld = self.din("s5cl", [nl, 128, 8, 2, 64])
        self.sstd = [self.din(n, [nl, NSEQ, 4, 64, 64]) for n in ("st_ssd", "st_gdn", "st_hg")]
        self.convd = self.din("st_conv", [nl, 128, 10, NSEQ, 3])
        self.s5sd = self.din("st_s5", [nl, 128, 8, 2, NSEQ])
        self.yout = self.dout("y_out", [2048 + NSEQ * TS, D])
        self.o_pst = self.dout("o_pst", [nl, 3, 128, 2, 64])
        self.o_sst = self.dout("o_sst", [nl, 3, 128, 2, NSEQ, 64])
        self.o_pconv = self.dout("o_pconv", [nl, 128, 10, 3])
        self.o_sconv = self.dout("o_sconv", [nl, 128, 10, NSEQ, 3])
        self.o_ps5 = self.dout("o_ps5", [nl, 128, 8, 2])
        self.o_ss5 = self.dout("o_ss5", [nl, 128, 8, 2, NSEQ])
        self.xscr = nc.dram_tensor("xscr", [NROW, D], F32, kind="Internal")

        self.ps = [self.st.enter_context(nc.psum_tensor(f"ps{i}", [128, 512], F32)) for i in range(8)]
        self.cst = sb("cstt", [128, NCST, 128])
        self.Wfm = sb("Wfm", [128, 8, NFM * 128], BF16)
        self.Wtm = sb("Wtm", [128, 8, NTM], BF16)
        self.Wout = sb("Wout", [128, 8, D], BF16)
        self.Wglu = sb("Wglu", [128, 2, 256], BF16)
        self.WB = sb("WB", [128, 2048])
        self.stg = [self.WB[:, 0:512], self.WB[:, 512:1024]]
        self.TMG = [sb("TMG0", [128, 512]), sb("TMG1", [128, 512])]
        self.VT4 = sb("VT4", [128, 256])
        self.NH = [sb(f"NH{i}", [128, 512], BF16) for i in range(4)]
        self.RWh = sb("RWh", [128, 512], BF16)
        self.AQW = sb("AQW", [128, 512])
        self.RW = sb("RW", [128, 512])
        self.lng = sb("lng", [128, D])
        self.lnb = sb("lnb", [128, D])
        self.cp = sb("cpt", [128, NCP])
        self.rp = sb("rpt", [128, 16])
        self.nega = sb("nega", [128, 8])
        self.lbr = sb("lbr", [128, 2, NL])
        self.lbt = sb("lbt", [128, 2, NL])
        self.hgc = sb("hgc", [128, 2, 3])
        self.xt = [sb(f"xt{i}", [128, D]) for i in range(2)]
        self.xT = sb("xT", [128, 8, 128], BF16)
        self.mixT = sb("mixT", [128, 8, 128], BF16)
        self.SZ = sb("SZ", [128, 8, 128], BF16)
        self.XC = sb("XC", [128, 10, 131])
        self.XCS = sb("XCS", [128, 10, NSEQ, 7])
        self.CS = sb("CS", [128, 10, NSEQ, 3])
        self.CP3 = sb("CP3", [128, 10, 3])
        self.CV = sb("CV", [128, 10, 128])
        self.UB = sb("UB", [128, 2, 128], BF16)
        self.QD = sb("QD", [128, 2, 128])
        self.TH = sb("TH", [128, 2, 128])
        self.TMs = sb("TMs", [128, NTM])
        self.SM = sb("SM", [128, 16])
        self.SM2 = sb("SM2", [128, 40])
        self.EB = sb("EB", [128, 2, 64])
        self.GM = sb("GM", [128, 2, 64])
        self.s5u = self.GM[:, 1, 0:32].rearrange("p (r g) -> p r g", r=4)
        self.PST = [sb(f"PST{m}", [128, 2, 64]) for m in range(3)]
        self.SST = [sb(f"SST{i}", [128, 2, NSEQ, 64]) for i in range(1)]
        self.scr = {i: sb(f"scr{i}", [128, 128]) for i in (list(range(0, 7)) + list(range(8, 28)) + [35, 36])}
        self.big = [sb("big0", [128, 1024]), self.WB[:, 1024:2048]]
        self.stat = sb("stat", [128, 32])
        self.s5p = sb("s5pt", [128, 3, 8])
        self.Bl = sb("Blt", [128, 8, 2, 128], BF16)
        self.Cl = sb("Clt", [128, 8, 2, 64], BF16)
        self.Hh = [sb(f"Hh{i}", [128, 2, 64], BF16) for i in range(4)]
        self.KI = sb("KI", [128, 8, 3, 64])
        self.KO = sb("KO", [128, 8, 3, 65])
        self.hglb = sb("hglb", [128, 2])
        self.s5t = sb("s5t", [128, 12, 8])
        self.S5C = sb("S5C", [128, 8, 2])
        self.S5S = sb("S5S", [128, 8, 2, NSEQ])
        self.S5O = sb("S5O", [128, 8, 2, NSEQ])
        self.s5w = [sb(f"s5w{i}", [128, 2, 128]) for i in range(8)]
        self.s5x = [sb(f"s5x{i}", [128, 2, 64]) for i in range(7)]
        sstf = self.SST[0][:].rearrange("p a s v -> p (a s v)")
        self.s5y = [sstf[:, 128 * i:128 * i + 128].rearrange("p (c t) -> p c t", c=2) for i in range(7)]
        self.s5z = [sstf[:, 1024 + 128 * i:1024 + 128 * i + 128].rearrange("p (c t) -> p c t", c=2) for i in range(7)]
        self.alias_keys = [f"s5y{i}" for i in range(7)] + [f"s5z{i}" for i in range(7)]
        self.Y5B = sb("Y5B", [128, 2, 128], BF16)
        self.s5i = sb("s5i", [128, 8], mybir.dt.int32)

        P = self.P
        P.dma(self.cst[:], self.cstd.ap().rearrange("c p j -> p c j"), "cst", writes=["cst"])
        P.dma(self.lbr[:], self.lbd.ap(), "cst", writes=["lbr"])
        import os
        self.stop = int(os.environ.get("KSTOP", "99"))
        if self.stop >= 1:
            self.hg_lower_bounds()
        lp = P.record(self.prologue)
        lw = P.record(lambda: self.load_w_in(0))
        P.replay([lp, lw], [1, 1])
        self.load_rest(0)
        self.load_out(0)
        for l in range(nl):
            self.layer(l)
        P.finish("sp")
        P.emit()
        self.st.close()
        return nc

    def C(self, idx, n0=None, n1=None):
        if n0 is None:
            return self.cst[:, idx, :]
        return self.cst[0:n0, idx, 0:n1]

    def hg_lower_bounds(self):
        op = self.op
        e = self.scr[0][:, 0:8].rearrange("p (b l) -> p b l", b=2)
        s = self.stat[:, 0:2]
        self.act(e, self.lbr[:], AF.Exp, R=["lbr"], W=["scr0"])
        op("dve", "tensor_reduce", s, e, AX.X, ALU.add, R=["scr0"], W=["stat"])
        op("dve", "reciprocal", s, s, R=["stat"], W=["stat"])
        op("dve", "tensor_tensor", e, e, s.unsqueeze(2).to_broadcast([128, 2, NL]), ALU.mult,
           R=["scr0", "stat"], W=["scr0"])
        op("dve", "memset", self.lbt[:, :, 0:1], 0.0, W=["lbt"])
        for l in range(1, NL):
            op("dve", "tensor_tensor", self.lbt[:, :, l:l + 1], self.lbt[:, :, l - 1:l], e[:, :, l:l + 1], ALU.add,
               R=["lbt", "scr0"], W=["lbt"])

    def layernorm(self, xt, xk, n):
        op = self.op
        st6 = self.scr[1][:, 0:12].rearrange("p (c s) -> p c s", c=2)
        mv = self.stat[:, 4:6]
        rs = self.stat[:, 6:7]
        for c in range(2):
            op("dve", "bn_stats", st6[0:n, c, :], xt[0:n, 512 * c:512 * c + 512], R=[xk], W=["scr1"])
        op("dve", "bn_aggr", mv[0:n], st6[0:n].rearrange("p c s -> p (c s)"), R=["scr1"], W=["stat"])
        self.act(rs[0:n], mv[0:n, 1:2], AF.Ln, bias=self.epsc[0:n, 0:1], R=["stat", "epsc"], W=["stat"])
        self.act(rs[0:n], rs[0:n], AF.Exp, scale=-0.5, R=["stat"], W=["stat"])
        op("dve", "scalar_tensor_tensor", xt[0:n], xt[0:n], mv[0:n, 0:1], self.lng[0:n], ALU.subtract, ALU.mult,
           R=[xk, "stat", "lng"], W=[xk])
        op("dve", "scalar_tensor_tensor", xt[0:n], xt[0:n], rs[0:n], self.lnb[0:n], ALU.mult, ALU.add,
           R=[xk, "stat", "lnb"], W=[xk])

    def prologue(self):
        P = self.P
        self.epsc = self.sb("epsc", [128, 4])
        self.op("dve", "memset", self.epsc[:, 0:1], 1e-5, W=["epsc"])
        self.op("dve", "memset", self.epsc[:, 1:2], 1e-6, W=["epsc"])
        self.op("dve", "memset", self.epsc[:, 2:3], 1.0, W=["epsc"])
        P.dma(self.lng[:], self.lnd.ap()[0, 0], "ln", writes=["lng"])
        P.dma(self.lnb[:], self.lnd.ap()[0, 1], "ln", writes=["lnb"])
        for i, (kind, r0, n) in enumerate(self.tiles):
            xt, xk = self.xt[i % 2], f"xt{i % 2}"
            P.dma(xt[0:n], self.xin.ap()[r0:r0 + n, :], xk, writes=[xk])
            self.layernorm(xt, xk, n)
            P.dma(self.xscr.ap()[r0:r0 + n, :], xt[0:n], xk, reads=[xk], writes=[f"xscr{i}"])

    def _load_cast(self, jobs, stg, keys):
        P = self.P
        for ci, (src, dst, key, w) in enumerate(jobs):
            si = ci % 2
            P.dma(stg[si][:, 0:w], src, keys[si], writes=[keys[si]])
            if ci % 2 == 1:
                self.op("pool", "tensor_copy", dst, stg[si][:, 0:w], R=[keys[si]], W=[key])
            else:
                self.op("act", "copy", dst, stg[si][:, 0:w], R=[keys[si]], W=[key])

    @staticmethod
    def _jobs(jobs, src2d, dst2d, key, width):
        c = 0
        while c < width:
            w = min(512, width - c)
            jobs.append((src2d[:, c:c + w], dst2d[:, c:c + w], key, w))
            c += w

    def load_w_in(self, l):
        jobs = []
        for k in range(8):
            self._jobs(jobs, self.wfm.ap()[l, 128 * k:128 * k + 128, :], self.Wfm[:, k, :], "Wfm", NFM * 128)
        for k in range(8):
            self._jobs(jobs, self.wtm.ap()[l, 128 * k:128 * k + 128, :], self.Wtm[:, k, :], "Wtm", NTM)
        self._load_cast(jobs, self.TMG, ["TMG0", "TMG1"])

    def load_out(self, l):
        P = self.P
        jobs = []
        for k in range(8):
            self._jobs(jobs, self.wout.ap()[l, 128 * k:128 * k + 128, :], self.Wout[:, k, :], "Wout", D)
        self._load_cast(jobs, self.stg, ["wb0", "wb1"])
        P.dma(self.lng[:], self.lnd.ap()[l + 1, 0], "ln", writes=["lng"])
        P.dma(self.lnb[:], self.lnd.ap()[l + 1, 1], "ln", writes=["lnb"])

    def load_rest(self, l):
        P = self.P
        jobs = []
        for k in range(2):
            self._jobs(jobs, self.glu.ap()[l, 128 * k:128 * k + 128, :], self.Wglu[:, k, :], "Wglu", 256)
        self._load_cast(jobs, self.stg, ["wb0", "wb1"])
        P.dma(self.cp[:], self.cpd.ap()[l], "prm", writes=["cp"])
        P.dma(self.rp[:], self.rpd.ap()[l], "prm", writes=["rp"])
        P.dma(self.s5p[:], self.s5pd.ap()[l], "prm", writes=["s5p"])
        jobs = []
        self._jobs(jobs, self.bld.ap()[l].rearrange("p g c m -> p (g c m)"), self.Bl[:].rearrange("p g c m -> p (g c m)"), "Bl", 2048)
        self._jobs(jobs, self.cld.ap()[l].rearrange("p g c m -> p (g c m)"), self.Cl[:].rearrange("p g c m -> p (g c m)"), "Cl", 1024)
        self._load_cast(jobs, self.stg, ["wb0", "wb1"])
        self.act(self.nega[:, 0:4], self.rp[:, 4:8], AF.Exp, R=["rp"], W=["nega"])
        self.act(self.nega[:, 4:8], self.rp[:, 12:16], AF.Exp, R=["rp"], W=["nega"])
        self.op("dve", "tensor_scalar", self.nega[:], self.nega[:], -1.0, None, ALU.mult, R=["nega"], W=["nega"])
        lb = self.lbt[:, :, l:l + 1]
        self.op("dve", "tensor_scalar", self.hgc[:, :, 0:1], lb, -0.5, 0.5, ALU.mult, ALU.add, R=["lbt"], W=["hgc"])
        self.op("dve", "tensor_scalar", self.hgc[:, :, 1:2], lb, 0.5, 0.5, ALU.mult, ALU.add, R=["lbt"], W=["hgc"])
        self.op("dve", "tensor_scalar", self.hgc[:, :, 2:3], lb, 0.5, -0.5, ALU.mult, ALU.add, R=["lbt"], W=["hgc"])
        if "s5" in self.parts:
            self.s5_prepare(l)

    def layer(self, l):
        P = self.P
        for m in range(3):
            self.op("pool", "memset", self.PST[m][:], 0.0, W=[f"PST{m}"])
        self.op("pool", "memset", self.XC[:, :, 0:3], 0.0, W=["XCa", "XCb"])
        self.op("pool", "memset", self.S5C[:], 0.0, W=["S5C"])
        if "s5" not in self.parts:
            self.op("pool", "memset", self.S5O[:], 0.0, W=["S5O"])
        P.dma(self.CS[:], self.convd.ap()[l], "CS", writes=["CS"])
        self.op("pool", "tensor_copy", self.XCS[:, :, :, 0:3], self.CS[:], R=["CS"], W=["XCSa", "XCSb"])
        P.dma(self.S5S[:], self.s5sd.ap()[l], "S5S", writes=["S5S"])
        tiles = list(enumerate(self.tiles))
        if self.stop == 6:
            tiles = [t for t in tiles if t[0] in (0, 1, 17)]
        pre0 = P.record(lambda: self.pre(l, *tiles[0]))
        P.replay([pre0], [1])
        more = (l + 1 < self.nl)
        for j, (i, tl) in enumerate(tiles):
            lastt = (j + 1 == len(tiles))
            extra = [P.record(lambda: self.load_w_in(l + 1))] if (lastt and more) else []
            self.mix(l, i, tl, extra)
            lp = P.record(lambda: self.post(l, i, tl))
            if not lastt:
                ln_ = P.record(lambda: self.pre(l, *tiles[j + 1]))
                P.replay([ln_, lp], [max(1, round(len(ln_) / max(1, len(lp)))), 1])
            elif more:
                lr = P.record(lambda: self.load_rest(l + 1))
                P.replay([lr, lp], [max(1, round(len(lr) / max(1, len(lp)))), 1])
                self.load_out(l + 1)
            else:
                P.replay([lp], [1])
        if self.stop < 99 and self.stop != 6:
            return
        self.op("pool", "tensor_copy", self.CS[:], self.XCS[:, :, :, 4:7], R=["XCSa", "XCSb"], W=["CS"])
        P.dma(self.o_sconv.ap()[l], self.CS[:], "CS", reads=["CS"])
        P.dma(self.o_ss5.ap()[l], self.S5O[:], "S5O", reads=["S5O"])

    def pre(self, l, i, tl):
        kind, r0, n = tl
        op, mm, act, tr = self.op, self.mm, self.act, self.tr
        ps = self.ps
        self.pool = "main"
        xt, xk = self.xt[i % 2], f"xt{i % 2}"
        self.P.dma(xt[0:n], self.xscr.ap()[r0:r0 + n, :], xk, reads=[f"xscr{i}"], writes=[xk])
        for half in range(2):
            b = self.psa()
            for j in range(4):
                k = 4 * half + j
                tr(ps[b][:, 128 * j:128 * j + n], xt[0:n, 128 * k:128 * k + 128], self.cst[0:n, C_ID, 0:n],
                   R=[xk, "cst"], W=[f"ps{b}"])
            src = ps[b][:].rearrange("p (j t) -> p j t", j=4)[:, :, 0:n]
            if half == 0:
                op("act", "copy", self.xT[:, 0:4, 0:n], src, R=[f"ps{b}"], W=["xT"])
            else:
                op("dve", "tensor_copy", self.xT[:, 4:8, 0:n], src, R=[f"ps{b}"], W=["xT"])
            self.psf(b)
        b = self.psa()
        for k in range(8):
            mm(ps[b][0:n, 0:NTM], self.xT[:, k, 0:n], self.Wtm[:, k, :], start=(k == 0), stop=(k == 7),
               R=["xT", "Wtm"], W=[f"ps{b}"])
        op("dve", "tensor_copy", self.TMs[0:n, :], ps[b][0:n, 0:NTM], R=[f"ps{b}"], W=["TMs"])
        self.psf(b)
        sample = (kind == "sample")

        def stage_a(g):
            bT = self.psa()
            for k in range(8):
                mm(ps[bT][0:n, :], self.xT[:, k, 0:n], self.Wfm[:, k, 512 * g:512 * g + 512],
                   start=(k == 0), stop=(k == 7), R=["xT", "Wfm"], W=[f"ps{bT}"])
            return bT

        def stage_b(g, bT):
            tg = self.TMG[g % 2]
            tgk = f"TMG{g % 2}"
            if g % 2 == 0:
                op("act", "copy", tg[0:n, :], ps[bT][0:n, :], R=[f"ps{bT}"], W=[tgk])
            else:
                op("dve", "tensor_copy", tg[0:n, :], ps[bT][0:n, :], R=[f"ps{bT}"], W=[tgk])
            self.psf(bT)
            b = self.psa()
            for j in range(4):
                tr(ps[b][:, 128 * j:128 * j + n], tg[0:n, 128 * j:128 * j + 128], self.cst[0:n, C_ID, 0:n],
                   R=[tgk, "cst"], W=[f"ps{b}"])
            src = ps[b][:].rearrange("p (j t) -> p j t", j=4)[:, :, 0:n]
            R = [f"ps{b}"]
            if g < 2:
                act(self.SZ[:, 4 * g:4 * g + 4, 0:n], src, AF.Silu, R=R, W=["SZ"])
            elif g <= 4:
                c0 = 4 * (g - 2)
                nb_ = 4 if g < 4 else 2
                ck = "a" if g == 2 else "b"
                if sample:
                    dst = self.XCS[:, c0:c0 + nb_, :, 3:7]
                    s4 = ps[b][:].rearrange("p (j s t) -> p j s t", j=4, t=TS)[:, 0:nb_, 0:NSEQ, :]
                    op("act", "copy", dst, s4, R=R, W=["XCS" + ck])
                else:
                    op("act", "copy", self.XC[:, c0:c0 + nb_, 3:3 + n], src[:, 0:nb_, :], R=R, W=["XC" + ck])
                if g == 4:
                    op("act", "copy", self.UB[:, :, 0:n], src[:, 2:4, :], R=R, W=["UB"])
            else:
                act(self.QD[:, :, 0:n], src[:, 0:2, :], AF.Silu, R=R, W=["QD"])
                act(self.TH[:, :, 0:n], src[:, 2:4, :], AF.Tanh, scale=0.5, R=R, W=["TH"])
            self.psf(b)

        order = [0, 1, 5, 2, 3, 4]
        banks = {}
        banks[order[0]] = stage_a(order[0])
        for q, g in enumerate(order):
            if q + 1 < len(order):
                banks[order[q + 1]] = stage_a(order[q + 1])
            stage_b(g, banks[g])

    def mix(self, l, i, tl, extra=()):
        kind, r0, n = tl
        op = self.op

        def s5_prefix():
            self.pool = "pfx"
            if "gdn" in self.parts:
                self.conv(kind, n, range(4, 10))
                self.gdn_prep(kind, n)
            self.pool = "main"

        def s5_body():
            self.pool = "s5"
            if "s5" in self.parts:
                self.s5_tile(l, kind, n)
            else:
                op("pool", "memset", self.mixT[:, 2:4, 0:n], 0.0, W=["mixT"])
            self.pool = "main"

        def main1():
            self.pool = "main"
            self.conv(kind, n, range(0, 4))
            if "gdn" not in self.parts:
                self.conv(kind, n, range(4, 10))
            self.decay_prelude(kind, n)
            if "ssd" in self.parts:
                self.ssd_tile(l, kind, n)
            else:
                op("pool", "memset", self.mixT[:, 0:2, 0:n], 0.0, W=["mixT"])

        def main2():
            self.pool = "main"
            if "gdn" in self.parts:
                self.gdn_tile(l, kind, n)
            else:
                op("pool", "memset", self.mixT[:, 4:6, 0:n], 0.0, W=["mixT"])
            if "hg" in self.parts:
                self.hg_tile(l, kind, n)
            else:
                op("pool", "memset", self.mixT[:, 6:8, 0:n], 0.0, W=["mixT"])

        P = self.P
        pre_ = P.record(s5_prefix)
        body = P.record(s5_body)
        m1 = P.record(main1)
        m2 = P.record(main2)
        assert len(pre_) <= len(m1), (len(pre_), len(m1))
        kb = min(len(body), round(len(m1) * len(body) / max(1, len(m1) + len(m2))))
        P.replay([m1, pre_, body[:kb]], [max(1, round(len(m1) / max(1, kb))), max(1, round(len(pre_) / max(1, kb))), 1])
        s5l = body
        k1 = kb
        rest = s5l[k1:]
        lists = [m2, rest] + list(extra)
        ratio = [max(1, round(len(m2) / max(1, len(rest)))), 1]
        for x in extra:
            ratio.append(max(1, round(len(x) / max(1, len(rest)))))
        P.replay(lists, ratio)

    def post(self, l, i, tl):
        kind, r0, n = tl
        op, mm = self.op, self.mm
        ps = self.ps
        P = self.P
        self.pool = "s5"
        xt, xk = self.xt[i % 2], f"xt{i % 2}"
        last = (l == self.nl - 1)
        alpha = (2.0 * NL) ** 0.25
        for half in range(2):
            b = self.psa()
            for k in range(8):
                mm(ps[b][0:n, :], self.mixT[:, k, 0:n], self.Wout[:, k, 512 * half:512 * half + 512],
                   start=(k == 0), stop=(k == 7), R=["mixT", "Wout"], W=[f"ps{b}"])
            op("dve", "scalar_tensor_tensor", xt[0:n, 512 * half:512 * half + 512], xt[0:n, 512 * half:512 * half + 512],
               alpha, ps[b][0:n, :], ALU.mult, ALU.add, R=[xk, f"ps{b}"], W=[xk])
            self.psf(b)
        self.pool = "main"
        self.layernorm(xt, xk, n)
        if not last:
            P.dma(self.xscr.ap()[r0:r0 + n, :], xt[0:n], xk, reads=[xk], writes=[f"xscr{i}"])
        elif kind != "meta":
            o0 = r0 - 16
            P.dma(self.yout.ap()[o0:o0 + n, :], xt[0:n], xk, reads=[xk])
        if kind == "prompt" and i == 16:
            for m in range(3):
                P.dma(self.o_pst.ap()[l, m], self.PST[m][:], f"PST{m}", reads=[f"PST{m}"])
            self.op("pool", "tensor_copy", self.CP3[:], self.XC[:, :, 0:3], R=["XCa", "XCb"], W=["CP3"])
            P.dma(self.o_pconv.ap()[l], self.CP3[:], "CP3", reads=["CP3"])
            P.dma(self.o_ps5.ap()[l], self.S5C[:], "S5C", reads=["S5C"])

    def conv(self, kind, n, cbs):
        op, act = self.op, self.act
        sample = (kind == "sample")
        cbs = list(cbs)
        ck = "a" if cbs[0] < 4 else "b"
        for cb in cbs:
            acc = self.scr[2 + cb % 2] if ck == "a" else self.scr[0 + cb % 2]
            ak = f"scr{2 + cb % 2}" if ck == "a" else f"scr{0 + cb % 2}"
            w = lambda j: self.cp[:, CP_CW + 4 * cb + j:CP_CW + 4 * cb + j + 1]
            bias = self.cp[:, CP_CB + cb:CP_CB + cb + 1]
            if sample:
                a3 = acc[:, 0:n].rearrange("p (s t) -> p s t", t=TS)
                xin = lambda j: self.XCS[:, cb, :, j:j + TS]
                key = "XCS" + ck
            else:
                a3 = acc[:, 0:n]
                xin = lambda j: self.XC[:, cb, j:j + n]
                key = "XC" + ck
            act(a3, xin(0), AF.Identity, scale=w(0), bias=bias, R=[key, "cp"], W=[ak])
            for j in range(1, 4):
                op("dve", "scalar_tensor_tensor", a3, xin(j), w(j), a3, ALU.mult, ALU.add, R=[key, "cp", ak], W=[ak])
            act(self.CV[:, cb, 0:n], acc[:, 0:n], AF.Silu, R=[ak], W=["CV" + ck])
        if not sample:
            c0, c1 = cbs[0], cbs[-1] + 1
            op("pool", "tensor_copy", self.XC[:, c0:c1, 0:3], self.XC[:, c0:c1, n:n + 3], R=["XC" + ck], W=["XC" + ck])

    def struct(self, kind, mixer):
        if kind == "meta":
            return 0, 16, 1, [[0]]
        if kind == "sample":
            return 2, TS, NSEQ, [list(range(NSEQ))]
        if mixer == "ssd":
            return 0, 128, 1, [[0]]
        return 1, 64, 2, [[0], [1]]

    def decay_prelude(self, kind, n):
        op, mm, act = self.op, self.mm, self.act
        ps = self.ps
        SM, SM2, TMs = self.SM, self.SM2, self.TMs
        tmp = SM2[:, 36:40]
        op("dve", "tensor_tensor", SM[0:n, 0:4], TMs[0:n, 0:4], self.rp[0:n, 0:4], ALU.add, R=["TMs", "rp"], W=["SM"])
        op("dve", "tensor_tensor", SM[0:n, 12:16], TMs[0:n, 8:12], self.rp[0:n, 8:12], ALU.add, R=["TMs", "rp"], W=["SM"])
        for c in (0, 12):
            act(SM[0:n, c:c + 4], SM[0:n, c:c + 4], AF.Exp, R=["SM"], W=["SM"])
            act(SM[0:n, c:c + 4], SM[0:n, c:c + 4], AF.Ln, bias=self.epsc[0:n, 2:3], R=["SM", "epsc"], W=["SM"])
        op("dve", "tensor_tensor", SM[0:n, 4:8], SM[0:n, 0:4], self.nega[0:n, 0:4], ALU.mult, R=["SM", "nega"], W=["SM"])
        op("dve", "tensor_tensor", SM[0:n, 12:16], SM[0:n, 12:16], self.nega[0:n, 4:8], ALU.mult, R=["SM", "nega"], W=["SM"])
        act(SM[0:n, 8:12], TMs[0:n, 4:8], AF.Exp, scale=-1.0, R=["TMs"], W=["SM"])
        op("dve", "tensor_scalar", SM[0:n, 8:12], SM[0:n, 8:12], 1.0, None, ALU.add, R=["SM"], W=["SM"])
        op("dve", "reciprocal", SM[0:n, 8:12], SM[0:n, 8:12], R=["SM"], W=["SM"])
        b = self.psa()
        for mi, (mixer, col) in enumerate((("ssd", 4), ("gdn", 12))):
            s, lb, nb, _ = self.struct(kind, mixer)
            g = SM[0:n, col:col + 4]
            mm(ps[b][0:n, 8 * mi:8 * mi + 4], self.C(C_TRI(s), n, n), g, R=["cst", "SM"], W=[f"ps{b}"])
            mm(ps[b][0:n, 8 * mi + 4:8 * mi + 8], self.C(C_TRISU(s), n, n), g, R=["cst", "SM"], W=[f"ps{b}"])
            gm2 = self.GM[0:n, mi, 0:4 * nb]
            gm = gm2.rearrange("p (h b) -> p h b", h=4)
            op("dve", "tensor_tensor", gm, g.unsqueeze(2).to_broadcast([n, 4, nb]),
               self.cst[0:n, C_BLK(s), 0:nb].unsqueeze(1).to_broadcast([n, 4, nb]), ALU.mult,
               R=["SM", "cst"], W=["GM"])
            mm(ps[b][:, 32 + 64 * mi:32 + 64 * mi + 4 * nb], self.cst[0:n, C_ONES, :], gm2, R=["cst", "GM"], W=[f"ps{b}"])
            act(self.EB[:, mi, 0:4 * nb], ps[b][:, 32 + 64 * mi:32 + 64 * mi + 4 * nb], AF.Exp, R=[f"ps{b}"], W=["EB"])
        R = [f"ps{b}"]
        op("dve", "tensor_copy", SM2[0:n, 0:4], ps[b][0:n, 0:4], R=R, W=["SM2"])
        op("dve", "tensor_scalar", SM2[0:n, 4:8], ps[b][0:n, 0:4], -1.0, None, ALU.mult, R=R, W=["SM2"])
        op("dve", "tensor_copy", SM2[0:n, 8:12], ps[b][0:n, 8:12], R=R, W=["SM2"])
        op("dve", "tensor_scalar", SM2[0:n, 12:16], ps[b][0:n, 8:12], -1.0, None, ALU.mult, R=R, W=["SM2"])
        act(SM2[0:n, 16:20], ps[b][0:n, 4:8], AF.Exp, R=R, W=["SM2"])
        act(SM2[0:n, 20:24], ps[b][0:n, 12:16], AF.Exp, R=R, W=["SM2"])
        act(SM2[0:n, 24:28], ps[b][0:n, 8:12], AF.Exp, R=R, W=["SM2"])
        self.psf(b)
        op("dve", "tensor_scalar", SM2[0:n, 28:32], SM[0:n, 8:12], -1.0, None, ALU.mult, R=["SM"], W=["SM2"])
        op("dve", "tensor_tensor", SM2[0:n, 32:36], SM2[0:n, 20:24], SM[0:n, 8:12], ALU.mult, R=["SM", "SM2"], W=["SM2"])

    def rowbc(self, out_ps, col, n, s, mask, nrows=None, pbase=0, W=()):
        gb = self.scr[4]
        m = nrows if nrows is not None else n
        self.op("dve", "tensor_scalar", gb[0:n, 0:m], self.cst[0:n, C_ONES, 0:m], col, None, ALU.mult,
                R=["cst", "SM", "SM2"], W=["scr4"])
        self.mm(out_ps, gb[0:n, 0:m], self.C(C_TRI(s), n, n), start=True, stop=(mask is None),
                R=["scr4", "cst"], W=W)
        if mask is not None:
            self.mm(out_ps, self.cst[0:n, C_ID, 0:m], self.C(mask, n, n), start=False, stop=True,
                    R=["cst"], W=W)

    def rms_gate(self, y, yk, n, gcol, zblk, mblk):
        op, mm, act = self.op, self.mm, self.act
        sq = self.scr[5]
        act(sq[:, 0:n], y, AF.Square, R=[yk], W=["scr5"])
        b = self.psa()
        mm(self.ps[b][:, 0:n], self.C(C_ONES64), sq[:, 0:n], R=["cst", "scr5"], W=[f"ps{b}"])
        rs = self.scr[6]
        act(rs[:, 0:n], self.ps[b][:, 0:n], AF.Ln, scale=1.0 / 64.0, bias=self.epsc[:, 1:2], R=[f"ps{b}", "epsc"], W=["scr6"])
        self.psf(b)
        act(rs[:, 0:n], rs[:, 0:n], AF.Exp, scale=-0.5, R=["scr6"], W=["scr6"])
        op("dve", "scalar_tensor_tensor", sq[:, 0:n], y, self.cp[:, gcol:gcol + 1], rs[:, 0:n], ALU.mult, ALU.mult,
           R=[yk, "cp", "scr6"], W=["scr5"])
        op("dve", "tensor_tensor", self.mixT[:, mblk, 0:n], sq[:, 0:n], self.SZ[:, zblk, 0:n], ALU.mult,
           R=["scr5", "SZ"], W=["mixT"])

    def ssd_tile(self, l, kind, n):
        op, mm, act, tr = self.op, self.mm, self.act, self.tr
        ps, scr = self.ps, self.scr
        s, lb, nb, rounds = self.struct(kind, "ssd")
        sample = (kind == "sample")
        CV, SM, SM2 = self.CV, self.SM, self.SM2
        if sample:
            self.load_sst(l, 0, 0)
        b = self.psa()
        for j in range(3):
            tr(ps[b][0:n, 128 * j:128 * j + 128], CV[:, j, 0:n], self.C(C_ID), R=["CVa", "cst"], W=[f"ps{b}"])
        xdt = scr[8]
        xdt2 = self.big[0]
        xdtv = xdt2[0:n, 0:256].rearrange("p (h d) -> p h d", h=4)
        op("dve", "tensor_tensor", xdtv, ps[b][0:n, 0:256].rearrange("p (h d) -> p h d", h=4),
           SM[0:n, 0:4].unsqueeze(2).to_broadcast([n, 4, 64]), ALU.mult, R=[f"ps{b}", "SM"], W=["big0"])
        bend = xdt2[0:n, 256:512].rearrange("p (h d) -> p h d", h=4)
        for h in range(4):
            g = h // 2
            op("dve", "tensor_scalar", bend[:, h, :], ps[b][0:n, 256 + 64 * g:256 + 64 * g + 64], SM2[0:n, 16 + h:17 + h], None,
               ALU.mult, R=[f"ps{b}", "SM2"], W=["big0"])
        self.psf(b)
        WB = self.WB
        wb = [WB[:, 512 * i:512 * i + 512] for i in range(4)]
        wv = lambda t: t[0:n, :].rearrange("p (h t) -> p h t", h=4)[:, :, 0:n]
        wh = lambda t, h: t[0:n, 128 * h:128 * h + n]
        pv = lambda b_: ps[b_][0:n, :].rearrange("p (h t) -> p h t", h=4)[:, :, 0:n]
        bc4 = lambda ap: ap.unsqueeze(1).to_broadcast([n, 4, n])
        col4 = lambda ap: ap.unsqueeze(2).to_broadcast([n, 4, n])
        ONESn = self.cst[0:n, C_ONES, 0:n]
        IDn = self.cst[0:n, C_ID, 0:n]
        a4 = SM[0:n, 4:8]
        na4 = SM2[0:n, 36:40]
        op("dve", "tensor_scalar", na4, a4, -1.0, None, ALU.mult, R=["SM"], W=["SM2"])
        op("dve", "tensor_tensor", wv(wb[0]), bc4(ONESn), col4(a4), ALU.mult, R=["cst", "SM"], W=["wb0"])
        op("dve", "tensor_tensor", wv(wb[1]), bc4(self.cst[0:n, C_TRI(s), 0:n]), col4(na4), ALU.mult, R=["cst", "SM2"], W=["wb1"])
        bA, bB, bC = self.psa(), self.psa(), self.psa()
        for g in range(2):
            mm(ps[bA][0:n, 128 * g:128 * g + n], CV[64 * g:64 * g + 64, 2, 0:n], CV[64 * g:64 * g + 64, 3, 0:n],
               R=["CVa"], W=[f"ps{bA}"])
        for h in range(4):
            o = ps[bB][0:n, 128 * h:128 * h + n]
            mm(o, wh(wb[0], h), self.C(C_TRI(s), n, n), start=True, stop=False, R=["wb0", "cst"], W=[f"ps{bB}"])
            mm(o, wh(wb[1], h), ONESn, start=False, stop=False, R=["wb1", "cst"], W=[f"ps{bB}"])
            mm(o, IDn, self.C(C_NBI(s), n, n), start=False, stop=True, R=["cst"], W=[f"ps{bB}"])
        for h in range(4):
            base, slot = 64 * (h // 2), h % 2
            mm(ps[bC][base:base + 64, 128 * slot:128 * slot + n], wb[0][0:n, 128 * h:128 * h + 64], self.C(C_TRI(s), n, n),
               R=["wb0", "cst"], W=[f"ps{bC}"])
        act(wv(wb[2]), pv(bB), AF.Exp, R=[f"ps{bB}"], W=["wb2"])
        w22 = wb[2][0:n, :].rearrange("p (g j t) -> p g j t", g=2, j=2)[:, :, :, 0:n]
        sc2 = ps[bA][0:n, 0:256].rearrange("p (g t) -> p g t", g=2)[:, :, 0:n].unsqueeze(2).to_broadcast([n, 2, 2, n])
        op("dve", "tensor_tensor", w22, sc2, w22, ALU.mult, R=[f"ps{bA}", "wb2"], W=["wb2"])
        cgv = wb[3][:, 0:256].rearrange("p (j t) -> p j t", j=2)[:, :, 0:n]
        act(cgv, ps[bC][:, 0:256].rearrange("p (j t) -> p j t", j=2)[:, :, 0:n], AF.Exp, R=[f"ps{bC}"], W=["wb3"])
        op("pool", "tensor_tensor", cgv, cgv, CV[:, 3, 0:n].unsqueeze(1).to_broadcast([128, 2, n]), ALU.mult,
           R=["wb3", "CVa"], W=["wb3"])
        for b_ in (bA, bB, bC):
            self.psf(b_)
        Yb = self.psa()
        for h in range(4):
            g = h // 2
            base = 64 * g
            slot = h % 2
            yo = ps[Yb][64 * slot:64 * slot + 64, 128 * g:128 * g + n]
            mm(yo, xdtv[:, h, :], wh(wb[2], h), start=True, stop=False, R=["big0", "wb2"], W=[f"ps{Yb}"])
            for bi in range(nb):
                c0 = bi * lb
                mm(yo[:, c0:c0 + lb], self.state_ap("ssd", 0, h, bi, sample),
                   wb[3][base:base + 64, 128 * slot + c0:128 * slot + c0 + lb],
                   start=False, stop=(bi == nb - 1), R=["wb3", self.state_key(0, sample)], W=[f"ps{Yb}"])
            if nb > 1:
                self.multi_state_update("ssd", 0, h, n, s, nb, bend[:, h, :], xdtv[:, h, :],
                                        self.EB[base:base + 64, 0, h * nb:h * nb + nb], ["EB"], ["big0"], ["big0"],
                                        [(self.big[0][:, 512:1024], "big0m")])
            else:
                bu = self.psa()
                mm(ps[bu][base:base + 64, 0:64], bend[:, h, :], xdtv[:, h, :], R=["big0"], W=[f"ps{bu}"])
                stp = self.state_ap("ssd", 0, h, 0, sample)
                op("dve", "scalar_tensor_tensor", stp, stp, self.EB[base:base + 64, 0, h * nb:h * nb + 1],
                   ps[bu][base:base + 64, 0:64], ALU.mult, ALU.add,
                   R=[self.state_key(0, sample), "EB", f"ps{bu}"], W=[self.state_key(0, sample)])
                self.psf(bu)
        for hb in range(2):
            y = scr[13]
            op("dve", "scalar_tensor_tensor", y[:, 0:n], CV[:, hb, 0:n], self.cp[:, CP_DSSD + hb:CP_DSSD + hb + 1],
               ps[Yb][:, 128 * hb:128 * hb + n], ALU.mult, ALU.add, R=["CVa", "cp", f"ps{Yb}"], W=["scr13"])
            self.rms_gate(y[:, 0:n], "scr13", n, CP_GSSD + hb, FM_ZA + hb, 0 + hb)
        self.psf(Yb)
        if sample:
            self.store_sst(l, 0, 0)

    def multi_state_update(self, mixer, m, h, n, s, nb, src_tm, rhs_tm, decay, dkeys, skeys, rkeys, mtiles):
        op, mm, ps = self.op, self.mm, self.ps
        base, slot = self.state_geo(mixer, h)
        skey = self.state_key(m, True)
        S = self.SST[self.sstbuf[m]]
        for gi, g0 in enumerate(range(0, nb, 8)):
            mt, mkey = mtiles[gi % len(mtiles)]
            mk = mt[0:n, 0:512].rearrange("p (b d) -> p b d", b=8)
            op("dve", "tensor_tensor", mk, src_tm.unsqueeze(1).to_broadcast([n, 8, 64]),
               self.cst[0:n, C_BLK(s), g0:g0 + 8].unsqueeze(2).to_broadcast([n, 8, 64]), ALU.mult,
               R=list(skeys) + ["cst"], W=[mkey])
            bu = self.psa()
            for q in range(8):
                mm(ps[bu][base:base + 64, 64 * q:64 * q + 64], mt[0:n, 64 * q:64 * q + 64], rhs_tm,
                   R=[mkey] + list(rkeys), W=[f"ps{bu}"])
            Sv = S[base:base + 64, slot, g0:g0 + 8, :]
            op("dve", "tensor_tensor", Sv, Sv, decay[:, g0:g0 + 8].unsqueeze(2).to_broadcast([64, 8, 64]), ALU.mult,
               R=[skey] + list(dkeys), W=[skey])
            op("dve", "tensor_tensor", Sv, Sv, ps[bu][base:base + 64, :].rearrange("p (b d) -> p b d", b=8), ALU.add,
               R=[skey, f"ps{bu}"], W=[skey])
            self.psf(bu)

    def state_geo(self, mixer, h):
        if mixer == "ssd":
            return 64 * (h // 2), h % 2
        return 64 * (h % 2), h // 2

    def state_ap(self, mixer, m, h, bi, sample):
        base, slot = self.state_geo(mixer, h)
        if sample:
            return self.SST[self.sstbuf[m]][base:base + 64, slot, bi, :]
        return self.PST[m][base:base + 64, slot, :]

    def state_key(self, m, sample):
        if sample:
            return f"SST{self.sstbuf[m]}"
        return f"PST{m}"

    sstbuf = {0: 0, 1: 0, 2: 0}

    def load_sst(self, l, m, _):
        buf = self.sstbuf[m]
        mixer = ("ssd", "gdn", "hg")[m]
        for h in range(4):
            base, slot = self.state_geo(mixer, h)
            self.P.dma(self.SST[buf][base:base + 64, slot, :, :],
                       self.sstd[m].ap()[l, :, h, :, :].rearrange("s k v -> k s v"),
                       f"SST{buf}", writes=[f"SST{buf}"] + self.alias_keys)

    def store_sst(self, l, m, _):
        buf = self.sstbuf[m]
        self.P.dma(self.o_sst.ap()[l, m], self.SST[buf][:], f"SST{buf}", reads=[f"SST{buf}"] + self.alias_keys)

    def gdn_prep(self, kind, n):
        op, mm, act = self.op, self.mm, self.act
        ps, scr = self.ps, self.scr
        CV = self.CV
        QN, KN = [scr[14], scr[15]], [scr[16], scr[17]]
        for i, (cb, dst, dk_, sc) in enumerate(((4, QN[0], "scr14", 0.125), (5, QN[1], "scr15", 0.125),
                                                (6, KN[0], "scr16", 1.0), (7, KN[1], "scr17", 1.0))):
            sq = scr[0]
            act(sq[:, 0:n], CV[:, cb, 0:n], AF.Square, R=["CVb"], W=["scr0"])
            b = self.psa()
            mm(ps[b][:, 0:n], self.C(C_ONES64), sq[:, 0:n], R=["cst", "scr0"], W=[f"ps{b}"])
            rs = scr[1]
            act(rs[:, 0:n], ps[b][:, 0:n], AF.Ln, bias=self.epsc[:, 1:2], R=[f"ps{b}", "epsc"], W=["scr1"])
            self.psf(b)
            act(rs[:, 0:n], rs[:, 0:n], AF.Exp, scale=-0.5, R=["scr1"], W=["scr1"])
            op("dve", "scalar_tensor_tensor", dst[:, 0:n], CV[:, cb, 0:n], sc, rs[:, 0:n], ALU.mult, ALU.mult,
               R=["CVb", "scr1"], W=[dk_])

    def gdn_tile(self, l, kind, n):
        op, mm, act, tr = self.op, self.mm, self.act, self.tr
        ps, scr = self.ps, self.scr
        s, lb, nb, rounds = self.struct(kind, "gdn")
        nlev = {4: 2, 16: 4, 64: 6}[lb]
        sample = (kind == "sample")
        m = 1
        if sample:
            self.load_sst(l, m, 0)
        skey = self.state_key(m, sample)
        CV, SM, SM2 = self.CV, self.SM, self.SM2
        QN, KN = [scr[14], scr[15]], [scr[16], scr[17]]
        b = self.psa()
        for j in range(2):
            tr(ps[b][0:n, 128 * j:128 * j + 128], CV[:, 8 + j, 0:n], self.C(C_ID), R=["CVb", "cst"], W=[f"ps{b}"])
            tr(ps[b][0:n, 256 + 128 * j:256 + 128 * j + 128], KN[j][:, 0:n], self.C(C_ID), R=[f"scr{16 + j}", "cst"], W=[f"ps{b}"])
        big0 = self.big[0]
        X1 = big0[0:n, 0:256]
        X2 = big0[0:n, 256:512]
        KEB = big0[0:n, 512:768]
        op("act", "copy", X1, ps[b][0:n, 0:256], R=[f"ps{b}"], W=["big0"])
        ktm = ps[b][0:n, 256:512].rearrange("p (h d) -> p h d", h=4)
        op("dve", "tensor_tensor", X2.rearrange("p (h d) -> p h d", h=4), ktm,
           SM2[0:n, 24:28].unsqueeze(2).to_broadcast([n, 4, 64]), ALU.mult, R=[f"ps{b}", "SM2"], W=["big0"])
        op("dve", "tensor_tensor", KEB.rearrange("p (h d) -> p h d", h=4), ktm,
           SM2[0:n, 32:36].unsqueeze(2).to_broadcast([n, 4, 64]), ALU.mult, R=[f"ps{b}", "SM2"], W=["big0"])
        self.psf(b)
        WN = [scr[22], scr[23]]
        QG = [scr[26], scr[27]]
        WB = self.WB
        wb = [WB[:, 512 * i:512 * i + 512] for i in range(4)]
        wv = lambda t: t[0:n, :].rearrange("p (h t) -> p h t", h=4)[:, :, 0:n]
        wh = lambda t, h: t[0:n, 128 * h:128 * h + n]
        pv = lambda b: ps[b][0:n, :].rearrange("p (h t) -> p h t", h=4)[:, :, 0:n]
        bc4 = lambda ap: ap.unsqueeze(1).to_broadcast([n, 4, n])
        col4 = lambda ap: ap.unsqueeze(2).to_broadcast([n, 4, n])
        AQW, RW = self.AQW, self.RW
        frn = self.use_fr and n == 128
        fo = (lambda ap: ap.bitcast(mybir.dt.float32r)) if frn else (lambda ap: ap)
        g4 = SM[0:n, 12:16]
        ng4 = SM2[0:n, 36:40]
        op("dve", "tensor_scalar", ng4, g4, -1.0, None, ALU.mult, R=["SM"], W=["SM2"])
        op("dve", "tensor_tensor", wv(wb[0]), bc4(self.cst[0:n, C_ONES, 0:n]), col4(g4), ALU.mult, R=["cst", "SM"], W=["wb0"])
        op("dve", "tensor_tensor", wv(wb[1]), bc4(self.cst[0:n, C_TRI(s), 0:n]), col4(ng4), ALU.mult, R=["cst", "SM2"], W=["wb1"])
        bA, bB, bC, bD = self.psa(), self.psa(), self.psa(), self.psa()
        ONESn = self.cst[0:n, C_ONES, 0:n]
        IDn = self.cst[0:n, C_ID, 0:n]
        for h in (0, 2, 1, 3):
            base, hb = 64 * (h % 2), h // 2
            kn = KN[hb][base:base + 64, 0:n]
            qn = QN[hb][base:base + 64, 0:n]
            mm(ps[bA][0:n, 128 * h:128 * h + n], kn, kn, R=[f"scr{16 + hb}"], W=[f"ps{bA}"])
            mm(ps[bB][0:n, 128 * h:128 * h + n], kn, qn, R=[f"scr{16 + hb}", f"scr{14 + hb}"], W=[f"ps{bB}"])
        def rmask(bank, mask):
            for h in range(4):
                o = ps[bank][0:n, 128 * h:128 * h + n]
                mm(o, wh(wb[0], h), self.C(C_TRI(s), n, n), start=True, stop=False, R=["wb0", "cst"], W=[f"ps{bank}"])
                mm(o, wh(wb[1], h), ONESn, start=False, stop=False, R=["wb1", "cst"], W=[f"ps{bank}"])
                mm(o, IDn, self.C(mask, n, n), start=False, stop=True, R=["cst"], W=[f"ps{bank}"])
        rmask(bC, C_NBS(s))
        for h in range(4):
            base, hb = 64 * (h % 2), h // 2
            mm(ps[bD][base:base + 64, 128 * hb:128 * hb + n], wb[0][0:n, 128 * h:128 * h + 64], self.C(C_TRI(s), n, n),
               R=["wb0", "cst"], W=[f"ps{bD}"])
        act(wv(wb[3]), pv(bC), AF.Exp, R=[f"ps{bC}"], W=["wb3"])
        for hb in range(2):
            act(QG[hb][:, 0:n], ps[bD][:, 128 * hb:128 * hb + n], AF.Exp, R=[f"ps{bD}"], W=[f"scr{26 + hb}"])
            op("pool", "tensor_tensor", QG[hb][:, 0:n], QG[hb][:, 0:n], QN[hb][:, 0:n], ALU.mult,
               R=[f"scr{26 + hb}", f"scr{14 + hb}"], W=[f"scr{26 + hb}"])
        op("pool", "tensor_tensor", wv(wb[2]), wv(wb[3]), bc4(IDn), ALU.add, R=["wb3", "cst"], W=["wb2"])
        op("dve", "tensor_tensor", wv(wb[0]), pv(bA), col4(SM2[0:n, 28:32]), ALU.mult, R=[f"ps{bA}", "SM2"], W=["wb0"])
        op("dve", "tensor_tensor", fo(wv(wb[0])), wv(wb[0]), wv(wb[3]), ALU.mult, R=["wb0", "wb3"], W=["wb0"])
        op("dve", "tensor_tensor", wv(AQW), pv(bB), col4(SM[0:n, 8:12]), ALU.mult, R=[f"ps{bB}", "SM"], W=["AQW"])
        op("pool", "tensor_tensor", wv(AQW), wv(AQW), wv(wb[2]), ALU.mult, R=["AQW", "wb2"], W=["AQW"])
        for h in range(4):
            tr(ps[bD][0:n, 128 * h:128 * h + n], wh(wb[0], h), IDn, R=["wb0", "cst"], W=[f"ps{bD}"])
        NH, RWh = self.NH, self.RWh
        op("act", "copy", wv(NH[1]), pv(bD), R=[f"ps{bD}"], W=["NH1"])
        op("pool", "tensor_copy", wv(NH[0]), wv(wb[0]), R=["wb0"], W=["NH0"])
        op("dve", "tensor_tensor", wv(RW), bc4(IDn), wv(wb[0]), ALU.add, R=["cst", "wb0"], W=["RW"])
        op("pool", "tensor_copy", wv(RWh), wv(RW), R=["RW"], W=["RWh"])
        cur = (0, 1)
        for k in range(1, nlev):
            nxt = (2, 3) if cur == (0, 1) else (0, 1)
            NTc, NNc, NTn, NNn = NH[cur[0]], NH[cur[1]], NH[nxt[0]], NH[nxt[1]]
            kc = [f"NH{cur[0]}", f"NH{cur[1]}"]
            for h in range(4):
                mm(ps[bA][0:n, 128 * h:128 * h + n], wh(NTc, h), wh(NNc, h), R=kc, W=[f"ps{bA}"])
            if k < nlev - 1:
                for h in range(4):
                    mm(ps[bB][0:n, 128 * h:128 * h + n], wh(NNc, h), wh(NTc, h), R=kc, W=[f"ps{bB}"])
            op("act", "copy", wv(NNn), pv(bA), R=[f"ps{bA}"], W=[f"NH{nxt[1]}"])
            if k < nlev - 1:
                op("dve", "tensor_copy", wv(NTn), pv(bB), R=[f"ps{bB}"], W=[f"NH{nxt[0]}"])
            for h in range(4):
                mm(ps[bC][0:n, 128 * h:128 * h + n], wh(NNn, h), wh(RWh, h), R=[f"NH{nxt[1]}", "RWh"], W=[f"ps{bC}"])
            op("dve", "tensor_tensor", wv(RW), wv(RW), pv(bC), ALU.add, R=["RW", f"ps{bC}"], W=["RW"])
            if k < nlev - 1:
                op("pool", "tensor_copy", wv(RWh), wv(RW), R=["RW"], W=["RWh"])
            cur = nxt
        for h in range(4):
            base, hb = 64 * (h % 2), h // 2
            mm(ps[bD][base:base + 64, 128 * hb:128 * hb + n], X2[:, 64 * h:64 * h + 64], wh(RW, h), R=["big0", "RW"], W=[f"ps{bD}"])
        for hb in range(2):
            act(WN[hb][:, 0:n], ps[bD][:, 128 * hb:128 * hb + n], AF.Copy, scale=-1.0, R=[f"ps{bD}"], W=[f"scr{22 + hb}"])
        for b_ in (bA, bB, bC, bD):
            self.psf(b_)
        Vb, Ob = self.psa(), self.psa()
        for rnd in rounds:
            r0, r1 = rnd[0] * lb, (rnd[-1] + 1) * lb
            if len(rnd) == 1:
                bi = rnd[0]
                VT4 = self.VT4
                for h in range(4):
                    base, hb = 64 * (h % 2), h // 2
                    vo = ps[Vb][r0:r1, 64 * h:64 * h + 64]
                    mm(vo, self.RW[r0:r1, 128 * h + r0:128 * h + r1], big0[r0:r1, 64 * h:64 * h + 64], start=True, stop=False,
                       R=["RW", "big0"], W=[f"ps{Vb}"])
                    mm(vo, WN[hb][base:base + 64, r0:r1], self.state_ap("gdn", m, h, bi, sample), start=False, stop=True,
                       R=[f"scr{22 + hb}", skey], W=[f"ps{Vb}"])
                op("act", "copy", VT4[r0:r1, :], ps[Vb][r0:r1, 0:256], R=[f"ps{Vb}"], W=["VT4"])
                bu = self.psa()
                for h in range(4):
                    base, hb = 64 * (h % 2), h // 2
                    oo = ps[Ob][base:base + 64, 128 * hb + r0:128 * hb + r1]
                    mm(oo, VT4[r0:r1, 64 * h:64 * h + 64], self.AQW[r0:r1, 128 * h + r0:128 * h + r1], start=True, stop=False,
                       R=["VT4", "AQW"], W=[f"ps{Ob}"])
                    mm(oo, self.state_ap("gdn", m, h, bi, sample), QG[hb][base:base + 64, r0:r1], start=False, stop=True,
                       R=[skey, f"scr{26 + hb}"], W=[f"ps{Ob}"])
                for h in range(4):
                    base, hb = 64 * (h % 2), h // 2
                    mm(ps[bu][base:base + 64, 64 * hb:64 * hb + 64], big0[r0:r1, 512 + 64 * h:512 + 64 * h + 64],
                       VT4[r0:r1, 64 * h:64 * h + 64], R=["big0", "VT4"], W=[f"ps{bu}"])
                for h in range(4):
                    base, hb = 64 * (h % 2), h // 2
                    stp = self.state_ap("gdn", m, h, bi, sample)
                    op("dve", "scalar_tensor_tensor", stp, stp, self.EB[base:base + 64, 1, h * nb + bi:h * nb + bi + 1],
                       ps[bu][base:base + 64, 64 * hb:64 * hb + 64], ALU.mult, ALU.add, R=[skey, "EB", f"ps{bu}"], W=[skey])
                self.psf(bu)
                continue
            for h in range(4):
                base, hb = 64 * (h % 2), h // 2
                Rm, rk = self.RW[:, 128 * h:128 * h + 128], "RW"
                vv = ps[Vb][base:base + 64, 128 * hb:128 * hb + n]
                mm(vv[:, r0:r1], big0[r0:r1, 64 * h:64 * h + 64], Rm[r0:r1, r0:r1], start=True, stop=False,
                   R=["big0", rk], W=[f"ps{Vb}"])
                for bi in rnd:
                    c0_ = bi * lb
                    mm(vv[:, c0_:c0_ + lb], self.state_ap("gdn", m, h, bi, sample), WN[hb][base:base + 64, c0_:c0_ + lb],
                       start=False, stop=(bi == rnd[-1]), R=[skey, f"scr{22 + hb}"], W=[f"ps{Vb}"])
                VR = scr[35]
                op("act", "copy", VR[base:base + 64, r0:r1], vv[:, r0:r1], R=[f"ps{Vb}"], W=["scr35"])
                bT = self.psa()
                mm(ps[bT][r0:r1, 0:64], VR[base:base + 64, r0:r1], self.cst[base:base + 64, C_ID, base:base + 64],
                   R=["scr35", "cst"], W=[f"ps{bT}"])
                VT = scr[36]
                op("dve", "tensor_copy", VT[r0:r1, 0:64], ps[bT][r0:r1, 0:64], R=[f"ps{bT}"], W=["scr36"])
                self.psf(bT)
                oo = ps[Ob][base:base + 64, 128 * hb:128 * hb + n]
                mm(oo[:, r0:r1], VT[r0:r1, 0:64], self.AQW[r0:r1, 128 * h + r0:128 * h + r1], start=True, stop=False, R=["scr36", "AQW"],
                   W=[f"ps{Ob}"])
                for bi in rnd:
                    c0_ = bi * lb
                    mm(oo[:, c0_:c0_ + lb], self.state_ap("gdn", m, h, bi, sample), QG[hb][base:base + 64, c0_:c0_ + lb],
                       start=False, stop=(bi == rnd[-1]), R=[skey, f"scr{26 + hb}"], W=[f"ps{Ob}"])
                self.multi_state_update("gdn", m, h, n, s, nb, big0[0:n, 512 + 64 * h:512 + 64 * h + 64], VT[0:n, 0:64],
                                        self.EB[base:base + 64, 1, h * nb:h * nb + nb], ["EB"], ["big0"], ["scr36"],
                                        [(self.WB[:, 1024:1536], "wb2"), (self.WB[:, 1536:2048], "wb3")])
        self.psf(Vb)
        for hb in range(2):
            self.rms_gate(ps[Ob][:, 128 * hb:128 * hb + n], f"ps{Ob}", n, CP_GGDN + hb, FM_ZC + hb, 4 + hb)
        self.psf(Ob)
        if sample:
            self.store_sst(l, m, 0)

    def hg_tile(self, l, kind, n):
        op, mm, act, tr = self.op, self.mm, self.act, self.tr
        ps, scr = self.ps, self.scr
        s, lb, nb, rounds = self.struct(kind, "hg")
        sample = (kind == "sample")
        m = 2
        if sample:
            self.load_sst(l, m, 0)
        skey = self.state_key(m, sample)
        LF, KK, GC, QG, KG, KE, TP = ([scr[a], scr[a + 1]] for a in (14, 16, 18, 20, 22, 24, 26))
        for hb in range(2):
            c0, c1, c2 = (self.hgc[:, hb, j:j + 1] for j in range(3))
            th = self.TH[:, hb, 0:n]
            act(LF[hb][:, 0:n], th, AF.Ln, scale=c0, bias=c1, R=["TH", "hgc"], W=[f"scr{14 + hb}"])
            op("dve", "tensor_scalar", KK[hb][:, 0:n], th, c2, c0, ALU.mult, ALU.add, R=["TH", "hgc"], W=[f"scr{16 + hb}"])
            op("dve", "tensor_tensor_scan", GC[hb][:, 0:n], self.cst[:, C_RST(s), 0:n], LF[hb][:, 0:n], 0.0, ALU.mult, ALU.add,
               R=["cst", f"scr{14 + hb}"], W=[f"scr{18 + hb}"])
            act(TP[hb][:, 0:n], GC[hb][:, 0:n], AF.Exp, R=[f"scr{18 + hb}"], W=[f"scr{26 + hb}"])
            op("dve", "tensor_tensor", QG[hb][:, 0:n], self.QD[:, hb, 0:n], TP[hb][:, 0:n], ALU.mult,
               R=["QD", f"scr{26 + hb}"], W=[f"scr{20 + hb}"])
            gcv = GC[hb][:, 0:n].rearrange("p (b t) -> p b t", t=lb)
            tpv = TP[hb][:, 0:n].rearrange("p (b t) -> p b t", t=lb)
            op("dve", "tensor_copy", LF[hb][:, 0:nb].unsqueeze(2), tpv[:, :, lb - 1:lb], R=[f"scr{26 + hb}"], W=[f"scr{14 + hb}"])
            act(TP[hb][:, 0:n], GC[hb][:, 0:n], AF.Exp, scale=-1.0, R=[f"scr{18 + hb}"], W=[f"scr{26 + hb}"])
            op("dve", "tensor_tensor", KG[hb][:, 0:n], KK[hb][:, 0:n], TP[hb][:, 0:n], ALU.mult,
               R=[f"scr{16 + hb}", f"scr{26 + hb}"], W=[f"scr{22 + hb}"])
            op("dve", "tensor_tensor", tpv, gcv[:, :, lb - 1:lb].to_broadcast([128, nb, lb]), gcv, ALU.subtract,
               R=[f"scr{18 + hb}"], W=[f"scr{26 + hb}"])
            act(TP[hb][:, 0:n], TP[hb][:, 0:n], AF.Exp, R=[f"scr{26 + hb}"], W=[f"scr{26 + hb}"])
            op("dve", "tensor_tensor", KE[hb][:, 0:n], KK[hb][:, 0:n], TP[hb][:, 0:n], ALU.mult,
               R=[f"scr{16 + hb}", f"scr{26 + hb}"], W=[f"scr{24 + hb}"])
        b = self.psa()
        for hb in range(2):
            tr(ps[b][0:n, 128 * hb:128 * hb + 128], KE[hb][:, 0:n], self.C(C_ID), R=[f"scr{24 + hb}", "cst"], W=[f"ps{b}"])
        KT = self.big[0][0:n, 512:768]
        op("act", "copy", KT, ps[b][0:n, 0:256], R=[f"ps{b}"], W=["big0"])
        self.psf(b)
        b = self.psa()
        for h in range(4):
            base, hb = 64 * (h % 2), h // 2
            mm(ps[b][0:n, 128 * h:128 * h + n], KG[hb][base:base + 64, 0:n], QG[hb][base:base + 64, 0:n],
               R=[f"scr{22 + hb}", f"scr{20 + hb}"], W=[f"ps{b}"])
        AT = self.big[0][0:n, 0:512].rearrange("p (h t) -> p h t", h=4)
        op("dve", "tensor_tensor", AT[:, :, 0:n], ps[b][0:n, :].rearrange("p (h t) -> p h t", h=4)[:, :, 0:n],
           self.cst[0:n, C_TRI(s), 0:n].unsqueeze(1).to_broadcast([n, 4, n]), ALU.mult, R=[f"ps{b}", "cst"], W=["big0"])
        self.psf(b)
        ob = [self.psa(), self.psa()]
        for rnd in rounds:
            r0, r1 = rnd[0] * lb, (rnd[-1] + 1) * lb
            if len(rnd) == 1:
                bi = rnd[0]
                for h in range(4):
                    base, hb = 64 * (h % 2), h // 2
                    vt = self.TMs[r0:r1, 12 + 64 * h:12 + 64 * h + 64]
                    oo = ps[ob[hb]][base:base + 64, r0:r1]
                    mm(oo, vt, self.big[0][r0:r1, 128 * h + r0:128 * h + r1], start=True, stop=False,
                       R=["TMs", "big0"], W=[f"ps{ob[hb]}"])
                    mm(oo, self.state_ap("hg", m, h, bi, sample), QG[hb][base:base + 64, r0:r1], start=False, stop=True,
                       R=[skey, f"scr{20 + hb}"], W=[f"ps{ob[hb]}"])
                bu = self.psa()
                for h in range(4):
                    base, hb = 64 * (h % 2), h // 2
                    vt = self.TMs[r0:r1, 12 + 64 * h:12 + 64 * h + 64]
                    mm(ps[bu][base:base + 64, 64 * hb:64 * hb + 64], self.big[0][r0:r1, 512 + 64 * h:512 + 64 * h + 64], vt,
                       R=["big0", "TMs"], W=[f"ps{bu}"])
                for h in range(4):
                    base, hb = 64 * (h % 2), h // 2
                    stp = self.state_ap("hg", m, h, bi, sample)
                    op("dve", "scalar_tensor_tensor", stp, stp, LF[hb][base:base + 64, bi:bi + 1],
                       ps[bu][base:base + 64, 64 * hb:64 * hb + 64], ALU.mult, ALU.add,
                       R=[skey, f"scr{14 + hb}", f"ps{bu}"], W=[skey])
                self.psf(bu)
                continue
            for h in range(4):
                base, hb = 64 * (h % 2), h // 2
                vt = self.TMs[0:n, 12 + 64 * h:12 + 64 * h + 64]
                oo = ps[ob[hb]][base:base + 64, 0:n]
                mm(oo[:, r0:r1], vt, AT[:, h, r0:r1], start=True, stop=False, R=["TMs", "big0"], W=[f"ps{ob[hb]}"])
                for bi in rnd:
                    c0_ = bi * lb
                    mm(oo[:, c0_:c0_ + lb], self.state_ap("hg", m, h, bi, sample), QG[hb][base:base + 64, c0_:c0_ + lb],
                       start=False, stop=(bi == rnd[-1]), R=[skey, f"scr{20 + hb}"], W=[f"ps{ob[hb]}"])
                self.multi_state_update("hg", m, h, n, s, nb, KT[:, 64 * h:64 * h + 64], vt,
                                        LF[hb][base:base + 64, 0:nb], [f"scr{14 + hb}"], ["big0"], ["TMs"],
                                        [(self.WB[:, 1024:1536], "wb2"), (self.WB[:, 1536:2048], "wb3")])
        for hb in range(2):
            self.rms_gate(ps[ob[hb]][:, 0:n], f"ps{ob[hb]}", n, CP_GHG + hb, FM_ZD + hb, 6 + hb)
            self.psf(ob[hb])
        if sample:
            self.store_sst(l, m, 0)

    def _poly(self, out, x, coefs, key):
        op = self.op
        op("dve", "memset", out, coefs[-1], W=[key])
        for c in reversed(coefs[:-1]):
            op("dve", "tensor_tensor", out, out, x, ALU.mult, R=[key], W=[key])
            op("dve", "tensor_scalar", out, out, float(c), None, ALU.add, R=[key], W=[key])

    def _cmul_tab(self, dre, dim, sre, sim, pr, pi, m, key, eng="dve", tmp=None, tkey="big0", pkey="s5t"):
        op = self.op
        tmp = self.big[0][:, 0:512] if tmp is None else tmp
        t = tmp[:, 0:8 * m].rearrange("p (g t) -> p g t", g=8)
        prb = pr.unsqueeze(2).to_broadcast([128, 8, m])
        pib = pi.unsqueeze(2).to_broadcast([128, 8, m])
        R = [key, pkey]
        op(eng, "tensor_tensor", t, sim, pib, ALU.mult, R=R, W=[tkey])
        op(eng, "tensor_tensor", dre, sre, prb, ALU.mult, R=R, W=[key])
        op(eng, "tensor_tensor", dre, dre, t, ALU.subtract, R=[key, tkey], W=[key])
        op(eng, "tensor_tensor", t, sim, prb, ALU.mult, R=R, W=[tkey])
        op(eng, "tensor_tensor", dim, sre, pib, ALU.mult, R=R, W=[key])
        op(eng, "tensor_tensor", dim, dim, t, ALU.add, R=[key, tkey], W=[key])

    def _pow_table(self, tab, key, bre, bim, count, ii=2, neg=True, eng="dve", T=None, r0=8, pkey="s5t", tmp=None, tkey="big0"):
        op = self.op
        T = self.s5t if T is None else T
        op(eng, "memset", tab[:, :, 0, 0:1], 1.0, W=[key])
        op(eng, "memset", tab[:, :, ii, 0:1], 0.0, W=[key])
        pr, pi = T[:, r0, :], T[:, r0 + 1, :]
        op(eng, "tensor_copy", pr, bre, R=["s5t"], W=[pkey])
        op(eng, "tensor_copy", pi, bim, R=["s5t"], W=[pkey])
        m = 1
        while m < count:
            w = min(m, count - m)
            self._cmul_tab(tab[:, :, 0, m:m + w], tab[:, :, ii, m:m + w], tab[:, :, 0, 0:w], tab[:, :, ii, 0:w], pr, pi, w, key, eng=eng, tmp=tmp, tkey=tkey, pkey=pkey)
            m *= 2
            if m < count:
                a, b2 = T[:, r0 + 2, :], T[:, r0 + 3, :]
                op(eng, "tensor_tensor", a, pr, pr, ALU.mult, R=[pkey], W=[pkey])
                op(eng, "tensor_tensor", b2, pi, pi, ALU.mult, R=[pkey], W=[pkey])
                op(eng, "tensor_tensor", a, a, b2, ALU.subtract, R=[pkey], W=[pkey])
                op(eng, "tensor_tensor", b2, pr, pi, ALU.mult, R=[pkey], W=[pkey])
                op(eng, "tensor_scalar", pi, b2, 2.0, None, ALU.mult, R=[pkey], W=[pkey])
                op(eng, "tensor_copy", pr, a, R=[pkey], W=[pkey])
        if neg:
            op(eng, "tensor_scalar", tab[:, :, 1, :], tab[:, :, 2, :], -1.0, None, ALU.mult, R=[key], W=[key])

    def s5_prepare(self, l):
        op = self.op
        T = self.s5t
        fact = lambda k: float(math.factorial(k))
        lre, lim, ldt = self.s5p[:, 0, :], self.s5p[:, 1, :], self.s5p[:, 2, :]
        K = "s5t"
        R = ["s5p", K]
        x = T[:, 0, :]
        op("dve", "tensor_scalar", x, ldt, 1.0 / 16.0, None, ALU.mult, R=R, W=[K])
        dt = T[:, 1, :]
        self._poly(dt, x, [1.0 / fact(k) for k in range(11)], K)
        for _ in range(4):
            op("dve", "tensor_tensor", dt, dt, dt, ALU.mult, R=[K], W=[K])
        op("dve", "tensor_tensor", x, lre, dt, ALU.mult, R=R, W=[K])
        mag = T[:, 2, :]
        self._poly(mag, x, [1.0 / fact(k) for k in range(7)], K)
        ang = T[:, 3, :]
        op("dve", "tensor_tensor", ang, lim, dt, ALU.mult, R=R, W=[K])
        kf = T[:, 4, :]
        op("dve", "tensor_scalar", kf, ang, 1.0 / (2.0 * math.pi), None, ALU.mult, R=[K], W=[K])
        op("dve", "tensor_copy", self.s5i[:], kf, R=[K], W=["s5i"])
        op("dve", "tensor_copy", kf, self.s5i[:], R=["s5i"], W=[K])
        C1 = 6.28125
        C2 = 2.0 * math.pi - C1
        op("dve", "scalar_tensor_tensor", ang, kf, -C1, ang, ALU.mult, ALU.add, R=[K], W=[K])
        op("dve", "scalar_tensor_tensor", ang, kf, -C2, ang, ALU.mult, ALU.add, R=[K], W=[K])
        op("dve", "tensor_scalar", ang, ang, 0.125, None, ALU.mult, R=[K], W=[K])
        y = T[:, 4, :]
        op("dve", "tensor_tensor", y, ang, ang, ALU.mult, R=[K], W=[K])
        sn, cs = T[:, 5, :], T[:, 6, :]
        self._poly(sn, y, [(-1.0) ** k / fact(2 * k + 1) for k in range(6)], K)
        op("dve", "tensor_tensor", sn, sn, ang, ALU.mult, R=[K], W=[K])
        self._poly(cs, y, [(-1.0) ** k / fact(2 * k) for k in range(7)], K)
        tmp = T[:, 7, :]
        for _ in range(3):
            op("dve", "tensor_tensor", tmp, sn, sn, ALU.mult, R=[K], W=[K])
            op("dve", "tensor_tensor", sn, sn, cs, ALU.mult, R=[K], W=[K])
            op("dve", "tensor_scalar", sn, sn, 2.0, None, ALU.mult, R=[K], W=[K])
            op("dve", "tensor_scalar", cs, tmp, -2.0, 1.0, ALU.mult, ALU.add, R=[K], W=[K])
        op("dve", "tensor_tensor", sn, sn, mag, ALU.mult, R=[K], W=[K])
        op("dve", "tensor_tensor", cs, cs, mag, ALU.mult, R=[K], W=[K])
        lbr, lbi = T[:, 6, :], T[:, 5, :]
        m2 = T[:, 0, :]
        a = T[:, 1, :]
        op("dve", "tensor_tensor", m2, lbr, lbr, ALU.mult, R=[K], W=[K])
        op("dve", "tensor_tensor", a, lbi, lbi, ALU.mult, R=[K], W=[K])
        op("dve", "tensor_tensor", m2, m2, a, ALU.add, R=[K], W=[K])
        op("dve", "reciprocal", m2, m2, R=[K], W=[K])
        ivr, ivi = T[:, 2, :], T[:, 3, :]
        op("dve", "tensor_tensor", ivr, lbr, m2, ALU.mult, R=[K], W=[K])
        op("dve", "tensor_tensor", ivi, lbi, m2, ALU.mult, R=[K], W=[K])
        op("dve", "tensor_scalar", ivi, ivi, -1.0, None, ALU.mult, R=[K], W=[K])
        den, b2 = T[:, 0, :], T[:, 1, :]
        nr = T[:, 4, :]
        op("dve", "tensor_scalar", nr, lbr, -1.0, None, ALU.add, R=[K], W=[K])
        cre, cim = T[:, 7, :], T[:, 4, :]
        t1, t2 = self.scr[36][:, 0:8], self.scr[36][:, 8:16]
        SK = "scr36"
        op("dve", "tensor_tensor", t1, nr, lre, ALU.mult, R=R, W=[SK])
        op("dve", "tensor_tensor", t2, lbi, lim, ALU.mult, R=R, W=[SK])
        op("dve", "tensor_tensor", cre, t1, t2, ALU.add, R=[SK], W=[K])
        op("dve", "tensor_tensor", t1, lbi, lre, ALU.mult, R=R, W=[SK])
        op("dve", "tensor_tensor", t2, nr, lim, ALU.mult, R=R, W=[SK])
        op("dve", "tensor_tensor", cim, t1, t2, ALU.subtract, R=[SK], W=[K])
        op("dve", "tensor_tensor", den, lre, lre, ALU.mult, R=R, W=[K])
        op("dve", "tensor_tensor", b2, lim, lim, ALU.mult, R=R, W=[K])
        op("dve", "tensor_tensor", den, den, b2, ALU.add, R=[K], W=[K])
        op("dve", "reciprocal", den, den, R=[K], W=[K])
        op("dve", "tensor_tensor", cre, cre, den, ALU.mult, R=[K], W=[K])
        op("dve", "tensor_tensor", cim, cim, den, ALU.mult, R=[K], W=[K])
        KI2 = self.big[1][:, 0:1024].rearrange("p (g c t) -> p g c t", g=8, c=2)
        self._pow_table(KI2, "wb2", ivr, ivi, 64, ii=1, neg=False, eng="pool", T=self.s5u, r0=0, pkey="GM",
                        tmp=self.big[0][:, 512:1024], tkey="big0m")
        self._pow_table(self.KO, "KO", lbr, lbi, 65)
        op("dve", "tensor_copy", self.KI[:, :, 0, :], KI2[:, :, 0, :], R=["wb2", "wb3"], W=["KI"])
        op("dve", "tensor_copy", self.KI[:, :, 2, :], KI2[:, :, 1, :], R=["wb2", "wb3"], W=["KI"])
        self._cmul_tab(KI2[:, :, 0, :], KI2[:, :, 1, :], self.KI[:, :, 0, :], self.KI[:, :, 2, :], cre, cim, 64, "KI")
        op("dve", "tensor_copy", self.KI[:, :, 0, :], KI2[:, :, 0, :], R=["wb2", "wb3", "KI"], W=["KI"])
        op("dve", "tensor_copy", self.KI[:, :, 2, :], KI2[:, :, 1, :], R=["wb2", "wb3", "KI"], W=["KI"])
        op("dve", "tensor_scalar", self.KI[:, :, 1, :], self.KI[:, :, 2, :], -1.0, None, ALU.mult, R=["KI"], W=["KI"])
        op("dve", "tensor_scalar", self.Cl[:, :, 1, :], self.Cl[:, :, 1, :], -1.0, None, ALU.mult, R=["Cl"], W=["Cl"])
        op("dve", "tensor_scalar", self.hglb[:], self.cp[:, CP_GLUB:CP_GLUB + 2], 0.5, None, ALU.mult, R=["cp"], W=["hglb"])

    def s5_tile(self, l, kind, n):
        op, mm, act = self.op, self.mm, self.act
        ps = self.ps
        sample = (kind == "sample")
        if sample:
            subs = [(0, 64)]
            rst = self.cst[:, C_RST(2), 0:64]
        elif kind == "meta":
            subs = [(0, 16)]
            rst = self.cst[:, C_RST(0), 0:16]
        else:
            subs = [(0, 64), (64, 64)]
            rst = self.cst[:, C_RST(1), 0:64]
        Y5 = self.s5w[7]
        KI, KO = self.KI, self.KO
        ybk = self.psa()
        rec = self.P.rec
        ways = 2 if sample else 4
        sets = [(self.s5w[0:7], "s5w"), (self.s5x, "s5x"), (self.s5y, "s5y"), (self.s5z, "s5z")]
        marks = []
        for (c0, w) in subs:
            for gp in range(8):
                if gp % ways == 0:
                    marks = []
                    pbanks = [self.psa(), self.psa()]
                marks.append(len(rec))
                (T1, T2, G, GA, T3, T4, H), kx = sets[gp % ways]
                blk, pb = gp // 4, 64 * ((gp % 4) // 2)
                bP = pbanks[(gp % ways) // 2]
                pc = 128 * (gp % 2)
                for c in range(2):
                    mm(ps[bP][:, pc + 64 * c:pc + 64 * c + w], self.Bl[pb:pb + 64, gp, c, :], self.UB[pb:pb + 64, blk, c0:c0 + w],
                       R=["Bl", "UB"], W=[f"ps{bP}"])
                Pv = ps[bP][:, pc:pc + 128].rearrange("p (c t) -> p c t", c=2)[:, :, 0:w]
                if sample:
                    tb = lambda a, b_: KI[:, gp, a:b_, 0:TS].unsqueeze(2).to_broadcast([128, b_ - a, NSEQ, TS])
                    v4 = lambda ap: ap.rearrange("p c (s t) -> p c s t", t=TS)
                    v3 = lambda ap: ap.rearrange("p (s t) -> p s t", t=TS)
                    ko = lambda a, b_: KO[:, gp, a:b_, 0:TS].unsqueeze(2).to_broadcast([128, b_ - a, NSEQ, TS])
                    kis = lambda a: KI[:, gp, a, 0:TS].unsqueeze(1).to_broadcast([128, NSEQ, TS])
                    kos = lambda a: KO[:, gp, a, 0:TS].unsqueeze(1).to_broadcast([128, NSEQ, TS])
                else:
                    tb = lambda a, b_: KI[:, gp, a:b_, 0:w]
                    v4 = lambda ap: ap
                    v3 = lambda ap: ap
                    ko = lambda a, b_: KO[:, gp, a:b_, 0:w]
                    kis = lambda a: KI[:, gp, a, 0:w]
                    kos = lambda a: KO[:, gp, a, 0:w]
                R = [f"ps{bP}", "KI"]
                op("dve", "tensor_tensor", v4(T1[:, :, 0:w]), v4(Pv), tb(0, 1).to_broadcast([128, 2] + ([NSEQ, TS] if sample else [w])),
                   ALU.mult, R=R, W=[kx + "0"])
                op("dve", "tensor_tensor", v3(T2[:, 0, 0:w]), v3(Pv[:, 1, :]), kis(1), ALU.mult, R=R, W=[kx + "1"])
                op("dve", "tensor_tensor", v3(T2[:, 1, 0:w]), v3(Pv[:, 0, :]), kis(2), ALU.mult, R=R, W=[kx + "1"])
                op("pool", "tensor_tensor", G[:, :, 0:w], T1[:, :, 0:w], T2[:, :, 0:w], ALU.add, R=[kx + "0", kx + "1"], W=[kx + "2"])
                lr, li, nli = KO[:, gp, 0, 1:2], KO[:, gp, 2, 1:2], KO[:, gp, 1, 1:2]
                if sample:
                    hre, him = self.S5S[:, gp, 0, :], self.S5S[:, gp, 1, :]
                    g0 = lambda c: G[:, c, 0:w].rearrange("p (s t) -> p s t", t=TS)[:, :, 0]
                    hk = "S5S"
                else:
                    hre, him = self.S5C[:, gp, 0:1], self.S5C[:, gp, 1:2]
                    g0 = lambda c: G[:, c, 0:1]
                    hk = "S5C"
                for (c, ha, sa, hb_, sb_) in ((0, hre, lr, him, nli), (1, hre, li, him, lr)):
                    op("dve", "scalar_tensor_tensor", g0(c), ha, sa, g0(c), ALU.mult, ALU.add, R=[hk, "KO", kx + "2"], W=[kx + "2"])
                    op("dve", "scalar_tensor_tensor", g0(c), hb_, sb_, g0(c), ALU.mult, ALU.add, R=[hk, "KO", kx + "2"], W=[kx + "2"])
                for c in range(2):
                    op("dve", "tensor_tensor_scan", GA[:, c, 0:w], rst[:, 0:w], G[:, c, 0:w], 0.0, ALU.mult, ALU.add,
                       R=["cst", kx + "2"], W=[kx + "3"])
                R = [kx + "3", "KO"]
                op("pool", "tensor_tensor", v4(T3[:, :, 0:w]), v4(GA[:, :, 0:w]),
                   ko(0, 1).to_broadcast([128, 2] + ([NSEQ, TS] if sample else [w])), ALU.mult, R=R, W=[kx + "4"])
                op("pool", "tensor_tensor", v3(T4[:, 0, 0:w]), v3(GA[:, 1, 0:w]), kos(1), ALU.mult, R=R, W=[kx + "5"])
                op("pool", "tensor_tensor", v3(T4[:, 1, 0:w]), v3(GA[:, 0, 0:w]), kos(2), ALU.mult, R=R, W=[kx + "5"])
                op("pool", "tensor_tensor", H[:, :, 0:w], T3[:, :, 0:w], T4[:, :, 0:w], ALU.add, R=[kx + "4", kx + "5"], W=[kx + "6"])
                yo = ps[ybk][pb:pb + 64, 128 * blk + c0:128 * blk + c0 + w]
                Hh = self.Hh[gp % ways]
                hk = f"Hh{gp % ways}"
                op("act", "copy", Hh[:, :, 0:w], H[:, :, 0:w], R=[kx + "6"], W=[hk])
                mm(yo, self.Cl[:, gp, 0, :], Hh[:, 0, 0:w], start=(gp % 2 == 0), stop=False, R=["Cl", hk], W=[f"ps{ybk}"])
                mm(yo, self.Cl[:, gp, 1, :], Hh[:, 1, 0:w], start=False, stop=(gp % 2 == 1), R=["Cl", hk], W=[f"ps{ybk}"])
                if sample:
                    op("pool", "tensor_copy", self.S5O[:, gp, :, :],
                       H[:, :, 0:w].rearrange("p c (s t) -> p c s t", t=TS)[:, :, :, TS - 1], R=[kx + "6"], W=["S5O"])
                else:
                    op("pool", "tensor_copy", self.S5C[:, gp, :].unsqueeze(2), H[:, :, w - 1:w], R=[kx + "6"], W=["S5C"])
                if gp % ways == ways - 1 and rec is not None:
                    marks.append(len(rec))
                    lists = [rec[marks[q]:marks[q + 1]] for q in range(ways)]
                    mix_ = []
                    for q in range(max(len(x) for x in lists)):
                        for x in lists:
                            if q < len(x):
                                mix_.append(x[q])
                    rec[marks[0]:] = mix_
                if gp % ways == ways - 1:
                    self.psf(pbanks[0])
                    self.psf(pbanks[1])
        for blk in range(2):
            op("dve", "scalar_tensor_tensor", Y5[:, blk, 0:n], self.UB[:, blk, 0:n], self.cp[:, CP_S5D + blk:CP_S5D + blk + 1],
               ps[ybk][:, 128 * blk:128 * blk + n], ALU.mult, ALU.add, R=["UB", "cp", f"ps{ybk}"], W=["s5w7"])
        self.psf(ybk)
        T1, T2, T3 = self.s5w[0], self.s5w[1], self.s5w[4]
        t = T1
        act(t[:, :, 0:n], Y5[:, :, 0:n], AF.Square, R=["s5w7"], W=["s5w0"])
        op("dve", "tensor_scalar", t[:, :, 0:n], t[:, :, 0:n], 0.044715, 1.0, ALU.mult, ALU.add, R=["s5w0"], W=["s5w0"])
        op("pool", "tensor_tensor", t[:, :, 0:n], t[:, :, 0:n], Y5[:, :, 0:n], ALU.mult, R=["s5w0", "s5w7"], W=["s5w0"])
        act(t[:, :, 0:n], t[:, :, 0:n], AF.Tanh, scale=0.7978845608028654, R=["s5w0"], W=["s5w0"])
        gl = T2
        op("dve", "scalar_tensor_tensor", gl[:, :, 0:n], t[:, :, 0:n], 1.0, Y5[:, :, 0:n], ALU.add, ALU.mult,
           R=["s5w0", "s5w7"], W=["s5w1"])
        op("act", "copy", self.Y5B[:, :, 0:n], gl[:, :, 0:n], R=["s5w1"], W=["Y5B"])
        for eb in range(2):
            b = self.psa()
            for kc in range(2):
                mm(ps[b][:, 0:n], self.Wglu[:, kc, 128 * eb:128 * eb + 128], self.Y5B[:, kc, 0:n], start=(kc == 0), stop=(kc == 1),
                   R=["Wglu", "Y5B"], W=[f"ps{b}"])
            th = T3
            act(th[:, eb, 0:n], ps[b][:, 0:n], AF.Tanh, scale=0.25, bias=self.hglb[:, eb:eb + 1], R=[f"ps{b}", "hglb"], W=["s5w4"])
            self.psf(b)
            op("dve", "scalar_tensor_tensor", th[:, eb, 0:n], th[:, eb, 0:n], 1.0, gl[:, eb, 0:n], ALU.add, ALU.mult,
               R=["s5w4", "s5w1"], W=["s5w4"])
            op("dve", "scalar_tensor_tensor", self.mixT[:, 2 + eb, 0:n], th[:, eb, 0:n], 0.25, self.SZ[:, 2 + eb, 0:n],
               ALU.mult, ALU.mult, R=["s5w4", "SZ"], W=["mixT"])


def _host_inputs(inp, nl):
    f = lambda a: np.ascontiguousarray(a, dtype=np.float32)
    w_in = inp["w_in"][:nl]
    shared = {}
    shared["wfm"] = f(w_in[:, :, FM_COLS])
    shared["wtm"] = f(w_in[:, :, TM_COLS])
    shared["wout"] = f(inp["w_out"][:nl])
    shared["glu"] = f(inp["s5_glu_w"][:nl])
    shared["cst"] = _build_consts()
    cp = np.zeros((nl, 128, NCP), np.float32)
    for l in range(nl):
        cw = np.concatenate([inp["ssd_conv_w"][l], inp["gdn_conv_w"][l]], axis=1)
        cbias = np.concatenate([inp["ssd_conv_b"][l], inp["gdn_conv_b"][l]], axis=0)
        for cb in range(10):
            for j in range(4):
                cp[l, :, CP_CW + 4 * cb + j] = cw[j, 128 * cb:128 * cb + 128]
            cp[l, :, CP_CB + cb] = cbias[128 * cb:128 * cb + 128]
        for hb in range(2):
            cp[l, :, CP_GSSD + hb] = inp["ssd_norm_g"][l].reshape(256)[128 * hb:128 * hb + 128]
            cp[l, :, CP_GGDN + hb] = inp["gdn_norm_g"][l].reshape(256)[128 * hb:128 * hb + 128]
            cp[l, :, CP_GHG + hb] = inp["hg_norm_g"][l].reshape(256)[128 * hb:128 * hb + 128]
            cp[l, :, CP_DSSD + hb] = np.repeat(inp["ssd_d"][l], 64)[128 * hb:128 * hb + 128]
            cp[l, :, CP_GLUB + hb] = inp["s5_glu_b"][l][128 * hb:128 * hb + 128]
            cp[l, :, CP_S5D + hb] = inp["s5_d"][l].reshape(256)[128 * hb:128 * hb + 128]
    shared["cp"] = cp
    rp = np.zeros((nl, 128, 16), np.float32)
    for l in range(nl):
        row = np.concatenate([inp["ssd_dt_bias"][l], inp["ssd_a_log"][l], inp["gdn_dt_bias"][l], inp["gdn_a_log"][l]])
        rp[l] = np.broadcast_to(row[None, :], (128, 16))
    shared["rp"] = rp
    lnp = np.zeros((nl + 1, 2, 128, D), np.float32)
    lnp[0, 0] = np.broadcast_to(inp["ln_in_g"][None, :], (128, D))
    lnp[0, 1] = np.broadcast_to(inp["ln_in_b"][None, :], (128, D))
    for l in range(nl):
        lnp[l + 1, 0] = np.broadcast_to(inp["ln_g"][l][None, :], (128, D))
        lnp[l + 1, 1] = np.broadcast_to(inp["ln_b"][l][None, :], (128, D))
    shared["lnp"] = lnp
    shared["lbraw"] = f(inp["hg_lb_raw"].reshape(NL, 2, 128).transpose(2, 1, 0))
    def pn(a):
        return a.reshape(8, 2, 64).transpose(1, 2, 0).reshape(128, 8)
    s5p = np.zeros((nl, 128, 3, 8), np.float32)
    bl = np.zeros((nl, 128, 8, 2, 128), np.float32)
    cl = np.zeros((nl, 128, 8, 2, 64), np.float32)
    for l in range(nl):
        s5p[l, :, 0] = pn(inp["s5_lam_re"][l])
        s5p[l, :, 1] = pn(inp["s5_lam_im"][l])
        s5p[l, :, 2] = pn(np.broadcast_to(inp["s5_log_dt"][l][:, None], (16, 64)))
        for c, (bsrc, csrc) in enumerate(((inp["s5_b_re"][l], inp["s5_c_re"][l]), (inp["s5_b_im"][l], inp["s5_c_im"][l]))):
            for gp in range(8):
                for g2 in range(2):
                    g = 2 * gp + g2
                    r0 = (gp % 4) * 32 + g2 * 16
                    bl[l, r0:r0 + 16, gp, c, g2 * 64:g2 * 64 + 64] = bsrc[g].T
                    j0 = (gp % 2) * 32 + g2 * 16
                    cl[l, g2 * 64:g2 * 64 + 64, gp, c, j0:j0 + 16] = csrc[g].T
    shared["s5p"], shared["s5bl"], shared["s5cl"] = s5p, bl, cl
    maps = []
    for c in range(NCORE):
        m = dict(shared)
        sl = slice(NSEQ * c, NSEQ * c + NSEQ)
        m["xin"] = f(np.concatenate([inp["meta_tokens"], inp["x_prompt"][c], inp["x_sample"][sl].reshape(NSEQ * TS, D)], 0))
        m["st_ssd"] = f(inp["state_ssd"][:nl, sl])
        m["st_gdn"] = f(inp["state_gdn"][:nl, sl])
        m["st_hg"] = f(inp["state_hgrn"][:nl, sl])
        cv = np.concatenate([inp["state_ssd_conv"][:nl, sl], inp["state_gdn_conv"][:nl, sl]], axis=-1)
        m["st_conv"] = f(cv.reshape(nl, NSEQ, 3, 10, 128).transpose(0, 4, 3, 1, 2))
        s5 = np.stack([inp["state_s5_re"][:nl, sl], inp["state_s5_im"][:nl, sl]], axis=0)
        s5 = s5.reshape(2, nl, NSEQ, 8, 2, 64).transpose(1, 4, 5, 3, 0, 2).reshape(nl, 128, 8, 2, NSEQ)
        m["st_s5"] = f(s5)
        maps.append(m)
    return maps


def _unpack_state(a, mixer):
    out_heads = []
    for h in range(4):
        if mixer == "ssd":
            base, slot = 64 * (h // 2), h % 2
        else:
            base, slot = 64 * (h % 2), h // 2
        out_heads.append(a[..., base:base + 64, slot, :] if a.ndim == 4 else a[..., base:base + 64, slot, :, :])
    return out_heads


_CACHE = {}


def run(inputs, nl=NL, parts=("ssd", "s5", "gdn", "hg")):
    key = (nl, tuple(parts))
    if key not in _CACHE:
        _CACHE[key] = Builder(nl, set(parts)).build()
    nc = _CACHE[key]
    inp = {k: np.asarray(v) for k, v in inputs.items()}
    maps = _host_inputs(inp, nl)
    res = run_bass_kernel_spmd(nc, maps, core_ids=list(range(NCORE)))
    return assemble(res.results, nl)


def assemble(R, nl):
    nco = len(R)
    y = np.stack([r["y_out"] for r in R])
    y_prompt = np.ascontiguousarray(y[:, :2048])
    y_sample = np.ascontiguousarray(y[:, 2048:].reshape(nco * NSEQ, TS, D))
    outs = {}
    pst = np.stack([r["o_pst"] for r in R], axis=1)
    sst = np.stack([r["o_sst"] for r in R], axis=1)
    for m, mixer in enumerate(("ssd", "gdn", "hg")):
        ph, sh = [], []
        for h in range(4):
            if mixer == "ssd":
                base, slot = 64 * (h // 2), h % 2
            else:
                base, slot = 64 * (h % 2), h // 2
            ph.append(pst[:, :, m, base:base + 64, slot, :])
            sh.append(sst[:, :, m, base:base + 64, slot, :, :].transpose(0, 1, 3, 2, 4))
        outs["p_" + mixer] = np.ascontiguousarray(np.stack(ph, axis=2))
        outs["s_" + mixer] = np.ascontiguousarray(np.stack(sh, axis=3).reshape(nl, nco * NSEQ, 4, 64, 64))
    pconv = np.stack([r["o_pconv"] for r in R], axis=1)
    pconv = pconv.transpose(0, 1, 4, 3, 2).reshape(nl, nco, 3, 1280)
    sconv = np.stack([r["o_sconv"] for r in R], axis=1)
    sconv = sconv.transpose(0, 1, 4, 5, 3, 2).reshape(nl, nco * NSEQ, 3, 1280)
    ps5 = np.stack([r["o_ps5"] for r in R], axis=1)
    ps5 = ps5.reshape(nl, nco, 2, 64, 8, 2).transpose(5, 0, 1, 4, 2, 3).reshape(2, nl, nco, 16, 64)
    ss5 = np.stack([r["o_ss5"] for r in R], axis=1)
    ss5 = ss5.reshape(nl, nco, 2, 64, 8, 2, NSEQ).transpose(5, 0, 1, 6, 4, 2, 3).reshape(2, nl, nco * NSEQ, 16, 64)
    c = np.ascontiguousarray
    return (y_prompt, y_sample,
            outs["p_ssd"], c(pconv[..., :512]), c(ps5[0]), c(ps5[1]), outs["p_gdn"], c(pconv[..., 512:]), outs["p_hg"],
            outs["s_ssd"], c(sconv[..., :512]), c(ss5[0]), c(ss5[1]), outs["s_gdn"], c(sconv[..., 512:]), outs["s_hg"])


def kernel(**inputs):
    return run(inputs)
```

```python
import contextlib
import math
import numpy as np
import concourse.bass as bass
import concourse.mybir as mybir
from concourse.bass_utils import run_bass_kernel_spmd

F32 = mybir.dt.float32
BF16 = mybir.dt.bfloat16
ALU = mybir.AluOpType
AF = mybir.ActivationFunctionType
AX = mybir.AxisListType

NL = 4
NCORE = 8
D = 1024
NSEQ = 16
TS = 4
NROW = 16 + 2048 + NSEQ * TS
BIG = 30000.0
ENGS = ("pe", "dve", "act", "pool", "sp")

FM_ZA, FM_ZB, FM_ZC, FM_ZD = 0, 2, 4, 6
FM_CONV = 8
FM_UB = 18
FM_QD = 20
FM_FD = 22
NFM = 24
NTM = 268

_R = dict(za=(0, 256), xs=(256, 512), B=(512, 640), C=(640, 768), dt=(768, 772), zb=(772, 1028),
          ub=(1028, 1284), zc=(1284, 1540), q=(1540, 1796), k=(1796, 2052), v=(2052, 2308),
          br=(2308, 2312), ar=(2312, 2316), zd=(2316, 2572), qd=(2572, 2828), fd=(2828, 3084),
          idd=(3084, 3340))
_FM_ORDER = ["za", "zb", "zc", "zd", "xs", "B", "C", "q", "k", "v", "ub", "qd", "fd"]
_TM_ORDER = ["dt", "br", "ar", "idd"]
FM_COLS = np.concatenate([np.arange(*_R[n]) for n in _FM_ORDER])
TM_COLS = np.concatenate([np.arange(*_R[n]) for n in _TM_ORDER])

C_ID, C_ONES, C_ONES64 = 0, 1, 2
def C_TRI(s): return 3 + 4 * s
def C_TRISU(s): return 4 + 4 * s
def C_NBI(s): return 5 + 4 * s
def C_NBS(s): return 6 + 4 * s
def C_RST(s): return 15 + s
def C_BLK(s): return 18 + s
NCST = 21
CP_CW, CP_CB, CP_GSSD, CP_GGDN, CP_GHG, CP_DSSD, CP_GLUB, CP_S5D = 0, 40, 50, 52, 54, 56, 58, 60
NCP = 62


def _build_consts():
    c = np.zeros((NCST, 128, 128), np.float32)
    c[C_ID] = np.eye(128)
    c[C_ONES] = 1.0
    for a in range(2):
        c[C_ONES64, 64 * a:64 * a + 64, 64 * a:64 * a + 64] = 1.0
    idx = np.arange(128)
    for s, lb in enumerate((128, 64, 4)):
        blk = idx // lb
        same = blk[:, None] == blk[None, :]
        le = idx[:, None] <= idx[None, :]
        lt = idx[:, None] < idx[None, :]
        gt = idx[:, None] > idx[None, :]
        c[C_TRI(s)] = (same & le)
        c[C_TRISU(s)] = (same & gt)
        c[C_NBI(s)] = np.where(same & le, 0.0, -BIG)
        c[C_NBS(s)] = np.where(same & lt, 0.0, -BIG)
        c[C_RST(s)] = np.broadcast_to(np.where(idx % lb == 0, 0.0, 1.0)[None, :], (128, 128))
        nb = 128 // lb
        oh = np.zeros((128, 128), np.float32)
        oh[idx, blk] = 1.0
        c[C_BLK(s)] = oh
    return c


class Prog:
    def __init__(self, nc, stack):
        self.nc = nc
        self.stack = stack
        self.q = {e: [] for e in ENGS}
        self.cnt = {}
        self.sems = {}
        self.seen = {e: {} for e in ENGS}
        self.keys = {}
        self.nops = 0
        self.pe_last = {}
        self.rec = None
        for e in ENGS:
            self._sem("E_" + e)

    def _sem(self, name):
        if name not in self.sems:
            self.sems[name] = self.stack.enter_context(self.nc.semaphore(name))
            self.cnt[name] = 0
        return self.sems[name]

    def _collect(self, eng, reads, writes):
        waits = {}
        xr = [k for k in reads if k.startswith("ps")]
        if xr:
            writes = list(writes) + xr

        def add(ev):
            if ev is None:
                return
            s, v = ev
            if eng == "pe" and s == "E_pe":
                return
            if s.startswith("D_"):
                v = self.cnt[s]
            if self.seen[eng].get(s, 0) >= v:
                return
            if waits.get(s, 0) < v:
                waits[s] = v

        for k in reads:
            st = self.keys.get(k)
            if st is not None:
                add(st["w"])
        for k in writes:
            st = self.keys.get(k)
            if st is not None:
                add(st["w"])
                for s, v in st["r"].items():
                    add((s, v))
        for s, v in waits.items():
            self.seen[eng][s] = v
        return list(waits.items())

    def _commit(self, ev, reads, writes):
        xr = [k for k in reads if k.startswith("ps")]
        if xr:
            writes = list(writes) + xr
            reads = [k for k in reads if not k.startswith("ps")]
        for k in reads:
            st = self.keys.setdefault(k, {"w": None, "r": {}})
            if st["r"].get(ev[0], 0) < ev[1]:
                st["r"][ev[0]] = ev[1]
        for k in writes:
            self.keys[k] = {"w": ev, "r": {}}

    def op(self, eng, fn, reads=(), writes=(), pe_sig=None):
        if self.rec is not None:
            self.rec.append(("op", eng, fn, tuple(reads), tuple(writes), pe_sig))
            return
        waits = self._collect(eng, reads, writes)
        s = "E_" + eng
        if pe_sig is not None:
            sig = pe_sig
            for k in writes:
                if k.startswith("ps"):
                    last = self.pe_last.get(k)
                    if last is not None and last[0] != sig and self.seen["pe"].get("E_pe", 0) < last[1]:
                        waits = [w for w in waits if w[0] != "E_pe"] + [("E_pe", last[1])]
                        self.seen["pe"]["E_pe"] = last[1]
                    self.pe_last[k] = (sig, self.cnt[s] + 1)
        self.cnt[s] += 1
        self.nops += 1
        ev = (s, self.cnt[s])
        sems = self.sems

        def emit(e):
            for ws, wv in waits:
                e.wait_ge(sems[ws], wv)
            fn(e).then_inc(sems[s], 1)

        self.q[eng].append(emit)
        self._commit(ev, reads, writes)

    def dma(self, out, in_, group, reads=(), writes=(), eng="sp"):
        if self.rec is not None:
            self.rec.append(("dma", out, in_, group, tuple(reads), tuple(writes), eng))
            return
        waits = self._collect(eng, reads, writes)
        s = "D_" + group
        self._sem(s)
        self.cnt[s] += 16
        self.nops += 1
        ev = (s, self.cnt[s])
        sems = self.sems

        def emit(e):
            for ws, wv in waits:
                e.wait_ge(sems[ws], wv)
            e.dma_start(out=out, in_=in_).then_inc(sems[s], 16)

        self.q[eng].append(emit)
        self._commit(ev, reads, writes)

    def record(self, fn):
        assert self.rec is None
        self.rec = []
        try:
            fn()
        finally:
            lst, self.rec = self.rec, None
        return lst

    def replay(self, lists, ratio):
        pos = [0] * len(lists)
        while any(p < len(l) for p, l in zip(pos, lists)):
            for i, l in enumerate(lists):
                for _ in range(ratio[i]):
                    if pos[i] < len(l):
                        it = l[pos[i]]
                        pos[i] += 1
                        if it[0] == "op":
                            self.op(it[1], it[2], it[3], it[4], it[5])
                        else:
                            self.dma(it[1], it[2], it[3], it[4], it[5], it[6])

    def finish(self, eng="sp"):
        waits = [(s, v) for s, v in self.cnt.items()
                 if v > 0 and self.seen[eng].get(s, 0) < v and s != "E_" + eng]
        sems = self.sems

        def emit(e):
            for ws, wv in waits:
                e.wait_ge(sems[ws], wv)

        self.q[eng].append(emit)

    def emit(self):
        q = self.q
        with self.nc.Block() as block:
            @block.tensor
            def _(e):
                for f in q["pe"]:
                    f(e)

            @block.vector
            def _(e):
                for f in q["dve"]:
                    f(e)

            @block.scalar
            def _(e):
                for f in q["act"]:
                    f(e)

            @block.gpsimd
            def _(e):
                for f in q["pool"]:
                    f(e)

            @block.sync
            def _(e):
                for f in q["sp"]:
                    f(e)


class Builder:
    def __init__(self, nl, parts):
        self.nl = nl
        self.parts = parts
        self.nc = bass.Bass("TRN2", target_bir_lowering=False)
        self.st = contextlib.ExitStack()
        self.P = Prog(self.nc, self.st)
        self.tiles = [("meta", 0, 16)] + [("prompt", 16 + 128 * i, 128) for i in range(16)] \
            + [("sample", 16 + 2048, NSEQ * TS)]
        import os
        r = int(os.environ.get('KROT', '0'))
        import os
        self.use_fr = os.environ.get("KFR", "0") == "1"
        self.pools = {"main": [0, 1, 2, 3], "s5": [4, 5, 6], "pfx": [7]}
        self.pool = "main"

    def din(self, name, shape, dt=F32):
        return self.nc.dram_tensor(name, list(shape), dt, kind="ExternalInput")

    def dout(self, name, shape, dt=F32):
        return self.nc.dram_tensor(name, list(shape), dt, kind="ExternalOutput")

    def sb(self, name, shape, dt=F32):
        return self.st.enter_context(self.nc.sbuf_tensor(name, list(shape), dt))

    def op(self, eng, meth, *a, R=(), W=(), **kw):
        self.P.op(eng, lambda e: getattr(e, meth)(*a, **kw), reads=R, writes=W)

    @staticmethod
    def _sig(ap):
        k = ap.shape[0]
        kk = 32 if k <= 32 else (64 if k <= 64 else 128)
        return (ap.base_partition() if kk < 128 else 0, kk)

    def mm(self, out, lhsT, rhs, start=True, stop=True, R=(), W=(), fr=False):
        sig = self._sig(lhsT)
        if fr and self.use_fr and lhsT.shape[-1] == 128 and rhs.shape[-1] % 2 == 0:
            lhsT = lhsT.bitcast(mybir.dt.float32r)
            rhs = rhs.bitcast(mybir.dt.float32r)
        self.P.op("pe", lambda e: e.matmul(out, lhsT, rhs, start=start, stop=stop), reads=R, writes=W,
                  pe_sig=sig)

    def tr(self, out, in_, ident, R=(), W=()):
        self.P.op("pe", lambda e: e.transpose(out, in_, ident), reads=R, writes=W, pe_sig=self._sig(in_))

    def act(self, out, in_, func, bias=None, scale=None, R=(), W=()):
        kw = {}
        if bias is not None:
            kw["bias"] = bias
        if scale is not None:
            kw["scale"] = scale
        self.P.op("act", lambda e: e.activation(out, in_, func, **kw), reads=R, writes=W)

    def psa(self):
        return self.pools[self.pool].pop(0)

    def psf(self, b):
        self.pools["main" if b < 4 else ("pfx" if b == 7 else "s5")].append(b)

    def build(self):
        nc, nl = self.nc, self.nl
        sb = self.sb
        self.xin = self.din("xin", [NROW, D])
        self.wfm = self.din("wfm", [nl, D, NFM * 128])
        self.wtm = self.din("wtm", [nl, D, NTM])
        self.wout = self.din("wout", [nl, D, D])
        self.glu = self.din("glu", [nl, 256, 256])
        self.cstd = self.din("cst", [NCST, 128, 128])
        self.cpd = self.din("cp", [nl, 128, NCP])
        self.rpd = self.din("rp", [nl, 128, 16])
        self.lnd = self.din("lnp", [nl + 1, 2, 128, D])
        self.lbd = self.din("lbraw", [128, 2, NL])
        self.s5pd = self.din("s5p", [nl, 128, 3, 8])
        self.bld = self.din("s5bl", [nl, 128, 8, 2, 128])
        self.cld = self.din("s5cl", [nl, 128, 8, 2, 64])
        self.sstd = [self.din(n, [nl, NSEQ, 4, 64, 64]) for n in ("st_ssd", "st_gdn", "st_hg")]
        self.convd = self.din("st_conv", [nl, 128, 10, NSEQ, 3])
        self.s5sd = self.din("st_s5", [nl, 128, 8, 2, NSEQ])
        self.yout = self.dout("y_out", [2048 + NSEQ * TS, D])
        self.o_pst = self.dout("o_pst", [nl, 3, 128, 2, 64])
        self.o_sst = self.dout("o_sst", [nl, 3, 128, 2, NSEQ, 64])
        self.o_pconv = self.dout("o_pconv", [nl, 128, 10, 3])
        self.o_sconv = self.dout("o_sconv", [nl, 128, 10, NSEQ, 3])
        self.o_ps5 = self.dout("o_ps5", [nl, 128, 8, 2])
        self.o_ss5 = self.dout("o_ss5", [nl, 128, 8, 2, NSEQ])
        self.xscr = nc.dram_tensor("xscr", [NROW, D], F32, kind="Internal")

        self.ps = [self.st.enter_context(nc.psum_tensor(f"ps{i}", [128, 512], F32)) for i in range(8)]
        self.cst = sb("cstt", [128, NCST, 128])
        self.Wfm = sb("Wfm", [128, 8, NFM * 128], BF16)
        self.Wtm = sb("Wtm", [128, 8, NTM], BF16)
        self.Wout = sb("Wout", [128, 8, D], BF16)
        self.Wglu = sb("Wglu", [128, 2, 256], BF16)
        self.WB = sb("WB", [128, 2048])
        self.stg = [self.WB[:, 0:512], self.WB[:, 512:1024]]
        self.TMG = [sb("TMG0", [128, 512]), sb("TMG1", [128, 512])]
        self.VT4 = sb("VT4", [128, 256])
        self.NH = [sb(f"NH{i}", [128, 512], BF16) for i in range(4)]
        self.RWh = sb("RWh", [128, 512], BF16)
        self.AQW = sb("AQW", [128, 512])
        self.RW = sb("RW", [128, 512])
        self.lng = sb("lng", [128, D])
        self.lnb = sb("lnb", [128, D])
        self.cp = sb("cpt", [128, NCP])
        self.rp = sb("rpt", [128, 16])
        self.nega = sb("nega", [128, 8])
        self.lbr = sb("lbr", [128, 2, NL])
        self.lbt = sb("lbt", [128, 2, NL])
        self.hgc = sb("hgc", [128, 2, 3])
        self.xt = [sb(f"xt{i}", [128, D]) for i in range(2)]
        self.xT = sb("xT", [128, 8, 128], BF16)
        self.mixT = sb("mixT", [128, 8, 128], BF16)
        self.SZ = sb("SZ", [128, 8, 128], BF16)
        self.XC = sb("XC", [128, 10, 131])
        self.XCS = sb("XCS", [128, 10, NSEQ, 7])
        self.CS = sb("CS", [128, 10, NSEQ, 3])
        self.CP3 = sb("CP3", [128, 10, 3])
        self.CV = sb("CV", [128, 10, 128])
        self.UB = sb("UB", [128, 2, 128], BF16)
        self.QD = sb("QD", [128, 2, 128])
        self.TH = sb("TH", [128, 2, 128])
        self.TMs = sb("TMs", [128, NTM])
        self.SM = sb("SM", [128, 16])
        self.SM2 = sb("SM2", [128, 40])
        self.EB = sb("EB", [128, 2, 64])
        self.GM = sb("GM", [128, 2, 64])
        self.s5u = self.GM[:, 1, 0:32].rearrange("p (r g) -> p r g", r=4)
        self.PST = [sb(f"PST{m}", [128, 2, 64]) for m in range(3)]
        self.SST = [sb(f"SST{i}", [128, 2, NSEQ, 64]) for i in range(1)]
        self.scr = {i: sb(f"scr{i}", [128, 128]) for i in (list(range(0, 7)) + list(range(8, 28)) + [35, 36])}
        self.big = [sb("big0", [128, 1024]), self.WB[:, 1024:2048]]
        self.stat = sb("stat", [128, 32])
        self.s5p = sb("s5pt", [128, 3, 8])
        self.Bl = sb("Blt", [128, 8, 2, 128], BF16)
        self.Cl = sb("Clt", [128, 8, 2, 64], BF16)
        self.Hh = [sb(f"Hh{i}", [128, 2, 64], BF16) for i in range(4)]
        self.KI = sb("KI", [128, 8, 3, 64])
        self.KO = sb("KO", [128, 8, 3, 65])
        self.hglb = sb("hglb", [128, 2])
        self.s5t = sb("s5t", [128, 12, 8])
        self.S5C = sb("S5C", [128, 8, 2])
        self.S5S = sb("S5S", [128, 8, 2, NSEQ])
        self.S5O = sb("S5O", [128, 8, 2, NSEQ])
        self.s5w = [sb(f"s5w{i}", [128, 2, 128]) for i in range(8)]
        self.s5x = [sb(f"s5x{i}", [128, 2, 64]) for i in range(7)]
        sstf = self.SST[0][:].rearrange("p a s v -> p (a s v)")
        self.s5y = [sstf[:, 128 * i:128 * i + 128].rearrange("p (c t) -> p c t", c=2) for i in range(7)]
        self.s5z = [sstf[:, 1024 + 128 * i:1024 + 128 * i + 128].rearrange("p (c t) -> p c t", c=2) for i in range(7)]
        self.alias_keys = [f"s5y{i}" for i in range(7)] + [f"s5z{i}" for i in range(7)]
        self.Y5B = sb("Y5B", [128, 2, 128], BF16)
        self.s5i = sb("s5i", [128, 8], mybir.dt.int32)

        P = self.P
        P.dma(self.cst[:], self.cstd.ap().rearrange("c p j -> p c j"), "cst", writes=["cst"])
        P.dma(self.lbr[:], self.lbd.ap(), "cst", writes=["lbr"])
        import os
        self.stop = int(os.environ.get("KSTOP", "99"))
        if self.stop >= 1:
            self.hg_lower_bounds()
        lp = P.record(self.prologue)
        lw = P.record(lambda: self.load_w_in(0))
        P.replay([lp, lw], [1, 1])
        self.load_rest(0)
        self.load_out(0)
        for l in range(nl):
            self.layer(l)
        P.finish("sp")
        P.emit()
        self.st.close()
        return nc

    def C(self, idx, n0=None, n1=None):
        if n0 is None:
            return self.cst[:, idx, :]
        return self.cst[0:n0, idx, 0:n1]

    def hg_lower_bounds(self):
        op = self.op
        e = self.scr[0][:, 0:8].rearrange("p (b l) -> p b l", b=2)
        s = self.stat[:, 0:2]
        self.act(e, self.lbr[:], AF.Exp, R=["lbr"], W=["scr0"])
        op("dve", "tensor_reduce", s, e, AX.X, ALU.add, R=["scr0"], W=["stat"])
        op("dve", "reciprocal", s, s, R=["stat"], W=["stat"])
        op("dve", "tensor_tensor", e, e, s.unsqueeze(2).to_broadcast([128, 2, NL]), ALU.mult,
           R=["scr0", "stat"], W=["scr0"])
        op("dve", "memset", self.lbt[:, :, 0:1], 0.0, W=["lbt"])
        for l in range(1, NL):
            op("dve", "tensor_tensor", self.lbt[:, :, l:l + 1], self.lbt[:, :, l - 1:l], e[:, :, l:l + 1], ALU.add,
               R=["lbt", "scr0"], W=["lbt"])

    def layernorm(self, xt, xk, n):
        op = self.op
        st6 = self.scr[1][:, 0:12].rearrange("p (c s) -> p c s", c=2)
        mv = self.stat[:, 4:6]
        rs = self.stat[:, 6:7]
        for c in range(2):
            op("dve", "bn_stats", st6[0:n, c, :], xt[0:n, 512 * c:512 * c + 512], R=[xk], W=["scr1"])
        op("dve", "bn_aggr", mv[0:n], st6[0:n].rearrange("p c s -> p (c s)"), R=["scr1"], W=["stat"])
        op("pool", "tensor_scalar", rs[0:n], mv[0:n, 1:2], 1e-5, None, ALU.add, R=["stat"], W=["stat"])
        op("pool", "tensor_tensor", rs[0:n], rs[0:n], self.epsc[0:n, 3:4], ALU.pow, R=["stat", "epsc"], W=["stat"])
        op("dve", "scalar_tensor_tensor", xt[0:n], xt[0:n], mv[0:n, 0:1], self.lng[0:n], ALU.subtract, ALU.mult,
           R=[xk, "stat", "lng"], W=[xk])
        op("dve", "scalar_tensor_tensor", xt[0:n], xt[0:n], rs[0:n], self.lnb[0:n], ALU.mult, ALU.add,
           R=[xk, "stat", "lnb"], W=[xk])

    def prologue(self):
        P = self.P
        self.epsc = self.sb("epsc", [128, 4])
        self.op("dve", "memset", self.epsc[:, 0:1], 1e-5, W=["epsc"])
        self.op("dve", "memset", self.epsc[:, 1:2], 1e-6, W=["epsc"])
        self.op("dve", "memset", self.epsc[:, 2:3], 1.0, W=["epsc"])
        self.op("dve", "memset", self.epsc[:, 3:4], -0.5, W=["epsc"])
        P.dma(self.lng[:], self.lnd.ap()[0, 0], "ln", writes=["lng"])
        P.dma(self.lnb[:], self.lnd.ap()[0, 1], "ln", writes=["lnb"])
        for i, (kind, r0, n) in enumerate(self.tiles):
            xt, xk = self.xt[i % 2], f"xt{i % 2}"
            P.dma(xt[0:n], self.xin.ap()[r0:r0 + n, :], xk, writes=[xk])
            self.layernorm(xt, xk, n)
            P.dma(self.xscr.ap()[r0:r0 + n, :], xt[0:n], xk, reads=[xk], writes=[f"xscr{i}"])

    def _load_cast(self, jobs, stg, keys):
        P = self.P
        for ci, (src, dst, key, w) in enumerate(jobs):
            si = ci % 2
            P.dma(stg[si][:, 0:w], src, keys[si], writes=[keys[si]])
            if ci % 2 == 1:
                self.op("pool", "tensor_copy", dst, stg[si][:, 0:w], R=[keys[si]], W=[key])
            else:
                self.op("act", "copy", dst, stg[si][:, 0:w], R=[keys[si]], W=[key])

    @staticmethod
    def _jobs(jobs, src2d, dst2d, key, width):
        c = 0
        while c < width:
            w = min(512, width - c)
            jobs.append((src2d[:, c:c + w], dst2d[:, c:c + w], key, w))
            c += w

    def load_w_in(self, l):
        jobs = []
        for k in range(8):
            self._jobs(jobs, self.wfm.ap()[l, 128 * k:128 * k + 128, :], self.Wfm[:, k, :], "Wfm", NFM * 128)
        for k in range(8):
            self._jobs(jobs, self.wtm.ap()[l, 128 * k:128 * k + 128, :], self.Wtm[:, k, :], "Wtm", NTM)
        self._load_cast(jobs, self.TMG, ["TMG0", "TMG1"])

    def load_out(self, l):
        P = self.P
        jobs = []
        for k in range(8):
            self._jobs(jobs, self.wout.ap()[l, 128 * k:128 * k + 128, :], self.Wout[:, k, :], "Wout", D)
        self._load_cast(jobs, self.stg, ["wb0", "wb1"])
        P.dma(self.lng[:], self.lnd.ap()[l + 1, 0], "ln", writes=["lng"])
        P.dma(self.lnb[:], self.lnd.ap()[l + 1, 1], "ln", writes=["lnb"])

    def load_rest(self, l):
        P = self.P
        jobs = []
        for k in range(2):
            self._jobs(jobs, self.glu.ap()[l, 128 * k:128 * k + 128, :], self.Wglu[:, k, :], "Wglu", 256)
        self._load_cast(jobs, self.stg, ["wb0", "wb1"])
        P.dma(self.cp[:], self.cpd.ap()[l], "prm", writes=["cp"])
        P.dma(self.rp[:], self.rpd.ap()[l], "prm", writes=["rp"])
        P.dma(self.s5p[:], self.s5pd.ap()[l], "prm", writes=["s5p"])
        jobs = []
        self._jobs(jobs, self.bld.ap()[l].rearrange("p g c m -> p (g c m)"), self.Bl[:].rearrange("p g c m -> p (g c m)"), "Bl", 2048)
        self._jobs(jobs, self.cld.ap()[l].rearrange("p g c m -> p (g c m)"), self.Cl[:].rearrange("p g c m -> p (g c m)"), "Cl", 1024)
        self._load_cast(jobs, self.stg, ["wb0", "wb1"])
        self.act(self.nega[:, 0:4], self.rp[:, 4:8], AF.Exp, R=["rp"], W=["nega"])
        self.act(self.nega[:, 4:8], self.rp[:, 12:16], AF.Exp, R=["rp"], W=["nega"])
        self.op("dve", "tensor_scalar", self.nega[:], self.nega[:], -1.0, None, ALU.mult, R=["nega"], W=["nega"])
        lb = self.lbt[:, :, l:l + 1]
        self.op("dve", "tensor_scalar", self.hgc[:, :, 0:1], lb, -0.5, 0.5, ALU.mult, ALU.add, R=["lbt"], W=["hgc"])
        self.op("dve", "tensor_scalar", self.hgc[:, :, 1:2], lb, 0.5, 0.5, ALU.mult, ALU.add, R=["lbt"], W=["hgc"])
        self.op("dve", "tensor_scalar", self.hgc[:, :, 2:3], lb, 0.5, -0.5, ALU.mult, ALU.add, R=["lbt"], W=["hgc"])
        if "s5" in self.parts:
            self.s5_prepare(l)

    def layer(self, l):
        P = self.P
        for m in range(3):
            self.op("pool", "memset", self.PST[m][:], 0.0, W=[f"PST{m}"])
        self.op("pool", "memset", self.XC[:, :, 0:3], 0.0, W=["XCa", "XCb"])
        self.op("pool", "memset", self.S5C[:], 0.0, W=["S5C"])
        if "s5" not in self.parts:
            self.op("pool", "memset", self.S5O[:], 0.0, W=["S5O"])
        P.dma(self.CS[:], self.convd.ap()[l], "CS", writes=["CS"])
        self.op("pool", "tensor_copy", self.XCS[:, :, :, 0:3], self.CS[:], R=["CS"], W=["XCSa", "XCSb"])
        P.dma(self.S5S[:], self.s5sd.ap()[l], "S5S", writes=["S5S"])
        tiles = list(enumerate(self.tiles))
        if self.stop == 6:
            tiles = [t for t in tiles if t[0] in (0, 1, 17)]
        pre0 = P.record(lambda: self.pre(l, *tiles[0]))
        P.replay([pre0], [1])
        more = (l + 1 < self.nl)
        for j, (i, tl) in enumerate(tiles):
            lastt = (j + 1 == len(tiles))
            extra = [P.record(lambda: self.load_w_in(l + 1))] if (lastt and more) else []
            self.mix(l, i, tl, extra)
            lp = P.record(lambda: self.post(l, i, tl))
            if not lastt:
                ln_ = P.record(lambda: self.pre(l, *tiles[j + 1]))
                P.replay([ln_, lp], [max(1, round(len(ln_) / max(1, len(lp)))), 1])
            elif more:
                lr = P.record(lambda: self.load_rest(l + 1))
                P.replay([lr, lp], [max(1, round(len(lr) / max(1, len(lp)))), 1])
                self.load_out(l + 1)
            else:
                P.replay([lp], [1])
        if self.stop < 99 and self.stop != 6:
            return
        self.op("pool", "tensor_copy", self.CS[:], self.XCS[:, :, :, 4:7], R=["XCSa", "XCSb"], W=["CS"])
        P.dma(self.o_sconv.ap()[l], self.CS[:], "CS", reads=["CS"])
        P.dma(self.o_ss5.ap()[l], self.S5O[:], "S5O", reads=["S5O"])

    def pre(self, l, i, tl):
        kind, r0, n = tl
        op, mm, act, tr = self.op, self.mm, self.act, self.tr
        ps = self.ps
        self.pool = "main"
        xt, xk = self.xt[i % 2], f"xt{i % 2}"
        self.P.dma(xt[0:n], self.xscr.ap()[r0:r0 + n, :], xk, reads=[f"xscr{i}"], writes=[xk])
        for half in range(2):
            b = self.psa()
            for j in range(4):
                k = 4 * half + j
                tr(ps[b][:, 128 * j:128 * j + n], xt[0:n, 128 * k:128 * k + 128], self.cst[0:n, C_ID, 0:n],
                   R=[xk, "cst"], W=[f"ps{b}"])
            src = ps[b][:].rearrange("p (j t) -> p j t", j=4)[:, :, 0:n]
            if half == 0:
                op("act", "copy", self.xT[:, 0:4, 0:n], src, R=[f"ps{b}"], W=["xT"])
            else:
                op("dve", "tensor_copy", self.xT[:, 4:8, 0:n], src, R=[f"ps{b}"], W=["xT"])
            self.psf(b)
        b = self.psa()
        for k in range(8):
            mm(ps[b][0:n, 0:NTM], self.xT[:, k, 0:n], self.Wtm[:, k, :], start=(k == 0), stop=(k == 7),
               R=["xT", "Wtm"], W=[f"ps{b}"])
        op("dve", "tensor_copy", self.TMs[0:n, :], ps[b][0:n, 0:NTM], R=[f"ps{b}"], W=["TMs"])
        self.psf(b)
        sample = (kind == "sample")

        def stage_a(g):
            bT = self.psa()
            for k in range(8):
                mm(ps[bT][0:n, :], self.xT[:, k, 0:n], self.Wfm[:, k, 512 * g:512 * g + 512],
                   start=(k == 0), stop=(k == 7), R=["xT", "Wfm"], W=[f"ps{bT}"])
            return bT

        def stage_b(g, bT):
            tg = self.TMG[g % 2]
            tgk = f"TMG{g % 2}"
            if g % 2 == 0:
                op("act", "copy", tg[0:n, :], ps[bT][0:n, :], R=[f"ps{bT}"], W=[tgk])
            else:
                op("dve", "tensor_copy", tg[0:n, :], ps[bT][0:n, :], R=[f"ps{bT}"], W=[tgk])
            self.psf(bT)
            b = self.psa()
            for j in range(4):
                tr(ps[b][:, 128 * j:128 * j + n], tg[0:n, 128 * j:128 * j + 128], self.cst[0:n, C_ID, 0:n],
                   R=[tgk, "cst"], W=[f"ps{b}"])
            src = ps[b][:].rearrange("p (j t) -> p j t", j=4)[:, :, 0:n]
            R = [f"ps{b}"]
            if g < 2:
                act(self.SZ[:, 4 * g:4 * g + 4, 0:n], src, AF.Silu, R=R, W=["SZ"])
            elif g <= 4:
                c0 = 4 * (g - 2)
                nb_ = 4 if g < 4 else 2
                ck = "a" if g == 2 else "b"
                if sample:
                    dst = self.XCS[:, c0:c0 + nb_, :, 3:7]
                    s4 = ps[b][:].rearrange("p (j s t) -> p j s t", j=4, t=TS)[:, 0:nb_, 0:NSEQ, :]
                    op("act", "copy", dst, s4, R=R, W=["XCS" + ck])
                else:
                    op("act", "copy", self.XC[:, c0:c0 + nb_, 3:3 + n], src[:, 0:nb_, :], R=R, W=["XC" + ck])
                if g == 4:
                    op("act", "copy", self.UB[:, :, 0:n], src[:, 2:4, :], R=R, W=["UB"])
            else:
                act(self.QD[:, :, 0:n], src[:, 0:2, :], AF.Silu, R=R, W=["QD"])
                act(self.TH[:, :, 0:n], src[:, 2:4, :], AF.Tanh, scale=0.5, R=R, W=["TH"])
            self.psf(b)

        order = [0, 1, 5, 2, 3, 4]
        banks = {}
        banks[order[0]] = stage_a(order[0])
        for q, g in enumerate(order):
            if q + 1 < len(order):
                banks[order[q + 1]] = stage_a(order[q + 1])
            stage_b(g, banks[g])

    def mix(self, l, i, tl, extra=()):
        kind, r0, n = tl
        op = self.op

        def s5_prefix():
            self.pool = "pfx"
            if "gdn" in self.parts:
                self.conv(kind, n, range(4, 10))
                self.gdn_prep(kind, n)
            self.pool = "main"

        def s5_body():
            self.pool = "s5"
            if "s5" in self.parts:
                self.s5_tile(l, kind, n)
            else:
                op("pool", "memset", self.mixT[:, 2:4, 0:n], 0.0, W=["mixT"])
            self.pool = "main"

        def main1():
            self.pool = "main"
            self.conv(kind, n, range(0, 4))
            if "gdn" not in self.parts:
                self.conv(kind, n, range(4, 10))
            self.decay_prelude(kind, n)
            if "ssd" in self.parts:
                self.ssd_tile(l, kind, n)
            else:
                op("pool", "memset", self.mixT[:, 0:2, 0:n], 0.0, W=["mixT"])

        def main2():
            self.pool = "main"
            if "gdn" in self.parts:
                self.gdn_tile(l, kind, n)
            else:
                op("pool", "memset", self.mixT[:, 4:6, 0:n], 0.0, W=["mixT"])
            if "hg" in self.parts:
                self.hg_tile(l, kind, n)
            else:
                op("pool", "memset", self.mixT[:, 6:8, 0:n], 0.0, W=["mixT"])

        P = self.P
        pre_ = P.record(s5_prefix)
        body = P.record(s5_body)
        m1 = P.record(main1)
        m2 = P.record(main2)
        assert len(pre_) <= len(m1), (len(pre_), len(m1))
        kb = min(len(body), round(len(m1) * len(body) / max(1, len(m1) + len(m2))))
        P.replay([m1, pre_, body[:kb]], [max(1, round(len(m1) / max(1, kb))), max(1, round(len(pre_) / max(1, kb))), 1])
        s5l = body
        k1 = kb
        rest = s5l[k1:]
        lists = [m2, rest] + list(extra)
        ratio = [max(1, round(len(m2) / max(1, len(rest)))), 1]
        for x in extra:
            ratio.append(max(1, round(len(x) / max(1, len(rest)))))
        P.replay(lists, ratio)

    def post(self, l, i, tl):
        kind, r0, n = tl
        op, mm = self.op, self.mm
        ps = self.ps
        P = self.P
        self.pool = "s5"
        xt, xk = self.xt[i % 2], f"xt{i % 2}"
        last = (l == self.nl - 1)
        alpha = (2.0 * NL) ** 0.25
        for half in range(2):
            b = self.psa()
            for k in range(8):
                mm(ps[b][0:n, :], self.mixT[:, k, 0:n], self.Wout[:, k, 512 * half:512 * half + 512],
                   start=(k == 0), stop=(k == 7), R=["mixT", "Wout"], W=[f"ps{b}"])
            op("dve", "scalar_tensor_tensor", xt[0:n, 512 * half:512 * half + 512], xt[0:n, 512 * half:512 * half + 512],
               alpha, ps[b][0:n, :], ALU.mult, ALU.add, R=[xk, f"ps{b}"], W=[xk])
            self.psf(b)
        self.pool = "main"
        self.layernorm(xt, xk, n)
        if not last:
            P.dma(self.xscr.ap()[r0:r0 + n, :], xt[0:n], xk, reads=[xk], writes=[f"xscr{i}"])
        elif kind != "meta":
            o0 = r0 - 16
            P.dma(self.yout.ap()[o0:o0 + n, :], xt[0:n], xk, reads=[xk])
        if kind == "prompt" and i == 16:
            for m in range(3):
                P.dma(self.o_pst.ap()[l, m], self.PST[m][:], f"PST{m}", reads=[f"PST{m}"])
            self.op("pool", "tensor_copy", self.CP3[:], self.XC[:, :, 0:3], R=["XCa", "XCb"], W=["CP3"])
            P.dma(self.o_pconv.ap()[l], self.CP3[:], "CP3", reads=["CP3"])
            P.dma(self.o_ps5.ap()[l], self.S5C[:], "S5C", reads=["S5C"])

    def conv(self, kind, n, cbs):
        op, act = self.op, self.act
        sample = (kind == "sample")
        cbs = list(cbs)
        ck = "a" if cbs[0] < 4 else "b"
        for cb in cbs:
            acc = self.scr[2 + cb % 2] if ck == "a" else self.scr[0 + cb % 2]
            ak = f"scr{2 + cb % 2}" if ck == "a" else f"scr{0 + cb % 2}"
            w = lambda j: self.cp[:, CP_CW + 4 * cb + j:CP_CW + 4 * cb + j + 1]
            bias = self.cp[:, CP_CB + cb:CP_CB + cb + 1]
            if sample:
                a3 = acc[:, 0:n].rearrange("p (s t) -> p s t", t=TS)
                xin = lambda j: self.XCS[:, cb, :, j:j + TS]
                key = "XCS" + ck
            else:
                a3 = acc[:, 0:n]
                xin = lambda j: self.XC[:, cb, j:j + n]
                key = "XC" + ck
            act(a3, xin(0), AF.Identity, scale=w(0), bias=bias, R=[key, "cp"], W=[ak])
            for j in range(1, 4):
                op("dve", "scalar_tensor_tensor", a3, xin(j), w(j), a3, ALU.mult, ALU.add, R=[key, "cp", ak], W=[ak])
            act(self.CV[:, cb, 0:n], acc[:, 0:n], AF.Silu, R=[ak], W=["CV" + ck])
        if not sample:
            c0, c1 = cbs[0], cbs[-1] + 1
            op("pool", "tensor_copy", self.XC[:, c0:c1, 0:3], self.XC[:, c0:c1, n:n + 3], R=["XC" + ck], W=["XC" + ck])

    def struct(self, kind, mixer):
        if kind == "meta":
            return 0, 16, 1, [[0]]
        if kind == "sample":
            return 2, TS, NSEQ, [list(range(NSEQ))]
        if mixer == "ssd":
            return 0, 128, 1, [[0]]
        return 1, 64, 2, [[0], [1]]

    def decay_prelude(self, kind, n):
        op, mm, act = self.op, self.mm, self.act
        ps = self.ps
        SM, SM2, TMs = self.SM, self.SM2, self.TMs
        tmp = SM2[:, 36:40]
        op("dve", "tensor_tensor", SM[0:n, 0:4], TMs[0:n, 0:4], self.rp[0:n, 0:4], ALU.add, R=["TMs", "rp"], W=["SM"])
        op("dve", "tensor_tensor", SM[0:n, 12:16], TMs[0:n, 8:12], self.rp[0:n, 8:12], ALU.add, R=["TMs", "rp"], W=["SM"])
        for c in (0, 12):
            act(SM[0:n, c:c + 4], SM[0:n, c:c + 4], AF.Exp, R=["SM"], W=["SM"])
            act(SM[0:n, c:c + 4], SM[0:n, c:c + 4], AF.Ln, bias=self.epsc[0:n, 2:3], R=["SM", "epsc"], W=["SM"])
        op("dve", "tensor_tensor", SM[0:n, 4:8], SM[0:n, 0:4], self.nega[0:n, 0:4], ALU.mult, R=["SM", "nega"], W=["SM"])
        op("dve", "tensor_tensor", SM[0:n, 12:16], SM[0:n, 12:16], self.nega[0:n, 4:8], ALU.mult, R=["SM", "nega"], W=["SM"])
        act(SM[0:n, 8:12], TMs[0:n, 4:8], AF.Exp, scale=-1.0, R=["TMs"], W=["SM"])
        op("dve", "tensor_scalar", SM[0:n, 8:12], SM[0:n, 8:12], 1.0, None, ALU.add, R=["SM"], W=["SM"])
        op("dve", "reciprocal", SM[0:n, 8:12], SM[0:n, 8:12], R=["SM"], W=["SM"])
        b = self.psa()
        for mi, (mixer, col) in enumerate((("ssd", 4), ("gdn", 12))):
            s, lb, nb, _ = self.struct(kind, mixer)
            g = SM[0:n, col:col + 4]
            mm(ps[b][0:n, 8 * mi:8 * mi + 4], self.C(C_TRI(s), n, n), g, R=["cst", "SM"], W=[f"ps{b}"])
            mm(ps[b][0:n, 8 * mi + 4:8 * mi + 8], self.C(C_TRISU(s), n, n), g, R=["cst", "SM"], W=[f"ps{b}"])
            gm2 = self.GM[0:n, mi, 0:4 * nb]
            gm = gm2.rearrange("p (h b) -> p h b", h=4)
            op("dve", "tensor_tensor", gm, g.unsqueeze(2).to_broadcast([n, 4, nb]),
               self.cst[0:n, C_BLK(s), 0:nb].unsqueeze(1).to_broadcast([n, 4, nb]), ALU.mult,
               R=["SM", "cst"], W=["GM"])
            mm(ps[b][:, 32 + 64 * mi:32 + 64 * mi + 4 * nb], self.cst[0:n, C_ONES, :], gm2, R=["cst", "GM"], W=[f"ps{b}"])
            act(self.EB[:, mi, 0:4 * nb], ps[b][:, 32 + 64 * mi:32 + 64 * mi + 4 * nb], AF.Exp, R=[f"ps{b}"], W=["EB"])
        R = [f"ps{b}"]
        op("dve", "tensor_copy", SM2[0:n, 0:4], ps[b][0:n, 0:4], R=R, W=["SM2"])
        op("dve", "tensor_scalar", SM2[0:n, 4:8], ps[b][0:n, 0:4], -1.0, None, ALU.mult, R=R, W=["SM2"])
        op("dve", "tensor_copy", SM2[0:n, 8:12], ps[b][0:n, 8:12], R=R, W=["SM2"])
        op("dve", "tensor_scalar", SM2[0:n, 12:16], ps[b][0:n, 8:12], -1.0, None, ALU.mult, R=R, W=["SM2"])
        act(SM2[0:n, 16:20], ps[b][0:n, 4:8], AF.Exp, R=R, W=["SM2"])
        act(SM2[0:n, 20:24], ps[b][0:n, 12:16], AF.Exp, R=R, W=["SM2"])
        act(SM2[0:n, 24:28], ps[b][0:n, 8:12], AF.Exp, R=R, W=["SM2"])
        self.psf(b)
        op("dve", "tensor_scalar", SM2[0:n, 28:32], SM[0:n, 8:12], -1.0, None, ALU.mult, R=["SM"], W=["SM2"])
        op("dve", "tensor_tensor", SM2[0:n, 32:36], SM2[0:n, 20:24], SM[0:n, 8:12], ALU.mult, R=["SM", "SM2"], W=["SM2"])

    def rowbc(self, out_ps, col, n, s, mask, nrows=None, pbase=0, W=()):
        gb = self.scr[4]
        m = nrows if nrows is not None else n
        self.op("dve", "tensor_scalar", gb[0:n, 0:m], self.cst[0:n, C_ONES, 0:m], col, None, ALU.mult,
                R=["cst", "SM", "SM2"], W=["scr4"])
        self.mm(out_ps, gb[0:n, 0:m], self.C(C_TRI(s), n, n), start=True, stop=(mask is None),
                R=["scr4", "cst"], W=W)
        if mask is not None:
            self.mm(out_ps, self.cst[0:n, C_ID, 0:m], self.C(mask, n, n), start=False, stop=True,
                    R=["cst"], W=W)

    def rms_gate(self, y, yk, n, gcol, zblk, mblk):
        op, mm, act = self.op, self.mm, self.act
        sq = self.scr[5]
        act(sq[:, 0:n], y, AF.Square, R=[yk], W=["scr5"])
        b = self.psa()
        mm(self.ps[b][:, 0:n], self.C(C_ONES64), sq[:, 0:n], R=["cst", "scr5"], W=[f"ps{b}"])
        rs = self.scr[6]
        act(rs[:, 0:n], self.ps[b][:, 0:n], AF.Ln, scale=1.0 / 64.0, bias=self.epsc[:, 1:2], R=[f"ps{b}", "epsc"], W=["scr6"])
        self.psf(b)
        act(rs[:, 0:n], rs[:, 0:n], AF.Exp, scale=-0.5, R=["scr6"], W=["scr6"])
        op("dve", "scalar_tensor_tensor", sq[:, 0:n], y, self.cp[:, gcol:gcol + 1], rs[:, 0:n], ALU.mult, ALU.mult,
           R=[yk, "cp", "scr6"], W=["scr5"])
        op("dve", "tensor_tensor", self.mixT[:, mblk, 0:n], sq[:, 0:n], self.SZ[:, zblk, 0:n], ALU.mult,
           R=["scr5", "SZ"], W=["mixT"])

    def ssd_tile(self, l, kind, n):
        op, mm, act, tr = self.op, self.mm, self.act, self.tr
        ps, scr = self.ps, self.scr
        s, lb, nb, rounds = self.struct(kind, "ssd")
        sample = (kind == "sample")
        CV, SM, SM2 = self.CV, self.SM, self.SM2
        if sample:
            self.load_sst(l, 0, 0)
        b = self.psa()
        for j in range(3):
            tr(ps[b][0:n, 128 * j:128 * j + 128], CV[:, j, 0:n], self.C(C_ID), R=["CVa", "cst"], W=[f"ps{b}"])
        xdt = scr[8]
        xdt2 = self.big[0]
        xdtv = xdt2[0:n, 0:256].rearrange("p (h d) -> p h d", h=4)
        op("dve", "tensor_tensor", xdtv, ps[b][0:n, 0:256].rearrange("p (h d) -> p h d", h=4),
           SM[0:n, 0:4].unsqueeze(2).to_broadcast([n, 4, 64]), ALU.mult, R=[f"ps{b}", "SM"], W=["big0"])
        bend = xdt2[0:n, 256:512].rearrange("p (h d) -> p h d", h=4)
        for h in range(4):
            g = h // 2
            op("dve", "tensor_scalar", bend[:, h, :], ps[b][0:n, 256 + 64 * g:256 + 64 * g + 64], SM2[0:n, 16 + h:17 + h], None,
               ALU.mult, R=[f"ps{b}", "SM2"], W=["big0"])
        self.psf(b)
        WB = self.WB
        wb = [WB[:, 512 * i:512 * i + 512] for i in range(4)]
        wv = lambda t: t[0:n, :].rearrange("p (h t) -> p h t", h=4)[:, :, 0:n]
        wh = lambda t, h: t[0:n, 128 * h:128 * h + n]
        pv = lambda b_: ps[b_][0:n, :].rearrange("p (h t) -> p h t", h=4)[:, :, 0:n]
        bc4 = lambda ap: ap.unsqueeze(1).to_broadcast([n, 4, n])
        col4 = lambda ap: ap.unsqueeze(2).to_broadcast([n, 4, n])
        ONESn = self.cst[0:n, C_ONES, 0:n]
        IDn = self.cst[0:n, C_ID, 0:n]
        a4 = SM[0:n, 4:8]
        na4 = SM2[0:n, 36:40]
        op("dve", "tensor_scalar", na4, a4, -1.0, None, ALU.mult, R=["SM"], W=["SM2"])
        op("dve", "tensor_tensor", wv(wb[0]), bc4(ONESn), col4(a4), ALU.mult, R=["cst", "SM"], W=["wb0"])
        op("dve", "tensor_tensor", wv(wb[1]), bc4(self.cst[0:n, C_TRI(s), 0:n]), col4(na4), ALU.mult, R=["cst", "SM2"], W=["wb1"])
        bA, bB, bC = self.psa(), self.psa(), self.psa()
        for g in range(2):
            mm(ps[bA][0:n, 128 * g:128 * g + n], CV[64 * g:64 * g + 64, 2, 0:n], CV[64 * g:64 * g + 64, 3, 0:n],
               R=["CVa"], W=[f"ps{bA}"])
        for h in range(4):
            o = ps[bB][0:n, 128 * h:128 * h + n]
            mm(o, wh(wb[0], h), self.C(C_TRI(s), n, n), start=True, stop=False, R=["wb0", "cst"], W=[f"ps{bB}"])
            mm(o, wh(wb[1], h), ONESn, start=False, stop=False, R=["wb1", "cst"], W=[f"ps{bB}"])
            mm(o, IDn, self.C(C_NBI(s), n, n), start=False, stop=True, R=["cst"], W=[f"ps{bB}"])
        for h in range(4):
            base, slot = 64 * (h // 2), h % 2
            mm(ps[bC][base:base + 64, 128 * slot:128 * slot + n], wb[0][0:n, 128 * h:128 * h + 64], self.C(C_TRI(s), n, n),
               R=["wb0", "cst"], W=[f"ps{bC}"])
        act(wv(wb[2]), pv(bB), AF.Exp, R=[f"ps{bB}"], W=["wb2"])
        w22 = wb[2][0:n, :].rearrange("p (g j t) -> p g j t", g=2, j=2)[:, :, :, 0:n]
        sc2 = ps[bA][0:n, 0:256].rearrange("p (g t) -> p g t", g=2)[:, :, 0:n].unsqueeze(2).to_broadcast([n, 2, 2, n])
        op("dve", "tensor_tensor", w22, sc2, w22, ALU.mult, R=[f"ps{bA}", "wb2"], W=["wb2"])
        cgv = wb[3][:, 0:256].rearrange("p (j t) -> p j t", j=2)[:, :, 0:n]
        act(cgv, ps[bC][:, 0:256].rearrange("p (j t) -> p j t", j=2)[:, :, 0:n], AF.Exp, R=[f"ps{bC}"], W=["wb3"])
        op("pool", "tensor_tensor", cgv, cgv, CV[:, 3, 0:n].unsqueeze(1).to_broadcast([128, 2, n]), ALU.mult,
           R=["wb3", "CVa"], W=["wb3"])
        for b_ in (bA, bB, bC):
            self.psf(b_)
        Yb = self.psa()
        for h in range(4):
            g = h // 2
            base = 64 * g
            slot = h % 2
            yo = ps[Yb][64 * slot:64 * slot + 64, 128 * g:128 * g + n]
            mm(yo, xdtv[:, h, :], wh(wb[2], h), start=True, stop=False, R=["big0", "wb2"], W=[f"ps{Yb}"])
            for bi in range(nb):
                c0 = bi * lb
                mm(yo[:, c0:c0 + lb], self.state_ap("ssd", 0, h, bi, sample),
                   wb[3][base:base + 64, 128 * slot + c0:128 * slot + c0 + lb],
                   start=False, stop=(bi == nb - 1), R=["wb3", self.state_key(0, sample)], W=[f"ps{Yb}"])
            if nb > 1:
                self.multi_state_update("ssd", 0, h, n, s, nb, bend[:, h, :], xdtv[:, h, :],
                                        self.EB[base:base + 64, 0, h * nb:h * nb + nb], ["EB"], ["big0"], ["big0"],
                                        [(self.big[0][:, 512:1024], "big0m")])
            else:
                bu = self.psa()
                mm(ps[bu][base:base + 64, 0:64], bend[:, h, :], xdtv[:, h, :], R=["big0"], W=[f"ps{bu}"])
                stp = self.state_ap("ssd", 0, h, 0, sample)
                op("dve", "scalar_tensor_tensor", stp, stp, self.EB[base:base + 64, 0, h * nb:h * nb + 1],
                   ps[bu][base:base + 64, 0:64], ALU.mult, ALU.add,
                   R=[self.state_key(0, sample), "EB", f"ps{bu}"], W=[self.state_key(0, sample)])
                self.psf(bu)
        for hb in range(2):
            y = scr[13]
            op("dve", "scalar_tensor_tensor", y[:, 0:n], CV[:, hb, 0:n], self.cp[:, CP_DSSD + hb:CP_DSSD + hb + 1],
               ps[Yb][:, 128 * hb:128 * hb + n], ALU.mult, ALU.add, R=["CVa", "cp", f"ps{Yb}"], W=["scr13"])
            self.rms_gate(y[:, 0:n], "scr13", n, CP_GSSD + hb, FM_ZA + hb, 0 + hb)
        self.psf(Yb)
        if sample:
            self.store_sst(l, 0, 0)

    def multi_state_update(self, mixer, m, h, n, s, nb, src_tm, rhs_tm, decay, dkeys, skeys, rkeys, mtiles):
        op, mm, ps = self.op, self.mm, self.ps
        base, slot = self.state_geo(mixer, h)
        skey = self.state_key(m, True)
        S = self.SST[self.sstbuf[m]]
        for gi, g0 in enumerate(range(0, nb, 8)):
            mt, mkey = mtiles[gi % len(mtiles)]
            mk = mt[0:n, 0:512].rearrange("p (b d) -> p b d", b=8)
            op("dve", "tensor_tensor", mk, src_tm.unsqueeze(1).to_broadcast([n, 8, 64]),
               self.cst[0:n, C_BLK(s), g0:g0 + 8].unsqueeze(2).to_broadcast([n, 8, 64]), ALU.mult,
               R=list(skeys) + ["cst"], W=[mkey])
            bu = self.psa()
            for q in range(8):
                mm(ps[bu][base:base + 64, 64 * q:64 * q + 64], mt[0:n, 64 * q:64 * q + 64], rhs_tm,
                   R=[mkey] + list(rkeys), W=[f"ps{bu}"])
            Sv = S[base:base + 64, slot, g0:g0 + 8, :]
            op("dve", "tensor_tensor", Sv, Sv, decay[:, g0:g0 + 8].unsqueeze(2).to_broadcast([64, 8, 64]), ALU.mult,
               R=[skey] + list(dkeys), W=[skey])
            op("dve", "tensor_tensor", Sv, Sv, ps[bu][base:base + 64, :].rearrange("p (b d) -> p b d", b=8), ALU.add,
               R=[skey, f"ps{bu}"], W=[skey])
            self.psf(bu)

    def state_geo(self, mixer, h):
        if mixer == "ssd":
            return 64 * (h // 2), h % 2
        return 64 * (h % 2), h // 2

    def state_ap(self, mixer, m, h, bi, sample):
        base, slot = self.state_geo(mixer, h)
        if sample:
            return self.SST[self.sstbuf[m]][base:base + 64, slot, bi, :]
        return self.PST[m][base:base + 64, slot, :]

    def state_key(self, m, sample):
        if sample:
            return f"SST{self.sstbuf[m]}"
        return f"PST{m}"

    sstbuf = {0: 0, 1: 0, 2: 0}

    def load_sst(self, l, m, _):
        buf = self.sstbuf[m]
        mixer = ("ssd", "gdn", "hg")[m]
        for h in range(4):
            base, slot = self.state_geo(mixer, h)
            self.P.dma(self.SST[buf][base:base + 64, slot, :, :],
                       self.sstd[m].ap()[l, :, h, :, :].rearrange("s k v -> k s v"),
                       f"SST{buf}", writes=[f"SST{buf}"] + self.alias_keys)

    def store_sst(self, l, m, _):
        buf = self.sstbuf[m]
        self.P.dma(self.o_sst.ap()[l, m], self.SST[buf][:], f"SST{buf}", reads=[f"SST{buf}"] + self.alias_keys)

    def gdn_prep(self, kind, n):
        op, mm, act = self.op, self.mm, self.act
        ps, scr = self.ps, self.scr
        CV = self.CV
        QN, KN = [scr[14], scr[15]], [scr[16], scr[17]]
        for i, (cb, dst, dk_, sc) in enumerate(((4, QN[0], "scr14", 0.125), (5, QN[1], "scr15", 0.125),
                                                (6, KN[0], "scr16", 1.0), (7, KN[1], "scr17", 1.0))):
            sq = scr[0]
            act(sq[:, 0:n], CV[:, cb, 0:n], AF.Square, R=["CVb"], W=["scr0"])
            b = self.psa()
            mm(ps[b][:, 0:n], self.C(C_ONES64), sq[:, 0:n], R=["cst", "scr0"], W=[f"ps{b}"])
            rs = scr[1]
            act(rs[:, 0:n], ps[b][:, 0:n], AF.Ln, bias=self.epsc[:, 1:2], R=[f"ps{b}", "epsc"], W=["scr1"])
            self.psf(b)
            act(rs[:, 0:n], rs[:, 0:n], AF.Exp, scale=-0.5, R=["scr1"], W=["scr1"])
            op("dve", "scalar_tensor_tensor", dst[:, 0:n], CV[:, cb, 0:n], sc, rs[:, 0:n], ALU.mult, ALU.mult,
               R=["CVb", "scr1"], W=[dk_])

    def gdn_tile(self, l, kind, n):
        op, mm, act, tr = self.op, self.mm, self.act, self.tr
        ps, scr = self.ps, self.scr
        s, lb, nb, rounds = self.struct(kind, "gdn")
        nlev = {4: 2, 16: 4, 64: 6}[lb]
        sample = (kind == "sample")
        m = 1
        if sample:
            self.load_sst(l, m, 0)
        skey = self.state_key(m, sample)
        CV, SM, SM2 = self.CV, self.SM, self.SM2
        QN, KN = [scr[14], scr[15]], [scr[16], scr[17]]
        b = self.psa()
        for j in range(2):
            tr(ps[b][0:n, 128 * j:128 * j + 128], CV[:, 8 + j, 0:n], self.C(C_ID), R=["CVb", "cst"], W=[f"ps{b}"])
            tr(ps[b][0:n, 256 + 128 * j:256 + 128 * j + 128], KN[j][:, 0:n], self.C(C_ID), R=[f"scr{16 + j}", "cst"], W=[f"ps{b}"])
        big0 = self.big[0]
        X1 = big0[0:n, 0:256]
        X2 = big0[0:n, 256:512]
        KEB = big0[0:n, 512:768]
        op("act", "copy", X1, ps[b][0:n, 0:256], R=[f"ps{b}"], W=["big0"])
        ktm = ps[b][0:n, 256:512].rearrange("p (h d) -> p h d", h=4)
        op("dve", "tensor_tensor", X2.rearrange("p (h d) -> p h d", h=4), ktm,
           SM2[0:n, 24:28].unsqueeze(2).to_broadcast([n, 4, 64]), ALU.mult, R=[f"ps{b}", "SM2"], W=["big0"])
        op("dve", "tensor_tensor", KEB.rearrange("p (h d) -> p h d", h=4), ktm,
           SM2[0:n, 32:36].unsqueeze(2).to_broadcast([n, 4, 64]), ALU.mult, R=[f"ps{b}", "SM2"], W=["big0"])
        self.psf(b)
        WN = [scr[22], scr[23]]
        QG = [scr[26], scr[27]]
        WB = self.WB
        wb = [WB[:, 512 * i:512 * i + 512] for i in range(4)]
        wv = lambda t: t[0:n, :].rearrange("p (h t) -> p h t", h=4)[:, :, 0:n]
        wh = lambda t, h: t[0:n, 128 * h:128 * h + n]
        pv = lambda b: ps[b][0:n, :].rearrange("p (h t) -> p h t", h=4)[:, :, 0:n]
        bc4 = lambda ap: ap.unsqueeze(1).to_broadcast([n, 4, n])
        col4 = lambda ap: ap.unsqueeze(2).to_broadcast([n, 4, n])
        AQW, RW = self.AQW, self.RW
        frn = self.use_fr and n == 128
        fo = (lambda ap: ap.bitcast(mybir.dt.float32r)) if frn else (lambda ap: ap)
        g4 = SM[0:n, 12:16]
        ng4 = SM2[0:n, 36:40]
        op("dve", "tensor_scalar", ng4, g4, -1.0, None, ALU.mult, R=["SM"], W=["SM2"])
        op("dve", "tensor_tensor", wv(wb[0]), bc4(self.cst[0:n, C_ONES, 0:n]), col4(g4), ALU.mult, R=["cst", "SM"], W=["wb0"])
        op("dve", "tensor_tensor", wv(wb[1]), bc4(self.cst[0:n, C_TRI(s), 0:n]), col4(ng4), ALU.mult, R=["cst", "SM2"], W=["wb1"])
        bA, bB, bC, bD = self.psa(), self.psa(), self.psa(), self.psa()
        ONESn = self.cst[0:n, C_ONES, 0:n]
        IDn = self.cst[0:n, C_ID, 0:n]
        for h in (0, 2, 1, 3):
            base, hb = 64 * (h % 2), h // 2
            kn = KN[hb][base:base + 64, 0:n]
            qn = QN[hb][base:base + 64, 0:n]
            mm(ps[bA][0:n, 128 * h:128 * h + n], kn, kn, R=[f"scr{16 + hb}"], W=[f"ps{bA}"])
            mm(ps[bB][0:n, 128 * h:128 * h + n], kn, qn, R=[f"scr{16 + hb}", f"scr{14 + hb}"], W=[f"ps{bB}"])
        def rmask(bank, mask):
            for h in range(4):
                o = ps[bank][0:n, 128 * h:128 * h + n]
                mm(o, wh(wb[0], h), self.C(C_TRI(s), n, n), start=True, stop=False, R=["wb0", "cst"], W=[f"ps{bank}"])
                mm(o, wh(wb[1], h), ONESn, start=False, stop=False, R=["wb1", "cst"], W=[f"ps{bank}"])
                mm(o, IDn, self.C(mask, n, n), start=False, stop=True, R=["cst"], W=[f"ps{bank}"])
        rmask(bC, C_NBS(s))
        for h in range(4):
            base, hb = 64 * (h % 2), h // 2
            mm(ps[bD][base:base + 64, 128 * hb:128 * hb + n], wb[0][0:n, 128 * h:128 * h + 64], self.C(C_TRI(s), n, n),
               R=["wb0", "cst"], W=[f"ps{bD}"])
        act(wv(wb[3]), pv(bC), AF.Exp, R=[f"ps{bC}"], W=["wb3"])
        for hb in range(2):
            act(QG[hb][:, 0:n], ps[bD][:, 128 * hb:128 * hb + n], AF.Exp, R=[f"ps{bD}"], W=[f"scr{26 + hb}"])
            op("pool", "tensor_tensor", QG[hb][:, 0:n], QG[hb][:, 0:n], QN[hb][:, 0:n], ALU.mult,
               R=[f"scr{26 + hb}", f"scr{14 + hb}"], W=[f"scr{26 + hb}"])
        op("pool", "tensor_tensor", wv(wb[2]), wv(wb[3]), bc4(IDn), ALU.add, R=["wb3", "cst"], W=["wb2"])
        op("dve", "tensor_tensor", wv(wb[0]), pv(bA), col4(SM2[0:n, 28:32]), ALU.mult, R=[f"ps{bA}", "SM2"], W=["wb0"])
        op("dve", "tensor_tensor", fo(wv(wb[0])), wv(wb[0]), wv(wb[3]), ALU.mult, R=["wb0", "wb3"], W=["wb0"])
        op("dve", "tensor_tensor", wv(AQW), pv(bB), col4(SM[0:n, 8:12]), ALU.mult, R=[f"ps{bB}", "SM"], W=["AQW"])
        op("pool", "tensor_tensor", wv(AQW), wv(AQW), wv(wb[2]), ALU.mult, R=["AQW", "wb2"], W=["AQW"])
        for h in range(4):
            tr(ps[bD][0:n, 128 * h:128 * h + n], wh(wb[0], h), IDn, R=["wb0", "cst"], W=[f"ps{bD}"])
        NH, RWh = self.NH, self.RWh
        op("act", "copy", wv(NH[1]), pv(bD), R=[f"ps{bD}"], W=["NH1"])
        op("pool", "tensor_copy", wv(NH[0]), wv(wb[0]), R=["wb0"], W=["NH0"])
        op("dve", "tensor_tensor", wv(RW), bc4(IDn), wv(wb[0]), ALU.add, R=["cst", "wb0"], W=["RW"])
        op("pool", "tensor_copy", wv(RWh), wv(RW), R=["RW"], W=["RWh"])
        cur = (0, 1)
        for k in range(1, nlev):
            nxt = (2, 3) if cur == (0, 1) else (0, 1)
            NTc, NNc, NTn, NNn = NH[cur[0]], NH[cur[1]], NH[nxt[0]], NH[nxt[1]]
            kc = [f"NH{cur[0]}", f"NH{cur[1]}"]
            for h in range(4):
                mm(ps[bA][0:n, 128 * h:128 * h + n], wh(NTc, h), wh(NNc, h), R=kc, W=[f"ps{bA}"])
            if k < nlev - 1:
                for h in range(4):
                    mm(ps[bB][0:n, 128 * h:128 * h + n], wh(NNc, h), wh(NTc, h), R=kc, W=[f"ps{bB}"])
            op("act", "copy", wv(NNn), pv(bA), R=[f"ps{bA}"], W=[f"NH{nxt[1]}"])
            if k < nlev - 1:
                op("dve", "tensor_copy", wv(NTn), pv(bB), R=[f"ps{bB}"], W=[f"NH{nxt[0]}"])
            for h in range(4):
                mm(ps[bC][0:n, 128 * h:128 * h + n], wh(NNn, h), wh(RWh, h), R=[f"NH{nxt[1]}", "RWh"], W=[f"ps{bC}"])
            op("dve", "tensor_tensor", wv(RW), wv(RW), pv(bC), ALU.add, R=["RW", f"ps{bC}"], W=["RW"])
            if k < nlev - 1:
                op("pool", "tensor_copy", wv(RWh), wv(RW), R=["RW"], W=["RWh"])
            cur = nxt
        for h in range(4):
            base, hb = 64 * (h % 2), h // 2
            mm(ps[bD][base:base + 64, 128 * hb:128 * hb + n], X2[:, 64 * h:64 * h + 64], wh(RW, h), R=["big0", "RW"], W=[f"ps{bD}"])
        for hb in range(2):
            act(WN[hb][:, 0:n], ps[bD][:, 128 * hb:128 * hb + n], AF.Copy, scale=-1.0, R=[f"ps{bD}"], W=[f"scr{22 + hb}"])
        for b_ in (bA, bB, bC, bD):
            self.psf(b_)
        Vb, Ob = self.psa(), self.psa()
        for rnd in rounds:
            r0, r1 = rnd[0] * lb, (rnd[-1] + 1) * lb
            if len(rnd) == 1:
                bi = rnd[0]
                VT4 = self.VT4
                for h in range(4):
                    base, hb = 64 * (h % 2), h // 2
                    vo = ps[Vb][r0:r1, 64 * h:64 * h + 64]
                    mm(vo, self.RW[r0:r1, 128 * h + r0:128 * h + r1], big0[r0:r1, 64 * h:64 * h + 64], start=True, stop=False,
                       R=["RW", "big0"], W=[f"ps{Vb}"])
                    mm(vo, WN[hb][base:base + 64, r0:r1], self.state_ap("gdn", m, h, bi, sample), start=False, stop=True,
                       R=[f"scr{22 + hb}", skey], W=[f"ps{Vb}"])
                op("act", "copy", VT4[r0:r1, :], ps[Vb][r0:r1, 0:256], R=[f"ps{Vb}"], W=["VT4"])
                bu = self.psa()
                for h in range(4):
                    base, hb = 64 * (h % 2), h // 2
                    oo = ps[Ob][base:base + 64, 128 * hb + r0:128 * hb + r1]
                    mm(oo, VT4[r0:r1, 64 * h:64 * h + 64], self.AQW[r0:r1, 128 * h + r0:128 * h + r1], start=True, stop=False,
                       R=["VT4", "AQW"], W=[f"ps{Ob}"])
                    mm(oo, self.state_ap("gdn", m, h, bi, sample), QG[hb][base:base + 64, r0:r1], start=False, stop=True,
                       R=[skey, f"scr{26 + hb}"], W=[f"ps{Ob}"])
                for h in range(4):
                    base, hb = 64 * (h % 2), h // 2
                    mm(ps[bu][base:base + 64, 64 * hb:64 * hb + 64], big0[r0:r1, 512 + 64 * h:512 + 64 * h + 64],
                       VT4[r0:r1, 64 * h:64 * h + 64], R=["big0", "VT4"], W=[f"ps{bu}"])
                for h in range(4):
                    base, hb = 64 * (h % 2), h // 2
                    stp = self.state_ap("gdn", m, h, bi, sample)
                    op("dve", "scalar_tensor_tensor", stp, stp, self.EB[base:base + 64, 1, h * nb + bi:h * nb + bi + 1],
                       ps[bu][base:base + 64, 64 * hb:64 * hb + 64], ALU.mult, ALU.add, R=[skey, "EB", f"ps{bu}"], W=[skey])
                self.psf(bu)
                continue
            for h in range(4):
                base, hb = 64 * (h % 2), h // 2
                Rm, rk = self.RW[:, 128 * h:128 * h + 128], "RW"
                vv = ps[Vb][base:base + 64, 128 * hb:128 * hb + n]
                mm(vv[:, r0:r1], big0[r0:r1, 64 * h:64 * h + 64], Rm[r0:r1, r0:r1], start=True, stop=False,
                   R=["big0", rk], W=[f"ps{Vb}"])
                for bi in rnd:
                    c0_ = bi * lb
                    mm(vv[:, c0_:c0_ + lb], self.state_ap("gdn", m, h, bi, sample), WN[hb][base:base + 64, c0_:c0_ + lb],
                       start=False, stop=(bi == rnd[-1]), R=[skey, f"scr{22 + hb}"], W=[f"ps{Vb}"])
                VR = scr[35]
                op("act", "copy", VR[base:base + 64, r0:r1], vv[:, r0:r1], R=[f"ps{Vb}"], W=["scr35"])
                bT = self.psa()
                mm(ps[bT][r0:r1, 0:64], VR[base:base + 64, r0:r1], self.cst[base:base + 64, C_ID, base:base + 64],
                   R=["scr35", "cst"], W=[f"ps{bT}"])
                VT = scr[36]
                op("dve", "tensor_copy", VT[r0:r1, 0:64], ps[bT][r0:r1, 0:64], R=[f"ps{bT}"], W=["scr36"])
                self.psf(bT)
                oo = ps[Ob][base:base + 64, 128 * hb:128 * hb + n]
                mm(oo[:, r0:r1], VT[r0:r1, 0:64], self.AQW[r0:r1, 128 * h + r0:128 * h + r1], start=True, stop=False, R=["scr36", "AQW"],
                   W=[f"ps{Ob}"])
                for bi in rnd:
                    c0_ = bi * lb
                    mm(oo[:, c0_:c0_ + lb], self.state_ap("gdn", m, h, bi, sample), QG[hb][base:base + 64, c0_:c0_ + lb],
                       start=False, stop=(bi == rnd[-1]), R=[skey, f"scr{26 + hb}"], W=[f"ps{Ob}"])
                self.multi_state_update("gdn", m, h, n, s, nb, big0[0:n, 512 + 64 * h:512 + 64 * h + 64], VT[0:n, 0:64],
                                        self.EB[base:base + 64, 1, h * nb:h * nb + nb], ["EB"], ["big0"], ["scr36"],
                                        [(self.WB[:, 1024:1536], "wb2"), (self.WB[:, 1536:2048], "wb3")])
        self.psf(Vb)
        for hb in range(2):
            self.rms_gate(ps[Ob][:, 128 * hb:128 * hb + n], f"ps{Ob}", n, CP_GGDN + hb, FM_ZC + hb, 4 + hb)
        self.psf(Ob)
        if sample:
            self.store_sst(l, m, 0)

    def hg_tile(self, l, kind, n):
        op, mm, act, tr = self.op, self.mm, self.act, self.tr
        ps, scr = self.ps, self.scr
        s, lb, nb, rounds = self.struct(kind, "hg")
        sample = (kind == "sample")
        m = 2
        if sample:
            self.load_sst(l, m, 0)
        skey = self.state_key(m, sample)
        LF, KK, GC, QG, KG, KE, TP = ([scr[a], scr[a + 1]] for a in (14, 16, 18, 20, 22, 24, 26))
        for hb in range(2):
            c0, c1, c2 = (self.hgc[:, hb, j:j + 1] for j in range(3))
            th = self.TH[:, hb, 0:n]
            act(LF[hb][:, 0:n], th, AF.Ln, scale=c0, bias=c1, R=["TH", "hgc"], W=[f"scr{14 + hb}"])
            op("dve", "tensor_scalar", KK[hb][:, 0:n], th, c2, c0, ALU.mult, ALU.add, R=["TH", "hgc"], W=[f"scr{16 + hb}"])
            op("dve", "tensor_tensor_scan", GC[hb][:, 0:n], self.cst[:, C_RST(s), 0:n], LF[hb][:, 0:n], 0.0, ALU.mult, ALU.add,
               R=["cst", f"scr{14 + hb}"], W=[f"scr{18 + hb}"])
            act(TP[hb][:, 0:n], GC[hb][:, 0:n], AF.Exp, R=[f"scr{18 + hb}"], W=[f"scr{26 + hb}"])
            op("dve", "tensor_tensor", QG[hb][:, 0:n], self.QD[:, hb, 0:n], TP[hb][:, 0:n], ALU.mult,
               R=["QD", f"scr{26 + hb}"], W=[f"scr{20 + hb}"])
            gcv = GC[hb][:, 0:n].rearrange("p (b t) -> p b t", t=lb)
            tpv = TP[hb][:, 0:n].rearrange("p (b t) -> p b t", t=lb)
            op("dve", "tensor_copy", LF[hb][:, 0:nb].unsqueeze(2), tpv[:, :, lb - 1:lb], R=[f"scr{26 + hb}"], W=[f"scr{14 + hb}"])
            act(TP[hb][:, 0:n], GC[hb][:, 0:n], AF.Exp, scale=-1.0, R=[f"scr{18 + hb}"], W=[f"scr{26 + hb}"])
            op("dve", "tensor_tensor", KG[hb][:, 0:n], KK[hb][:, 0:n], TP[hb][:, 0:n], ALU.mult,
               R=[f"scr{16 + hb}", f"scr{26 + hb}"], W=[f"scr{22 + hb}"])
            op("dve", "tensor_tensor", tpv, gcv[:, :, lb - 1:lb].to_broadcast([128, nb, lb]), gcv, ALU.subtract,
               R=[f"scr{18 + hb}"], W=[f"scr{26 + hb}"])
            act(TP[hb][:, 0:n], TP[hb][:, 0:n], AF.Exp, R=[f"scr{26 + hb}"], W=[f"scr{26 + hb}"])
            op("dve", "tensor_tensor", KE[hb][:, 0:n], KK[hb][:, 0:n], TP[hb][:, 0:n], ALU.mult,
               R=[f"scr{16 + hb}", f"scr{26 + hb}"], W=[f"scr{24 + hb}"])
        b = self.psa()
        for hb in range(2):
            tr(ps[b][0:n, 128 * hb:128 * hb + 128], KE[hb][:, 0:n], self.C(C_ID), R=[f"scr{24 + hb}", "cst"], W=[f"ps{b}"])
        KT = self.big[0][0:n, 512:768]
        op("act", "copy", KT, ps[b][0:n, 0:256], R=[f"ps{b}"], W=["big0"])
        self.psf(b)
        b = self.psa()
        for h in range(4):
            base, hb = 64 * (h % 2), h // 2
            mm(ps[b][0:n, 128 * h:128 * h + n], KG[hb][base:base + 64, 0:n], QG[hb][base:base + 64, 0:n],
               R=[f"scr{22 + hb}", f"scr{20 + hb}"], W=[f"ps{b}"])
        AT = self.big[0][0:n, 0:512].rearrange("p (h t) -> p h t", h=4)
        op("dve", "tensor_tensor", AT[:, :, 0:n], ps[b][0:n, :].rearrange("p (h t) -> p h t", h=4)[:, :, 0:n],
           self.cst[0:n, C_TRI(s), 0:n].unsqueeze(1).to_broadcast([n, 4, n]), ALU.mult, R=[f"ps{b}", "cst"], W=["big0"])
        self.psf(b)
        ob = [self.psa(), self.psa()]
        for rnd in rounds:
            r0, r1 = rnd[0] * lb, (rnd[-1] + 1) * lb
            if len(rnd) == 1:
                bi = rnd[0]
                for h in range(4):
                    base, hb = 64 * (h % 2), h // 2
                    vt = self.TMs[r0:r1, 12 + 64 * h:12 + 64 * h + 64]
                    oo = ps[ob[hb]][base:base + 64, r0:r1]
                    mm(oo, vt, self.big[0][r0:r1, 128 * h + r0:128 * h + r1], start=True, stop=False,
                       R=["TMs", "big0"], W=[f"ps{ob[hb]}"])
                    mm(oo, self.state_ap("hg", m, h, bi, sample), QG[hb][base:base + 64, r0:r1], start=False, stop=True,
                       R=[skey, f"scr{20 + hb}"], W=[f"ps{ob[hb]}"])
                bu = self.psa()
                for h in range(4):
                    base, hb = 64 * (h % 2), h // 2
                    vt = self.TMs[r0:r1, 12 + 64 * h:12 + 64 * h + 64]
                    mm(ps[bu][base:base + 64, 64 * hb:64 * hb + 64], self.big[0][r0:r1, 512 + 64 * h:512 + 64 * h + 64], vt,
                       R=["big0", "TMs"], W=[f"ps{bu}"])
                for h in range(4):
                    base, hb = 64 * (h % 2), h // 2
                    stp = self.state_ap("hg", m, h, bi, sample)
                    op("dve", "scalar_tensor_tensor", stp, stp, LF[hb][base:base + 64, bi:bi + 1],
                       ps[bu][base:base + 64, 64 * hb:64 * hb + 64], ALU.mult, ALU.add,
                       R=[skey, f"scr{14 + hb}", f"ps{bu}"], W=[skey])
                self.psf(bu)
                continue
            for h in range(4):
                base, hb = 64 * (h % 2), h // 2
                vt = self.TMs[0:n, 12 + 64 * h:12 + 64 * h + 64]
                oo = ps[ob[hb]][base:base + 64, 0:n]
                mm(oo[:, r0:r1], vt, AT[:, h, r0:r1], start=True, stop=False, R=["TMs", "big0"], W=[f"ps{ob[hb]}"])
                for bi in rnd:
                    c0_ = bi * lb
                    mm(oo[:, c0_:c0_ + lb], self.state_ap("hg", m, h, bi, sample), QG[hb][base:base + 64, c0_:c0_ + lb],
                       start=False, stop=(bi == rnd[-1]), R=[skey, f"scr{20 + hb}"], W=[f"ps{ob[hb]}"])
                self.multi_state_update("hg", m, h, n, s, nb, KT[:, 64 * h:64 * h + 64], vt,
                                        LF[hb][base:base + 64, 0:nb], [f"scr{14 + hb}"], ["big0"], ["TMs"],
                                        [(self.WB[:, 1024:1536], "wb2"), (self.WB[:, 1536:2048], "wb3")])
        for hb in range(2):
            self.rms_gate(ps[ob[hb]][:, 0:n], f"ps{ob[hb]}", n, CP_GHG + hb, FM_ZD + hb, 6 + hb)
            self.psf(ob[hb])
        if sample:
            self.store_sst(l, m, 0)

    def _poly(self, out, x, coefs, key):
        op = self.op
        op("dve", "memset", out, coefs[-1], W=[key])
        for c in reversed(coefs[:-1]):
            op("dve", "tensor_tensor", out, out, x, ALU.mult, R=[key], W=[key])
            op("dve", "tensor_scalar", out, out, float(c), None, ALU.add, R=[key], W=[key])

    def _cmul_tab(self, dre, dim, sre, sim, pr, pi, m, key, eng="dve", tmp=None, tkey="big0", pkey="s5t"):
        op = self.op
        tmp = self.big[0][:, 0:512] if tmp is None else tmp
        t = tmp[:, 0:8 * m].rearrange("p (g t) -> p g t", g=8)
        prb = pr.unsqueeze(2).to_broadcast([128, 8, m])
        pib = pi.unsqueeze(2).to_broadcast([128, 8, m])
        R = [key, pkey]
        op(eng, "tensor_tensor", t, sim, pib, ALU.mult, R=R, W=[tkey])
        op(eng, "tensor_tensor", dre, sre, prb, ALU.mult, R=R, W=[key])
        op(eng, "tensor_tensor", dre, dre, t, ALU.subtract, R=[key, tkey], W=[key])
        op(eng, "tensor_tensor", t, sim, prb, ALU.mult, R=R, W=[tkey])
        op(eng, "tensor_tensor", dim, sre, pib, ALU.mult, R=R, W=[key])
        op(eng, "tensor_tensor", dim, dim, t, ALU.add, R=[key, tkey], W=[key])

    def _pow_table(self, tab, key, bre, bim, count, ii=2, neg=True, eng="dve", T=None, r0=8, pkey="s5t", tmp=None, tkey="big0"):
        op = self.op
        T = self.s5t if T is None else T
        op(eng, "memset", tab[:, :, 0, 0:1], 1.0, W=[key])
        op(eng, "memset", tab[:, :, ii, 0:1], 0.0, W=[key])
        pr, pi = T[:, r0, :], T[:, r0 + 1, :]
        op(eng, "tensor_copy", pr, bre, R=["s5t"], W=[pkey])
        op(eng, "tensor_copy", pi, bim, R=["s5t"], W=[pkey])
        m = 1
        while m < count:
            w = min(m, count - m)
            self._cmul_tab(tab[:, :, 0, m:m + w], tab[:, :, ii, m:m + w], tab[:, :, 0, 0:w], tab[:, :, ii, 0:w], pr, pi, w, key, eng=eng, tmp=tmp, tkey=tkey, pkey=pkey)
            m *= 2
            if m < count:
                a, b2 = T[:, r0 + 2, :], T[:, r0 + 3, :]
                op(eng, "tensor_tensor", a, pr, pr, ALU.mult, R=[pkey], W=[pkey])
                op(eng, "tensor_tensor", b2, pi, pi, ALU.mult, R=[pkey], W=[pkey])
                op(eng, "tensor_tensor", a, a, b2, ALU.subtract, R=[pkey], W=[pkey])
                op(eng, "tensor_tensor", b2, pr, pi, ALU.mult, R=[pkey], W=[pkey])
                op(eng, "tensor_scalar", pi, b2, 2.0, None, ALU.mult, R=[pkey], W=[pkey])
                op(eng, "tensor_copy", pr, a, R=[pkey], W=[pkey])
        if neg:
            op(eng, "tensor_scalar", tab[:, :, 1, :], tab[:, :, 2, :], -1.0, None, ALU.mult, R=[key], W=[key])

    def s5_prepare(self, l):
        op = self.op
        T = self.s5t
        fact = lambda k: float(math.factorial(k))
        lre, lim, ldt = self.s5p[:, 0, :], self.s5p[:, 1, :], self.s5p[:, 2, :]
        K = "s5t"
        R = ["s5p", K]
        x = T[:, 0, :]
        op("dve", "tensor_scalar", x, ldt, 1.0 / 16.0, None, ALU.mult, R=R, W=[K])
        dt = T[:, 1, :]
        self._poly(dt, x, [1.0 / fact(k) for k in range(11)], K)
        for _ in range(4):
            op("dve", "tensor_tensor", dt, dt, dt, ALU.mult, R=[K], W=[K])
        op("dve", "tensor_tensor", x, lre, dt, ALU.mult, R=R, W=[K])
        mag = T[:, 2, :]
        self._poly(mag, x, [1.0 / fact(k) for k in range(7)], K)
        ang = T[:, 3, :]
        op("dve", "tensor_tensor", ang, lim, dt, ALU.mult, R=R, W=[K])
        kf = T[:, 4, :]
        op("dve", "tensor_scalar", kf, ang, 1.0 / (2.0 * math.pi), None, ALU.mult, R=[K], W=[K])
        op("dve", "tensor_copy", self.s5i[:], kf, R=[K], W=["s5i"])
        op("dve", "tensor_copy", kf, self.s5i[:], R=["s5i"], W=[K])
        C1 = 6.28125
        C2 = 2.0 * math.pi - C1
        op("dve", "scalar_tensor_tensor", ang, kf, -C1, ang, ALU.mult, ALU.add, R=[K], W=[K])
        op("dve", "scalar_tensor_tensor", ang, kf, -C2, ang, ALU.mult, ALU.add, R=[K], W=[K])
        op("dve", "tensor_scalar", ang, ang, 0.125, None, ALU.mult, R=[K], W=[K])
        y = T[:, 4, :]
        op("dve", "tensor_tensor", y, ang, ang, ALU.mult, R=[K], W=[K])
        sn, cs = T[:, 5, :], T[:, 6, :]
        self._poly(sn, y, [(-1.0) ** k / fact(2 * k + 1) for k in range(6)], K)
        op("dve", "tensor_tensor", sn, sn, ang, ALU.mult, R=[K], W=[K])
        self._poly(cs, y, [(-1.0) ** k / fact(2 * k) for k in range(7)], K)
        tmp = T[:, 7, :]
        for _ in range(3):
            op("dve", "tensor_tensor", tmp, sn, sn, ALU.mult, R=[K], W=[K])
            op("dve", "tensor_tensor", sn, sn, cs, ALU.mult, R=[K], W=[K])
            op("dve", "tensor_scalar", sn, sn, 2.0, None, ALU.mult, R=[K], W=[K])
            op("dve", "tensor_scalar", cs, tmp, -2.0, 1.0, ALU.mult, ALU.add, R=[K], W=[K])
        op("dve", "tensor_tensor", sn, sn, mag, ALU.mult, R=[K], W=[K])
        op("dve", "tensor_tensor", cs, cs, mag, ALU.mult, R=[K], W=[K])
        lbr, lbi = T[:, 6, :], T[:, 5, :]
        m2 = T[:, 0, :]
        a = T[:, 1, :]
        op("dve", "tensor_tensor", m2, lbr, lbr, ALU.mult, R=[K], W=[K])
        op("dve", "tensor_tensor", a, lbi, lbi, ALU.mult, R=[K], W=[K])
        op("dve", "tensor_tensor", m2, m2, a, ALU.add, R=[K], W=[K])
        op("dve", "reciprocal", m2, m2, R=[K], W=[K])
        ivr, ivi = T[:, 2, :], T[:, 3, :]
        op("dve", "tensor_tensor", ivr, lbr, m2, ALU.mult, R=[K], W=[K])
        op("dve", "tensor_tensor", ivi, lbi, m2, ALU.mult, R=[K], W=[K])
        op("dve", "tensor_scalar", ivi, ivi, -1.0, None, ALU.mult, R=[K], W=[K])
        den, b2 = T[:, 0, :], T[:, 1, :]
        nr = T[:, 4, :]
        op("dve", "tensor_scalar", nr, lbr, -1.0, None, ALU.add, R=[K], W=[K])
        cre, cim = T[:, 7, :], T[:, 4, :]
        t1, t2 = self.scr[36][:, 0:8], self.scr[36][:, 8:16]
        SK = "scr36"
        op("dve", "tensor_tensor", t1, nr, lre, ALU.mult, R=R, W=[SK])
        op("dve", "tensor_tensor", t2, lbi, lim, ALU.mult, R=R, W=[SK])
        op("dve", "tensor_tensor", cre, t1, t2, ALU.add, R=[SK], W=[K])
        op("dve", "tensor_tensor", t1, lbi, lre, ALU.mult, R=R, W=[SK])
        op("dve", "tensor_tensor", t2, nr, lim, ALU.mult, R=R, W=[SK])
        op("dve", "tensor_tensor", cim, t1, t2, ALU.subtract, R=[SK], W=[K])
        op("dve", "tensor_tensor", den, lre, lre, ALU.mult, R=R, W=[K])
        op("dve", "tensor_tensor", b2, lim, lim, ALU.mult, R=R, W=[K])
        op("dve", "tensor_tensor", den, den, b2, ALU.add, R=[K], W=[K])
        op("dve", "reciprocal", den, den, R=[K], W=[K])
        op("dve", "tensor_tensor", cre, cre, den, ALU.mult, R=[K], W=[K])
        op("dve", "tensor_tensor", cim, cim, den, ALU.mult, R=[K], W=[K])
        KI2 = self.big[1][:, 0:1024].rearrange("p (g c t) -> p g c t", g=8, c=2)
        self._pow_table(KI2, "wb2", ivr, ivi, 64, ii=1, neg=False, eng="pool", T=self.s5u, r0=0, pkey="GM",
                        tmp=self.big[0][:, 512:1024], tkey="big0m")
        self._pow_table(self.KO, "KO", lbr, lbi, 65)
        op("dve", "tensor_copy", self.KI[:, :, 0, :], KI2[:, :, 0, :], R=["wb2", "wb3"], W=["KI"])
        op("dve", "tensor_copy", self.KI[:, :, 2, :], KI2[:, :, 1, :], R=["wb2", "wb3"], W=["KI"])
        self._cmul_tab(KI2[:, :, 0, :], KI2[:, :, 1, :], self.KI[:, :, 0, :], self.KI[:, :, 2, :], cre, cim, 64, "KI")
        op("dve", "tensor_copy", self.KI[:, :, 0, :], KI2[:, :, 0, :], R=["wb2", "wb3", "KI"], W=["KI"])
        op("dve", "tensor_copy", self.KI[:, :, 2, :], KI2[:, :, 1, :], R=["wb2", "wb3", "KI"], W=["KI"])
        op("dve", "tensor_scalar", self.KI[:, :, 1, :], self.KI[:, :, 2, :], -1.0, None, ALU.mult, R=["KI"], W=["KI"])
        op("dve", "tensor_scalar", self.Cl[:, :, 1, :], self.Cl[:, :, 1, :], -1.0, None, ALU.mult, R=["Cl"], W=["Cl"])
        op("dve", "tensor_scalar", self.hglb[:], self.cp[:, CP_GLUB:CP_GLUB + 2], 0.5, None, ALU.mult, R=["cp"], W=["hglb"])

    def s5_tile(self, l, kind, n):
        op, mm, act = self.op, self.mm, self.act
        ps = self.ps
        sample = (kind == "sample")
        if sample:
            subs = [(0, 64)]
            rst = self.cst[:, C_RST(2), 0:64]
        elif kind == "meta":
            subs = [(0, 16)]
            rst = self.cst[:, C_RST(0), 0:16]
        else:
            subs = [(0, 64), (64, 64)]
            rst = self.cst[:, C_RST(1), 0:64]
        Y5 = self.s5w[7]
        KI, KO = self.KI, self.KO
        ybk = self.psa()
        rec = self.P.rec
        ways = 2 if sample else 4
        sets = [(self.s5w[0:7], "s5w"), (self.s5x, "s5x"), (self.s5y, "s5y"), (self.s5z, "s5z")]
        marks = []
        for (c0, w) in subs:
            for gp in range(8):
                if gp % ways == 0:
                    marks = []
                    pbanks = [self.psa(), self.psa()]
                marks.append(len(rec))
                (T1, T2, G, GA, T3, T4, H), kx = sets[gp % ways]
                blk, pb = gp // 4, 64 * ((gp % 4) // 2)
                bP = pbanks[(gp % ways) // 2]
                pc = 128 * (gp % 2)
                for c in range(2):
                    mm(ps[bP][:, pc + 64 * c:pc + 64 * c + w], self.Bl[pb:pb + 64, gp, c, :], self.UB[pb:pb + 64, blk, c0:c0 + w],
                       R=["Bl", "UB"], W=[f"ps{bP}"])
                Pv = ps[bP][:, pc:pc + 128].rearrange("p (c t) -> p c t", c=2)[:, :, 0:w]
                if sample:
                    tb = lambda a, b_: KI[:, gp, a:b_, 0:TS].unsqueeze(2).to_broadcast([128, b_ - a, NSEQ, TS])
                    v4 = lambda ap: ap.rearrange("p c (s t) -> p c s t", t=TS)
                    v3 = lambda ap: ap.rearrange("p (s t) -> p s t", t=TS)
                    ko = lambda a, b_: KO[:, gp, a:b_, 0:TS].unsqueeze(2).to_broadcast([128, b_ - a, NSEQ, TS])
                    kis = lambda a: KI[:, gp, a, 0:TS].unsqueeze(1).to_broadcast([128, NSEQ, TS])
                    kos = lambda a: KO[:, gp, a, 0:TS].unsqueeze(1).to_broadcast([128, NSEQ, TS])
                else:
                    tb = lambda a, b_: KI[:, gp, a:b_, 0:w]
                    v4 = lambda ap: ap
                    v3 = lambda ap: ap
                    ko = lambda a, b_: KO[:, gp, a:b_, 0:w]
                    kis = lambda a: KI[:, gp, a, 0:w]
                    kos = lambda a: KO[:, gp, a, 0:w]
                R = [f"ps{bP}", "KI"]
                op("dve", "tensor_tensor", v4(T1[:, :, 0:w]), v4(Pv), tb(0, 1).to_broadcast([128, 2] + ([NSEQ, TS] if sample else [w])),
                   ALU.mult, R=R, W=[kx + "0"])
                op("dve", "tensor_tensor", v3(T2[:, 0, 0:w]), v3(Pv[:, 1, :]), kis(1), ALU.mult, R=R, W=[kx + "1"])
                op("dve", "tensor_tensor", v3(T2[:, 1, 0:w]), v3(Pv[:, 0, :]), kis(2), ALU.mult, R=R, W=[kx + "1"])
                op("pool", "tensor_tensor", G[:, :, 0:w], T1[:, :, 0:w], T2[:, :, 0:w], ALU.add, R=[kx + "0", kx + "1"], W=[kx + "2"])
                lr, li, nli = KO[:, gp, 0, 1:2], KO[:, gp, 2, 1:2], KO[:, gp, 1, 1:2]
                if sample:
                    hre, him = self.S5S[:, gp, 0, :], self.S5S[:, gp, 1, :]
                    g0 = lambda c: G[:, c, 0:w].rearrange("p (s t) -> p s t", t=TS)[:, :, 0]
                    hk = "S5S"
                else:
                    hre, him = self.S5C[:, gp, 0:1], self.S5C[:, gp, 1:2]
                    g0 = lambda c: G[:, c, 0:1]
                    hk = "S5C"
                for (c, ha, sa, hb_, sb_) in ((0, hre, lr, him, nli), (1, hre, li, him, lr)):
                    op("dve", "scalar_tensor_tensor", g0(c), ha, sa, g0(c), ALU.mult, ALU.add, R=[hk, "KO", kx + "2"], W=[kx + "2"])
                    op("dve", "scalar_tensor_tensor", g0(c), hb_, sb_, g0(c), ALU.mult, ALU.add, R=[hk, "KO", kx + "2"], W=[kx + "2"])
                for c in range(2):
                    op("dve", "tensor_tensor_scan", GA[:, c, 0:w], rst[:, 0:w], G[:, c, 0:w], 0.0, ALU.mult, ALU.add,
                       R=["cst", kx + "2"], W=[kx + "3"])
                R = [kx + "3", "KO"]
                op("pool", "tensor_tensor", v4(T3[:, :, 0:w]), v4(GA[:, :, 0:w]),
                   ko(0, 1).to_broadcast([128, 2] + ([NSEQ, TS] if sample else [w])), ALU.mult, R=R, W=[kx + "4"])
                op("pool", "tensor_tensor", v3(T4[:, 0, 0:w]), v3(GA[:, 1, 0:w]), kos(1), ALU.mult, R=R, W=[kx + "5"])
                op("pool", "tensor_tensor", v3(T4[:, 1, 0:w]), v3(GA[:, 0, 0:w]), kos(2), ALU.mult, R=R, W=[kx + "5"])
                op("dve", "tensor_tensor", H[:, :, 0:w], T3[:, :, 0:w], T4[:, :, 0:w], ALU.add, R=[kx + "4", kx + "5"], W=[kx + "6"])
                yo = ps[ybk][pb:pb + 64, 128 * blk + c0:128 * blk + c0 + w]
                Hh = self.Hh[gp % ways]
                hk = f"Hh{gp % ways}"
                op("act", "copy", Hh[:, :, 0:w], H[:, :, 0:w], R=[kx + "6"], W=[hk])
                mm(yo, self.Cl[:, gp, 0, :], Hh[:, 0, 0:w], start=(gp % 2 == 0), stop=False, R=["Cl", hk], W=[f"ps{ybk}"])
                mm(yo, self.Cl[:, gp, 1, :], Hh[:, 1, 0:w], start=False, stop=(gp % 2 == 1), R=["Cl", hk], W=[f"ps{ybk}"])
                if sample:
                    op("pool", "tensor_copy", self.S5O[:, gp, :, :],
                       H[:, :, 0:w].rearrange("p c (s t) -> p c s t", t=TS)[:, :, :, TS - 1], R=[kx + "6"], W=["S5O"])
                else:
                    op("pool", "tensor_copy", self.S5C[:, gp, :].unsqueeze(2), H[:, :, w - 1:w], R=[kx + "6"], W=["S5C"])
                if gp % ways == ways - 1 and rec is not None:
                    marks.append(len(rec))
                    lists = [rec[marks[q]:marks[q + 1]] for q in range(ways)]
                    mix_ = []
                    for q in range(max(len(x) for x in lists)):
                        for x in lists:
                            if q < len(x):
                                mix_.append(x[q])
                    rec[marks[0]:] = mix_
                if gp % ways == ways - 1:
                    self.psf(pbanks[0])
                    self.psf(pbanks[1])
        for blk in range(2):
            op("dve", "scalar_tensor_tensor", Y5[:, blk, 0:n], self.UB[:, blk, 0:n], self.cp[:, CP_S5D + blk:CP_S5D + blk + 1],
               ps[ybk][:, 128 * blk:128 * blk + n], ALU.mult, ALU.add, R=["UB", "cp", f"ps{ybk}"], W=["s5w7"])
        self.psf(ybk)
        T1, T2, T3 = self.s5w[0], self.s5w[1], self.s5w[4]
        t = T1
        act(t[:, :, 0:n], Y5[:, :, 0:n], AF.Square, R=["s5w7"], W=["s5w0"])
        op("dve", "tensor_scalar", t[:, :, 0:n], t[:, :, 0:n], 0.044715, 1.0, ALU.mult, ALU.add, R=["s5w0"], W=["s5w0"])
        op("pool", "tensor_tensor", t[:, :, 0:n], t[:, :, 0:n], Y5[:, :, 0:n], ALU.mult, R=["s5w0", "s5w7"], W=["s5w0"])
        act(t[:, :, 0:n], t[:, :, 0:n], AF.Tanh, scale=0.7978845608028654, R=["s5w0"], W=["s5w0"])
        gl = T2
        op("dve", "scalar_tensor_tensor", gl[:, :, 0:n], t[:, :, 0:n], 1.0, Y5[:, :, 0:n], ALU.add, ALU.mult,
           R=["s5w0", "s5w7"], W=["s5w1"])
        op("act", "copy", self.Y5B[:, :, 0:n], gl[:, :, 0:n], R=["s5w1"], W=["Y5B"])
        for eb in range(2):
            b = self.psa()
            for kc in range(2):
                mm(ps[b][:, 0:n], self.Wglu[:, kc, 128 * eb:128 * eb + 128], self.Y5B[:, kc, 0:n], start=(kc == 0), stop=(kc == 1),
                   R=["Wglu", "Y5B"], W=[f"ps{b}"])
            th = T3
            act(th[:, eb, 0:n], ps[b][:, 0:n], AF.Tanh, scale=0.25, bias=self.hglb[:, eb:eb + 1], R=[f"ps{b}", "hglb"], W=["s5w4"])
            self.psf(b)
            op("dve", "scalar_tensor_tensor", th[:, eb, 0:n], th[:, eb, 0:n], 1.0, gl[:, eb, 0:n], ALU.add, ALU.mult,
               R=["s5w4", "s5w1"], W=["s5w4"])
            op("dve", "scalar_tensor_tensor", self.mixT[:, 2 + eb, 0:n], th[:, eb, 0:n], 0.25, self.SZ[:, 2 + eb, 0:n],
               ALU.mult, ALU.mult, R=["s5w4", "SZ"], W=["mixT"])


def _host_inputs(inp, nl):
    f = lambda a: np.ascontiguousarray(a, dtype=np.float32)
    w_in = inp["w_in"][:nl]
    shared = {}
    shared["wfm"] = f(w_in[:, :, FM_COLS])
    shared["wtm"] = f(w_in[:, :, TM_COLS])
    shared["wout"] = f(inp["w_out"][:nl])
    shared["glu"] = f(inp["s5_glu_w"][:nl])
    shared["cst"] = _build_consts()
    cp = np.zeros((nl, 128, NCP), np.float32)
    for l in range(nl):
        cw = np.concatenate([inp["ssd_conv_w"][l], inp["gdn_conv_w"][l]], axis=1)
        cbias = np.concatenate([inp["ssd_conv_b"][l], inp["gdn_conv_b"][l]], axis=0)
        for cb in range(10):
            for j in range(4):
                cp[l, :, CP_CW + 4 * cb + j] = cw[j, 128 * cb:128 * cb + 128]
            cp[l, :, CP_CB + cb] = cbias[128 * cb:128 * cb + 128]
        for hb in range(2):
            cp[l, :, CP_GSSD + hb] = inp["ssd_norm_g"][l].reshape(256)[128 * hb:128 * hb + 128]
            cp[l, :, CP_GGDN + hb] = inp["gdn_norm_g"][l].reshape(256)[128 * hb:128 * hb + 128]
            cp[l, :, CP_GHG + hb] = inp["hg_norm_g"][l].reshape(256)[128 * hb:128 * hb + 128]
            cp[l, :, CP_DSSD + hb] = np.repeat(inp["ssd_d"][l], 64)[128 * hb:128 * hb + 128]
            cp[l, :, CP_GLUB + hb] = inp["s5_glu_b"][l][128 * hb:128 * hb + 128]
            cp[l, :, CP_S5D + hb] = inp["s5_d"][l].reshape(256)[128 * hb:128 * hb + 128]
    shared["cp"] = cp
    rp = np.zeros((nl, 128, 16), np.float32)
    for l in range(nl):
        row = np.concatenate([inp["ssd_dt_bias"][l], inp["ssd_a_log"][l], inp["gdn_dt_bias"][l], inp["gdn_a_log"][l]])
        rp[l] = np.broadcast_to(row[None, :], (128, 16))
    shared["rp"] = rp
    lnp = np.zeros((nl + 1, 2, 128, D), np.float32)
    lnp[0, 0] = np.broadcast_to(inp["ln_in_g"][None, :], (128, D))
    lnp[0, 1] = np.broadcast_to(inp["ln_in_b"][None, :], (128, D))
    for l in range(nl):
        lnp[l + 1, 0] = np.broadcast_to(inp["ln_g"][l][None, :], (128, D))
        lnp[l + 1, 1] = np.broadcast_to(inp["ln_b"][l][None, :], (128, D))
    shared["lnp"] = lnp
    shared["lbraw"] = f(inp["hg_lb_raw"].reshape(NL, 2, 128).transpose(2, 1, 0))
    def pn(a):
        return a.reshape(8, 2, 64).transpose(1, 2, 0).reshape(128, 8)
    s5p = np.zeros((nl, 128, 3, 8), np.float32)
    bl = np.zeros((nl, 128, 8, 2, 128), np.float32)
    cl = np.zeros((nl, 128, 8, 2, 64), np.float32)
    for l in range(nl):
        s5p[l, :, 0] = pn(inp["s5_lam_re"][l])
        s5p[l, :, 1] = pn(inp["s5_lam_im"][l])
        s5p[l, :, 2] = pn(np.broadcast_to(inp["s5_log_dt"][l][:, None], (16, 64)))
        for c, (bsrc, csrc) in enumerate(((inp["s5_b_re"][l], inp["s5_c_re"][l]), (inp["s5_b_im"][l], inp["s5_c_im"][l]))):
            for gp in range(8):
                for g2 in range(2):
                    g = 2 * gp + g2
                    r0 = (gp % 4) * 32 + g2 * 16
                    bl[l, r0:r0 + 16, gp, c, g2 * 64:g2 * 64 + 64] = bsrc[g].T
                    j0 = (gp % 2) * 32 + g2 * 16
                    cl[l, g2 * 64:g2 * 64 + 64, gp, c, j0:j0 + 16] = csrc[g].T
    shared["s5p"], shared["s5bl"], shared["s5cl"] = s5p, bl, cl
    maps = []
    for c in range(NCORE):
        m = dict(shared)
        sl = slice(NSEQ * c, NSEQ * c + NSEQ)
        m["xin"] = f(np.concatenate([inp["meta_tokens"], inp["x_prompt"][c], inp["x_sample"][sl].reshape(NSEQ * TS, D)], 0))
        m["st_ssd"] = f(inp["state_ssd"][:nl, sl])
        m["st_gdn"] = f(inp["state_gdn"][:nl, sl])
        m["st_hg"] = f(inp["state_hgrn"][:nl, sl])
        cv = np.concatenate([inp["state_ssd_conv"][:nl, sl], inp["state_gdn_conv"][:nl, sl]], axis=-1)
        m["st_conv"] = f(cv.reshape(nl, NSEQ, 3, 10, 128).transpose(0, 4, 3, 1, 2))
        s5 = np.stack([inp["state_s5_re"][:nl, sl], inp["state_s5_im"][:nl, sl]], axis=0)
        s5 = s5.reshape(2, nl, NSEQ, 8, 2, 64).transpose(1, 4, 5, 3, 0, 2).reshape(nl, 128, 8, 2, NSEQ)
        m["st_s5"] = f(s5)
        maps.append(m)
    return maps


def _unpack_state(a, mixer):
    out_heads = []
    for h in range(4):
        if mixer == "ssd":
            base, slot = 64 * (h // 2), h % 2
        else:
            base, slot = 64 * (h % 2), h // 2
        out_heads.append(a[..., base:base + 64, slot, :] if a.ndim == 4 else a[..., base:base + 64, slot, :, :])
    return out_heads


_CACHE = {}


def run(inputs, nl=NL, parts=("ssd", "s5", "gdn", "hg")):
    key = (nl, tuple(parts))
    if key not in _CACHE:
        _CACHE[key] = Builder(nl, set(parts)).build()
    nc = _CACHE[key]
    inp = {k: np.asarray(v) for k, v in inputs.items()}
    maps = _host_inputs(inp, nl)
    res = run_bass_kernel_spmd(nc, maps, core_ids=list(range(NCORE)))
    return assemble(res.results, nl)


def assemble(R, nl):
    nco = len(R)
    y = np.stack([r["y_out"] for r in R])
    y_prompt = np.ascontiguousarray(y[:, :2048])
    y_sample = np.ascontiguousarray(y[:, 2048:].reshape(nco * NSEQ, TS, D))
    outs = {}
    pst = np.stack([r["o_pst"] for r in R], axis=1)
    sst = np.stack([r["o_sst"] for r in R], axis=1)
    for m, mixer in enumerate(("ssd", "gdn", "hg")):
        ph, sh = [], []
        for h in range(4):
            if mixer == "ssd":
                base, slot = 64 * (h // 2), h % 2
            else:
                base, slot = 64 * (h % 2), h // 2
            ph.append(pst[:, :, m, base:base + 64, slot, :])
            sh.append(sst[:, :, m, base:base + 64, slot, :, :].transpose(0, 1, 3, 2, 4))
        outs["p_" + mixer] = np.ascontiguousarray(np.stack(ph, axis=2))
        outs["s_" + mixer] = np.ascontiguousarray(np.stack(sh, axis=3).reshape(nl, nco * NSEQ, 4, 64, 64))
    pconv = np.stack([r["o_pconv"] for r in R], axis=1)
    pconv = pconv.transpose(0, 1, 4, 3, 2).reshape(nl, nco, 3, 1280)
    sconv = np.stack([r["o_sconv"] for r in R], axis=1)
    sconv = sconv.transpose(0, 1, 4, 5, 3, 2).reshape(nl, nco * NSEQ, 3, 1280)
    ps5 = np.stack([r["o_ps5"] for r in R], axis=1)
    ps5 = ps5.reshape(nl, nco, 2, 64, 8, 2).transpose(5, 0, 1, 4, 2, 3).reshape(2, nl, nco, 16, 64)
    ss5 = np.stack([r["o_ss5"] for r in R], axis=1)
    ss5 = ss5.reshape(nl, nco, 2, 64, 8, 2, NSEQ).transpose(5, 0, 1, 6, 4, 2, 3).reshape(2, nl, nco * NSEQ, 16, 64)
    c = np.ascontiguousarray
    return (y_prompt, y_sample,
            outs["p_ssd"], c(pconv[..., :512]), c(ps5[0]), c(ps5[1]), outs["p_gdn"], c(pconv[..., 512:]), outs["p_hg"],
            outs["s_ssd"], c(sconv[..., :512]), c(ss5[0]), c(ss5[1]), outs["s_gdn"], c(sconv[..., 512:]), outs["s_hg"])


def kernel(**inputs):
    return run(inputs)
```
